# Optimizing a Trainium2 kernel written in Bass

```python
import math
import jax, jax.numpy as jnp
from jax import lax
import numpy as np

D_MODEL = 1024
BATCH = 16
SEQ = 2048
DEPTH = 4

D_MIX = D_MODEL
D_SSM = D_MIX // 2
D_ATT = D_MIX - D_SSM
SSM_GROUP = 16
SSM_GROUPS = D_SSM // SSM_GROUP
SSM_STATE = 64
DIFF_HEAD = 64
N_DIFF_HEADS = D_ATT // (2 * DIFF_HEAD)
DIFF_VDIM = 2 * DIFF_HEAD
ROT_DIM = DIFF_HEAD // 4
ROPE_THETA = 500000.0
Q_BLOCK = 128
D_FF = -(-8 * D_MODEL // (3 * 256)) * 256
IN_COLS = D_SSM + 3 * D_ATT
DEEPNORM_ALPHA = (2 * DEPTH) ** 0.25
DEEPNORM_BETA = (8 * DEPTH) ** -0.25
LN_EPS = 1e-5
RMS_EPS = 1e-5

kernel_name = "hybrid_s5_diffattn_deepnorm_adaln"


def layer_norm(x, g, b):
    x32 = x.astype(jnp.float32)
    mu = jnp.mean(x32, axis=-1, keepdims=True)
    xc = x32 - mu
    var = jnp.mean(xc * xc, axis=-1, keepdims=True)
    y = xc * lax.rsqrt(var + LN_EPS) * g.astype(jnp.float32) + b.astype(jnp.float32)
    return y.astype(x.dtype)


def apply_rotary(t, cos, sin):
    t_rot, t_pass = t[..., :ROT_DIM], t[..., ROT_DIM:]
    t1, t2 = t_rot[..., :ROT_DIM // 2], t_rot[..., ROT_DIM // 2:]
    rotated = jnp.concatenate([t1 * cos - t2 * sin, t2 * cos + t1 * sin], axis=-1)
    return jnp.concatenate([rotated.astype(t.dtype), t_pass], axis=-1)


def s5_mixer(u, a_re, a_im, log_step, b_re, b_im, c_re, c_im, d_skip, glu_w, glu_b):
    bsz, seqlen = u.shape[0], u.shape[1]
    f32 = jnp.float32
    u32 = u.astype(f32).reshape(bsz, seqlen, SSM_GROUPS, SSM_GROUP)
    delta = jnp.exp(log_step.astype(f32))[:, None]
    lam_re = jnp.minimum(a_re.astype(f32), -1e-4)
    lam_im = a_im.astype(f32)
    mag = jnp.exp(lam_re * delta)
    ang = lam_im * delta
    lb_re, lb_im = mag * jnp.cos(ang), mag * jnp.sin(ang)
    den = lam_re * lam_re + lam_im * lam_im
    n_re, n_im = lb_re - 1.0, lb_im
    k_re = (n_re * lam_re + n_im * lam_im) / den
    k_im = (n_im * lam_re - n_re * lam_im) / den
    b_re32, b_im32 = b_re.astype(f32), b_im.astype(f32)
    bb_re = k_re[..., None] * b_re32 - k_im[..., None] * b_im32
    bb_im = k_re[..., None] * b_im32 + k_im[..., None] * b_re32
    bu_re = jnp.einsum('blgc,gpc->blgp', u32, bb_re)
    bu_im = jnp.einsum('blgc,gpc->blgp', u32, bb_im)
    a_t_re = jnp.broadcast_to(lb_re, (1, seqlen, SSM_GROUPS, SSM_STATE))
    a_t_im = jnp.broadcast_to(lb_im, (1, seqlen, SSM_GROUPS, SSM_STATE))

    def combine(left, right):
        a1r, a1i, b1r, b1i = left
        a2r, a2i, b2r, b2i = right
        ar = a2r * a1r - a2i * a1i
        ai = a2r * a1i + a2i * a1r
        br = a2r * b1r - a2i * b1i + b2r
        bi = a2r * b1i + a2i * b1r + b2i
        return (ar, ai, br, bi)

    _, _, st_re, st_im = lax.associative_scan(combine, (a_t_re, a_t_im, bu_re, bu_im), axis=1)
    y = (jnp.einsum('blgp,gcp->blgc', st_re, c_re.astype(f32))
         - jnp.einsum('blgp,gcp->blgc', st_im, c_im.astype(f32))
         + d_skip.astype(f32) * u32)
    y = jax.nn.gelu(y).reshape(bsz, seqlen, D_SSM)
    y = y * jax.nn.sigmoid(y @ glu_w.astype(f32) + glu_b.astype(f32))
    return y.astype(u.dtype)


def diff_attention(q, k, v, lam, lam_init, subln_w):
    bsz, seqlen = q.shape[0], q.shape[1]
    f32 = jnp.float32
    n_blocks = seqlen // Q_BLOCK
    q = q * (DIFF_HEAD ** -0.5)
    qb = q.reshape(bsz, n_blocks, Q_BLOCK, N_DIFF_HEADS, 2, DIFF_HEAD).transpose(1, 0, 2, 3, 4, 5)
    v32 = v.astype(f32)
    key_idx = jnp.arange(seqlen)
    gain = subln_w.astype(f32) * (1.0 - lam_init)

    def block(args):
        q_blk, i = args
        s = jnp.einsum('bqhmd,bkhmd->bhmqk', q_blk, k).astype(f32)
        q_idx = i * Q_BLOCK + jnp.arange(Q_BLOCK)
        mask = key_idx[None, :] <= q_idx[:, None]
        s = jnp.where(mask, s, -jnp.inf)
        p = jax.nn.softmax(s, axis=-1)
        w = p[:, :, 0] - lam * p[:, :, 1]
        o = jnp.einsum('bhqk,bkhe->bqhe', w, v32)
        o = o * lax.rsqrt(jnp.mean(o * o, axis=-1, keepdims=True) + RMS_EPS) * gain
        return o

    out = lax.map(block, (qb, jnp.arange(n_blocks)))
    return out.transpose(1, 0, 2, 3, 4).reshape(bsz, seqlen, N_DIFF_HEADS * DIFF_VDIM)


def setup_inputs(seed: int = 0) -> dict:
    key = jax.random.key(seed)
    ks = jax.random.split(key, 32)
    f32 = jnp.float32

    def nrm(k, shape, scale):
        return jax.random.normal(k, shape, f32) * scale

    nl, G, P, C = DEPTH, SSM_GROUPS, SSM_STATE, SSM_GROUP
    x = nrm(ks[0], (BATCH, SEQ, D_MODEL), 1.0)
    c = nrm(ks[1], (BATCH, D_MODEL), 1.0)
    offset = jax.random.randint(ks[2], (BATCH, 1), 0, 4096, dtype=jnp.int32)
    positions = offset + jnp.arange(SEQ, dtype=jnp.int32)[None, :]
    mod_w = nrm(ks[3], (nl, D_MODEL, 6 * D_MODEL), 0.1 * D_MODEL ** -0.5)
    mod_b = nrm(ks[4], (nl, 6 * D_MODEL), 0.01)
    w_in = nrm(ks[5], (nl, D_MODEL, IN_COLS), D_MODEL ** -0.5)
    n_idx = jnp.arange(P, dtype=f32)
    ssm_a_re = -0.5 + nrm(ks[6], (nl, G, P), 0.01)
    ssm_a_im = math.pi * n_idx + nrm(ks[7], (nl, G, P), 0.01)
    ssm_log_step = jax.random.uniform(ks[8], (nl, G), f32, math.log(1e-3), math.log(1e-1))
    ssm_b_re = nrm(ks[9], (nl, G, P, C), (2 * C) ** -0.5)
    ssm_b_im = nrm(ks[10], (nl, G, P, C), (2 * C) ** -0.5)
    ssm_c_re = nrm(ks[11], (nl, G, C, P), P ** -0.5)
    ssm_c_im = nrm(ks[12], (nl, G, C, P), P ** -0.5)
    ssm_d = nrm(ks[13], (nl, G, C), 1.0)
    glu_w = nrm(ks[14], (nl, D_SSM, D_SSM), D_SSM ** -0.5)
    glu_b = nrm(ks[15], (nl, D_SSM), 0.01)
    lam_q1 = nrm(ks[16], (nl, DIFF_HEAD), 0.1)
    lam_k1 = nrm(ks[17], (nl, DIFF_HEAD), 0.1)
    lam_q2 = nrm(ks[18], (nl, DIFF_HEAD), 0.1)
    lam_k2 = nrm(ks[19], (nl, DIFF_HEAD), 0.1)
    subln_w = 1.0 + nrm(ks[20], (nl, DIFF_VDIM), 0.02)
    w_out = nrm(ks[21], (nl, D_MIX, D_MODEL), DEEPNORM_BETA * D_MIX ** -0.5)
    ln1_g = 1.0 + nrm(ks[22], (nl, D_MODEL), 0.02)
    ln1_b = nrm(ks[23], (nl, D_MODEL), 0.01)
    ffn_w_gate = nrm(ks[24], (nl, D_MODEL, D_FF), D_MODEL ** -0.5)
    ffn_w_up = nrm(ks[25], (nl, D_MODEL, D_FF), D_MODEL ** -0.5)
    ffn_w_down = nrm(ks[26], (nl, D_FF, D_MODEL), DEEPNORM_BETA * D_FF ** -0.5)
    ln2_g = 1.0 + nrm(ks[27], (nl, D_MODEL), 0.02)
    ln2_b = nrm(ks[28], (nl, D_MODEL), 0.01)
    return {"x": x, "c": c, "positions": positions,
            "mod_w": mod_w, "mod_b": mod_b, "w_in": w_in,
            "ssm_a_re": ssm_a_re, "ssm_a_im": ssm_a_im, "ssm_log_step": ssm_log_step,
            "ssm_b_re": ssm_b_re, "ssm_b_im": ssm_b_im, "ssm_c_re": ssm_c_re, "ssm_c_im": ssm_c_im,
            "ssm_d": ssm_d, "glu_w": glu_w, "glu_b": glu_b,
            "lam_q1": lam_q1, "lam_k1": lam_k1, "lam_q2": lam_q2, "lam_k2": lam_k2,
            "subln_w": subln_w, "w_out": w_out, "ln1_g": ln1_g, "ln1_b": ln1_b,
            "ffn_w_gate": ffn_w_gate, "ffn_w_up": ffn_w_up, "ffn_w_down": ffn_w_down,
            "ln2_g": ln2_g, "ln2_b": ln2_b}


def reference(x, c, positions, mod_w, mod_b, w_in, ssm_a_re, ssm_a_im, ssm_log_step,
              ssm_b_re, ssm_b_im, ssm_c_re, ssm_c_im, ssm_d, glu_w, glu_b,
              lam_q1, lam_k1, lam_q2, lam_k2, subln_w, w_out, ln1_g, ln1_b,
              ffn_w_gate, ffn_w_up, ffn_w_down, ln2_g, ln2_b):
    bsz, seqlen = x.shape[0], x.shape[1]
    f32 = jnp.float32
    cond = jax.nn.silu(c)
    freqs = ROPE_THETA ** (-jnp.arange(0, ROT_DIM, 2, dtype=f32) / ROT_DIM)
    angles = positions.astype(f32)[..., None] * freqs
    cos = jnp.cos(angles)[:, :, None, None, :]
    sin = jnp.sin(angles)[:, :, None, None, :]

    for l in range(DEPTH):
        lam_init = 0.8 - 0.6 * math.exp(-0.3 * l)
        mod = cond @ mod_w[l] + mod_b[l]
        shift1, scale1, gate1, shift2, scale2, gate2 = [m[:, None, :] for m in jnp.split(mod, 6, axis=-1)]

        h = x * (1.0 + scale1) + shift1
        proj = h @ w_in[l]
        u, q, k, v = jnp.split(proj, [D_SSM, D_SSM + D_ATT, D_SSM + 2 * D_ATT], axis=-1)
        ssm_out = s5_mixer(u, ssm_a_re[l], ssm_a_im[l], ssm_log_step[l], ssm_b_re[l], ssm_b_im[l],
                           ssm_c_re[l], ssm_c_im[l], ssm_d[l], glu_w[l], glu_b[l])
        q = apply_rotary(q.reshape(bsz, seqlen, N_DIFF_HEADS, 2, DIFF_HEAD), cos, sin)
        k = apply_rotary(k.reshape(bsz, seqlen, N_DIFF_HEADS, 2, DIFF_HEAD), cos, sin)
        v = v.reshape(bsz, seqlen, N_DIFF_HEADS, DIFF_VDIM)
        lam = (jnp.exp(jnp.sum(lam_q1[l].astype(f32) * lam_k1[l].astype(f32)))
               - jnp.exp(jnp.sum(lam_q2[l].astype(f32) * lam_k2[l].astype(f32))) + lam_init)
        att_out = diff_attention(q, k, v, lam, lam_init, subln_w[l])
        mix = jnp.concatenate([ssm_out, att_out.astype(x.dtype)], axis=-1) @ w_out[l]
        x = layer_norm(DEEPNORM_ALPHA * x + (1.0 + gate1) * mix, ln1_g[l], ln1_b[l])

        h = x * (1.0 + scale2) + shift2
        ffn = (jax.nn.silu(h @ ffn_w_gate[l]) * (h @ ffn_w_up[l])) @ ffn_w_down[l]
        x = layer_norm(DEEPNORM_ALPHA * x + (1.0 + gate2) * ffn, ln2_g[l], ln2_b[l])
    return x
```

```python
import contextlib
import math
import numpy as np
import ml_dtypes
import concourse.bass as bass
import concourse.mybir as mybir
from concourse.bass_utils import run_bass_kernel_spmd

F32 = mybir.dt.float32
BF16 = mybir.dt.bfloat16
I32 = mybir.dt.int32
ALU = mybir.AluOpType
AF = mybir.ActivationFunctionType
AX = mybir.AxisListType

D = 1024
S = 2048
NT = 16
DFF = 2816
NF = 22
DEPTH = 4
ALPHA = float((2 * DEPTH) ** 0.25)
TWO_PI = float(2 * np.pi)
ENG = ["pe", "act", "dve", "pool", "sp"]
EPOCH = 8192


class Prog:
    def __init__(self, nc):
        self.nc = nc
        self.stack = contextlib.ExitStack()
        self.ops = {e: [] for e in ENG}
        self.count = {e: 0 for e in ENG}
        self.seen = {e: {} for e in ENG}
        self.lastw = {}
        self.readers = {}
        self.dma_val = {}
        self.sems = {}
        self.pending = {}
        self.bufkeys = {}

    def sb(self, name, shape, dt):
        return self.stack.enter_context(self.nc.sbuf_tensor("s_" + name, list(shape), dt))

    def ps(self, name, shape, dt):
        return self.stack.enter_context(self.nc.psum_tensor(name, list(shape), dt))

    def sem(self, name):
        if name not in self.sems:
            self.sems[name] = self.stack.enter_context(self.nc.semaphore(name))
        return self.sems[name]

    def _touch(self, k):
        b = k[0] if isinstance(k, tuple) else k
        ks = self.bufkeys.setdefault(b, set())
        if k not in ks:
            ks.add(k)
            pend = self.pending.get(b)
            if pend and k not in self.lastw and k not in self.readers:
                self.readers[k] = [(s, v) for s, v in pend.items()]

    def handoff(self, old_bufs, new_bufs):
        t = {}
        for b in old_bufs:
            for k in self.bufkeys.get(b, ()):
                toks = list(self.readers.get(k, ()))
                if self.lastw.get(k) is not None:
                    toks.append(self.lastw[k])
                for s, v in toks:
                    if t.get(s, 0) < v:
                        t[s] = v
            for s, v in self.pending.get(b, {}).items():
                if t.get(s, 0) < v:
                    t[s] = v
        for b in new_bufs:
            for k in self.bufkeys.get(b, ()):
                self.lastw.pop(k, None)
                self.readers.pop(k, None)
            self.bufkeys[b] = set()
            self.pending[b] = dict(t)

    def _need(self, eng, tok, waits):
        if tok is None:
            return
        stream, val = tok
        if self.seen[eng].get(stream, 0) >= val:
            return
        self.seen[eng][stream] = val
        waits.append(tok)

    def op(self, eng, fn, reads=(), writes=(), dma=None, dma_n=1):
        waits = []
        is_dma = dma is not None
        for k in reads:
            self._touch(k)
        for k in writes:
            self._touch(k)
        for k in reads:
            tok = self.lastw.get(k)
            if tok is not None:
                if (not is_dma) and tok[0] == ("eng", eng) and eng == "pe":
                    pass
                else:
                    self._need(eng, tok, waits)
            else:
                for tok2 in self.readers.get(k, ()):
                    pass
        for k in writes:
            tok = self.lastw.get(k)
            if tok is not None:
                if (not is_dma) and tok[0] == ("eng", eng):
                    pass
                else:
                    self._need(eng, tok, waits)
            for tok in self.readers.get(k, ()):
                if (not is_dma) and tok[0] == ("eng", eng):
                    continue
                self._need(eng, tok, waits)
        if is_dma:
            prev = self.dma_val.get(dma, 0)
            if prev:
                self._need(eng, (("dma", dma), prev), waits)
            val = prev + 16 * dma_n
            self.dma_val[dma] = val
            mytok = (("dma", dma), val)
            self.ops[eng].append((waits, fn, ("dma", dma)))
        else:
            self.count[eng] += 1
            mytok = (("eng", eng), self.count[eng])
            self.ops[eng].append((waits, fn, ("eng", self.count[eng])))
        for k in writes:
            self.lastw[k] = mytok
            self.readers[k] = []
        for k in reads:
            self.readers.setdefault(k, []).append(mytok)
        return mytok

    def _semfor(self, stream, val):
        if stream[0] == "dma":
            return self.sem("d_" + str(stream[1])), val
        e = stream[1]
        ep = (val - 1) // EPOCH
        return self.sem("e_%s_%d" % (e, ep)), (val - 1) % EPOCH + 1

    def emit(self, final_tokens):
        nc = self.nc
        for e in ENG:
            for waits, fn, inc in self.ops[e]:
                for stream, val in waits:
                    self._semfor(stream, val)
                if inc[0] == "dma":
                    self.sem("d_" + str(inc[1]))
                else:
                    self._semfor(("eng", e), inc[1])
        for stream, val in final_tokens:
            self._semfor(stream, val)
        prog = self
        with nc.Block() as block:
            def replay(e, handle):
                for waits, fn, inc in prog.ops[e]:
                    for stream, val in waits:
                        s, v = prog._semfor(stream, val)
                        handle.wait_ge(s, v)
                    if inc[0] == "dma":
                        s = prog.sem("d_" + str(inc[1]))
                        fn(handle, lambda ins, s=s: ins.then_inc(s, 16))
                    else:
                        ins = fn(handle)
                        s, v = prog._semfor(("eng", e), inc[1])
                        ins.then_inc(s, 1)
                if e == "sp":
                    for stream, val in final_tokens:
                        s, v = prog._semfor(stream, val)
                        handle.wait_ge(s, v)

            @block.tensor
            def _(h):
                replay("pe", h)

            @block.scalar
            def _(h):
                replay("act", h)

            @block.vector
            def _(h):
                replay("dve", h)

            @block.gpsimd
            def _(h):
                replay("pool", h)

            @block.sync
            def _(h):
                replay("sp", h)

    def close(self):
        self.stack.close()


PARAM_SPECS = [
    ("mod_w", [DEPTH, D, 6 * D]), ("mod_b", [DEPTH, 6 * D]), ("w_in", [DEPTH, D, 2048]),
    ("ssm_a_re", [DEPTH, 32, 64]), ("ssm_a_im", [DEPTH, 32, 64]), ("ssm_log_step", [DEPTH, 32]),
    ("ssm_b_re", [DEPTH, 32, 64, 16]), ("ssm_b_im", [DEPTH, 32, 64, 16]),
    ("ssm_c_re", [DEPTH, 32, 16, 64]), ("ssm_c_im", [DEPTH, 32, 16, 64]), ("ssm_d", [DEPTH, 32, 16]),
    ("glu_w", [DEPTH, 512, 512]), ("glu_b", [DEPTH, 512]),
    ("lam_q1", [DEPTH, 64]), ("lam_k1", [DEPTH, 64]), ("lam_q2", [DEPTH, 64]), ("lam_k2", [DEPTH, 64]),
    ("subln_w", [DEPTH, 128]), ("w_out", [DEPTH, D, D]), ("ln1_g", [DEPTH, D]), ("ln1_b", [DEPTH, D]),
    ("ffn_w_gate", [DEPTH, D, DFF]), ("ffn_w_up", [DEPTH, D, DFF]), ("ffn_w_down", [DEPTH, DFF, D]),
    ("ln2_g", [DEPTH, D]), ("ln2_b", [DEPTH, D]),
]


def host_consts():
    c = {}
    c["ident"] = np.eye(128, dtype=np.float32)
    R = np.zeros((128, 128), np.float32)
    for i in range(128):
        d = i % 64
        if d < 8:
            R[i + 8, i] = 1.0
        elif d < 16:
            R[i - 8, i] = 1.0
    c["rmat"] = R
    c["negtri"] = np.where(np.arange(128)[:, None] <= np.arange(128)[None, :], 0.0, -30000.0).astype(np.float32)
    c["bmask"] = (np.arange(128)[:, None] // 16 == np.arange(128)[None, :] // 16).astype(np.float32)
    c["jidx"] = np.arange(256, dtype=np.float32)[None, :]
    freqs = 500000.0 ** (-np.arange(0, 16, 2, dtype=np.float64) / 16.0)
    fc = np.zeros((128, 2), np.float32)
    for i in range(128):
        d = i % 64
        if d < 8:
            fc[i, 0] = freqs[d] / (2 * np.pi)
            fc[i, 1] = -freqs[d] / (2 * np.pi)
        elif d < 16:
            fc[i, 0] = freqs[d - 8] / (2 * np.pi)
            fc[i, 1] = freqs[d - 8] / (2 * np.pi)
    c["ropef"] = fc
    return c


def build_program(nseq=2, nl=DEPTH, dbg=None, do_attn=True, do_ssm=True):
    nc = bass.Bass("TRN2", target_bir_lowering=False)
    din = {}
    din["x"] = nc.dram_tensor("x", [nseq, S, D], F32, kind="ExternalInput").ap()
    din["c"] = nc.dram_tensor("c", [nseq, D], F32, kind="ExternalInput").ap()
    din["positions"] = nc.dram_tensor("positions", [nseq, S], I32, kind="ExternalInput").ap()
    for name, shp in PARAM_SPECS:
        din[name] = nc.dram_tensor(name, shp, F32, kind="ExternalInput").ap()
    hc = host_consts()
    for name, arr in hc.items():
        din[name] = nc.dram_tensor(name, list(arr.shape), F32, kind="ExternalInput").ap()
    y_out = nc.dram_tensor("y", [nseq, S, D], F32, kind="ExternalOutput").ap()
    dbg_out = {}
    if dbg:
        for name, shp in dbg.items():
            dbg_out[name] = nc.dram_tensor(name, shp, F32, kind="ExternalOutput").ap()

    tab_bt8 = nc.dram_tensor("tab_bt8", [DEPTH, 4, 128, 2 * 8 * 128], BF16).ap()
    tab_ctr = nc.dram_tensor("tab_ctr", [DEPTH, 4, 128, 4 * 2 * 8 * 32], BF16).ap()
    tab_kt = nc.dram_tensor("tab_kt", [DEPTH, 4, 128, 8 * 128], BF16).ap()
    tab_rot = nc.dram_tensor("tab_rot", [DEPTH, 4, 128, 2 * 4 * 256], F32).ap()
    tab_small = nc.dram_tensor("tab_small", [DEPTH, 128, 32], F32).ap()

    p = Prog(nc)
    fin = []

    def dma(q, out, in_, reads=(), writes=(), key=None):
        return p.op(q, lambda h, inc: inc(h.dma_start(out=out, in_=in_)), reads=reads, writes=writes, dma=key)

    def act(out, in_, func, reads, writes, **kw):
        return p.op("act", lambda h: h.activation(out, in_, func, **kw), reads=reads, writes=writes)

    def tt(out, a, b, op, reads, writes, eng="dve"):
        return p.op(eng, lambda h: h.tensor_tensor(out, a, b, op), reads=reads, writes=writes)

    def ts(out, a, s1, s2, op0, op1, reads, writes, eng="dve"):
        if s2 is None:
            return p.op(eng, lambda h: h.tensor_scalar(out, a, s1, None, op0), reads=reads, writes=writes)
        return p.op(eng, lambda h: h.tensor_scalar(out, a, s1, s2, op0, op1), reads=reads, writes=writes)

    def stt(out, a, sc, b, op0, op1, reads, writes, eng="dve"):
        return p.op(eng, lambda h: h.scalar_tensor_tensor(out, a, sc, b, op0, op1), reads=reads, writes=writes)

    def cp(out, a, reads, writes, eng="dve"):
        return p.op(eng, lambda h: h.tensor_copy(out, a), reads=reads, writes=writes)

    def mm_group(mms, reads, writes):
        def fn(h):
            ins = None
            for (o, l, r, st, sp_, kw) in mms:
                ins = h.matmul(o, l, r, start=st, stop=sp_, **kw)
            return ins
        return p.op("pe", fn, reads=reads, writes=writes)

    def tr_group(trs, reads, writes):
        def fn(h):
            ins = None
            for (o, i, idn) in trs:
                ins = h.transpose(o, i, idn)
            return ins
        return p.op("pe", fn, reads=reads, writes=writes)

    x_sb = p.sb("x_sb", [128, NT, D], F32)
    ARENA_F32 = 112 * 256 + 64
    arena = p.sb("arena", [128, ARENA_F32], F32)

    def carve(off_bytes, shape, dt):
        size = 2 if dt == BF16 else 4
        n = int(np.prod(shape[1:]))
        assert off_bytes % 4 == 0 and (n * size) % 4 == 0
        assert off_bytes + n * size <= ARENA_F32 * 4, (off_bytes, shape)
        v = arena[:, off_bytes // 4:(off_bytes + n * size) // 4]
        if dt != F32:
            v = v.bitcast(dt)
        if len(shape) == 2:
            return v
        names = " ".join("a%d" % i for i in range(len(shape) - 1))
        kw = {"a%d" % i: shape[i + 1] for i in range(len(shape) - 1)}
        return v.rearrange("p (%s) -> p %s" % (names, names), **kw)

    KB = 1024
    ident_f = p.sb("ident_f", [128, 128], F32)
    ident_b = p.sb("ident_b", [128, 128], BF16)
    rmat_b = p.sb("rmat_b", [128, 128], BF16)
    negtri_b = p.sb("negtri_b", [128, 128], BF16)
    bmask_f = p.sb("bmask_f", [128, 128], F32)
    jidx_f = p.sb("jidx_f", [128, 256], F32)
    ropef = p.sb("ropef", [128, 2], F32)
    modcolB = [p.sb("modcol%d" % i, [128, 48], F32) for i in range(2)]
    modbB = [p.sb("modb_col%d" % i, [128, 48], F32) for i in range(2)]
    cond_all = p.sb("cond_all", [128, 2, 8], BF16)
    gate_b = p.sb("gate_b", [128, 1, D], F32)
    lngb = p.sb("lngb", [128, 2, D], F32)
    modw_buf = [p.sb("modw%d" % i, [128, 8, 128], BF16) for i in range(2)]
    cond_col = p.sb("cond_col", [128, 8], F32)
    cond_rep = p.sb("cond_rep", [128, 8, 128], BF16)
    gain_b = p.sb("gain_b", [128, 128], F32)
    lamv = p.sb("lamv", [128, 4, 64], F32)
    lams = p.sb("lams", [128, 8], F32)
    st_sb = p.sb("st_sb", [128, 12], F32)
    mv_sb = p.sb("mv_sb", [128, 2], F32)
    rs_sb = p.sb("rs_sb", [128, 2], F32)
    ltmp = p.sb("ltmp", [128, D], F32)
    stmp = [p.sb("stmp%d" % i, [128, 512], F32) for i in range(2)]
    ep_z = p.sb("ep_z", [128, 2, 4], F32)
    ep_ss = p.sb("ep_ss", [128, 4], F32)

    ps_all = p.ps("ps_all", [128, 8, 512], F32)
    psb = [ps_all[:, i, :] for i in range(8)]

    def psb_bf(i):
        return psb[i].bitcast(BF16)

    dma("sp", ident_f[:], din["ident"], writes=["ident_f"], key="c0")
    dma("pool", ident_b[:], din["ident"], writes=["ident_b"], key="c1")
    dma("pool", rmat_b[:], din["rmat"], writes=["rmat_b"], key="c2")
    dma("pool", negtri_b[:], din["negtri"], writes=["negtri_b"], key="c3")
    dma("sp", bmask_f[:], din["bmask"], writes=["bmask_f"], key="c4")
    dma("sp", jidx_f[:], din["jidx"].broadcast_to([128, 256]), writes=["jidx_f"], key="c5")
    dma("sp", ropef[:], din["ropef"], writes=["ropef"], key="c6")

    def frac_round(dst, src, itmp, ftmp, reads, writes, eng="dve"):
        cp(itmp, src, reads=reads, writes=["_frac_i"], eng=eng)
        cp(ftmp, itmp, reads=["_frac_i"], writes=["_frac_f"], eng=eng)
        tt(dst, src, ftmp, ALU.subtract, reads=list(reads) + ["_frac_f"], writes=writes, eng=eng)


    S5_MAIN_BUFS = ["tabsB", "tabsC", "rotb", "small", "r8tab", "t1", "t2", "b1", "b2", "Sprev", "Sprev0", "yact", "gluw", "sg"]
    S5_BUILD_BUFS = ["spf", "Ppow", "aq", "Bs", "BX", "Cld", "CX", "BXb", "BPall", "CPall", "BT8all", "CTrall", "KTall",
                     "bt1", "bt2", "bitmp", "bftmp", "bcs", "smallb"]

    def ssm_build(l):
        p.handoff(S5_BUILD_BUFS, S5_BUILD_BUFS)
        spf = carve(0, [128, 40, 16], F32)
        Pre = carve(2560, [128, 9, 16], F32)
        Pim = carve(3328, [128, 9, 16], F32)
        aq = carve(4 * KB, [128, 3, 128], F32)
        Bs = carve(5632, [128, 2, 16, 16], F32)
        BX = carve(7680, [128, 2, 16, 32], F32)
        Cld = carve(11776, [128, 2, 4, 128], F32)
        CX = carve(15872, [128, 2, 16, 32], F32)
        BXb = carve(19968, [128, 2, 16, 32], BF16)
        smallb = carve(22016, [128, 32], F32)
        BPall = carve(22 * KB, [128, 8, 2, 512], BF16)
        CPall = carve(38 * KB, [128, 9, 2, 512], BF16)
        BT8all = carve(56 * KB, [128, 4, 2, 8, 128], BF16)
        CTrall = carve(72 * KB, [128, 16, 2, 8, 32], BF16)
        KTall = carve(88 * KB, [128, 4, 8, 128], BF16)
        bt1 = carve(96 * KB, [128, 1024], F32)
        bt2 = carve(100 * KB, [128, 1024], F32)
        bitmp = carve(104 * KB, [128, 1024], F32).bitcast(I32)
        bftmp = carve(108 * KB, [128, 1024], F32)
        bcs = carve(22 * KB, [128, 2, 4, 256], F32)

        def sl(i):
            return spf[:, i, :]
        L_RE, L_IM, DL, LR, TRN, MAG, COSA, SINA, LBR, LBI, DEN, RDEN, NRE, KR, KI, TA, TB, R8, FF, FHI, FLO, TI, TF, TC, GLUB, DCOL = range(26)
        sk = lambda i: ("spf", i)

        dma("sp", aq[0:16, 0, :], din["ssm_a_re"][l].rearrange("(q g) p -> q (g p)", g=2), writes=[("aq", 0)], key="aq0")
        dma("sp", aq[0:16, 1, :], din["ssm_a_im"][l].rearrange("(q g) p -> q (g p)", g=2), writes=[("aq", 1)], key="aq1")
        dma("sp", ltmp[0:16, 0:2], din["ssm_log_step"][l].rearrange("(q g) -> q g", g=2), writes=["ltmp"], key="aq2")
        act(ltmp[0:16, 0:2], ltmp[0:16, 0:2], AF.Exp, reads=["ltmp"], writes=["ltmp"])
        cp(aq[0:16, 2, :].rearrange("q (g p) -> q g p", g=2), ltmp[0:16, 0:2].unsqueeze(2).broadcast_to([16, 2, 64]),
           reads=["ltmp"], writes=[("aq", 2)])
        tr_group([(psb[7][:, 16 * i:16 * i + 16], aq[0:16, i, :], ident_f[0:16, 0:16]) for i in range(3)],
                 reads=[("aq", 0), ("aq", 1), ("aq", 2), "ident_f"], writes=[("ps", 7)])
        ts(sl(L_RE), psb[7][:, 0:16], -1e-4, None, ALU.min, None, reads=[("ps", 7)], writes=[sk(L_RE)])
        cp(sl(L_IM), psb[7][:, 16:32], reads=[("ps", 7)], writes=[sk(L_IM)])
        cp(sl(DL), psb[7][:, 32:48], reads=[("ps", 7)], writes=[sk(DL)])
        dma("sp", ltmp[0:4, 0:128], din["ssm_d"][l].rearrange("g c -> (g c)").rearrange("(o p) -> o p", p=128), writes=["ltmp"], key="aq3")
        dma("sp", ltmp[4:8, 0:128], din["glu_b"][l].rearrange("(o p) -> o p", p=128), writes=["ltmp"], key="aq3")
        tr_group([(psb[7][:, 64:72], ltmp[0:8, 0:128], ident_f[0:8, 0:8])], reads=["ltmp", "ident_f"], writes=[("ps", 7)])
        cp(spf[:, DCOL, 0:4], psb[7][:, 64:68], reads=[("ps", 7)], writes=[sk(DCOL)])
        cp(smallb[:, 16:20], psb[7][:, 68:72], reads=[("ps", 7)], writes=["smallb"])
        for c, nm in enumerate(("ssm_b_re", "ssm_b_im")):
            dma("sp", Bs[:, c, :, :], din[nm][l].rearrange("(q g) p c -> (g p) q c", g=2), writes=[("Bs", c)], key="bs%d" % c)
        for c, nm in enumerate(("ssm_c_re", "ssm_c_im")):
            csrc = din[nm][l].rearrange("(o q g) c s -> q c o g s", o=4, q=4, g=2)
            for q4_ in range(4):
                for o_ in range(4):
                    dma("sp", Cld[16 * q4_:16 * q4_ + 16, c, o_, :].rearrange("p (g s) -> p g s", g=2), csrc[q4_][:, o_, :, :],
                        writes=[("Cld", c)], key="cld%d_%d" % (q4_, o_))

        tt(sl(LR), sl(L_RE), sl(DL), ALU.mult, reads=[sk(L_RE), sk(DL)], writes=[sk(LR)])
        stt(sl(TRN), sl(L_IM), 1.0 / TWO_PI, sl(DL), ALU.mult, ALU.mult, reads=[sk(L_IM), sk(DL)], writes=[sk(TRN)])
        act(sl(MAG), sl(LR), AF.Exp, reads=[sk(LR)], writes=[sk(MAG)])
        act(smallb[:, 0:16], sl(LR), AF.Exp, reads=[sk(LR)], writes=["smallb"], scale=8.0)
        tiv = spf[:, TI, :].bitcast(I32)

        def frac16(dst, src):
            cp(tiv, sl(src), reads=[sk(src)], writes=[sk(TI)])
            cp(sl(TF), tiv, reads=[sk(TI)], writes=[sk(TF)])
            tt(sl(dst), sl(src), sl(TF), ALU.subtract, reads=[sk(src), sk(TF)], writes=[sk(dst)])
        frac16(TA, TRN)
        act(sl(SINA), sl(TA), AF.Sin, reads=[sk(TA)], writes=[sk(SINA)], scale=TWO_PI)
        ts(sl(TC), sl(TRN), 0.25, None, ALU.add, None, reads=[sk(TRN)], writes=[sk(TC)])
        frac16(TA, TC)
        act(sl(COSA), sl(TA), AF.Sin, reads=[sk(TA)], writes=[sk(COSA)], scale=TWO_PI)
        tt(sl(LBR), sl(MAG), sl(COSA), ALU.mult, reads=[sk(MAG), sk(COSA)], writes=[sk(LBR)])
        tt(sl(LBI), sl(MAG), sl(SINA), ALU.mult, reads=[sk(MAG), sk(SINA)], writes=[sk(LBI)])
        tt(sl(TA), sl(L_RE), sl(L_RE), ALU.mult, reads=[sk(L_RE)], writes=[sk(TA)])
        tt(sl(TB), sl(L_IM), sl(L_IM), ALU.mult, reads=[sk(L_IM)], writes=[sk(TB)])
        tt(sl(DEN), sl(TA), sl(TB), ALU.add, reads=[sk(TA), sk(TB)], writes=[sk(DEN)])
        p.op("dve", lambda h: h.reciprocal(spf[:, RDEN, :], spf[:, DEN, :]), reads=[sk(DEN)], writes=[sk(RDEN)])
        ts(sl(NRE), sl(LBR), -1.0, None, ALU.add, None, reads=[sk(LBR)], writes=[sk(NRE)])
        tt(sl(TA), sl(NRE), sl(L_RE), ALU.mult, reads=[sk(NRE), sk(L_RE)], writes=[sk(TA)])
        tt(sl(TB), sl(LBI), sl(L_IM), ALU.mult, reads=[sk(LBI), sk(L_IM)], writes=[sk(TB)])
        tt(sl(TA), sl(TA), sl(TB), ALU.add, reads=[sk(TA), sk(TB)], writes=[sk(TA)])
        tt(sl(KR), sl(TA), sl(RDEN), ALU.mult, reads=[sk(TA), sk(RDEN)], writes=[sk(KR)])
        tt(sl(TA), sl(LBI), sl(L_RE), ALU.mult, reads=[sk(LBI), sk(L_RE)], writes=[sk(TA)])
        tt(sl(TB), sl(NRE), sl(L_IM), ALU.mult, reads=[sk(NRE), sk(L_IM)], writes=[sk(TB)])
        tt(sl(TA), sl(TA), sl(TB), ALU.subtract, reads=[sk(TA), sk(TB)], writes=[sk(TA)])
        tt(sl(KI), sl(TA), sl(RDEN), ALU.mult, reads=[sk(TA), sk(RDEN)], writes=[sk(KI)])
        p.op("dve", lambda h: h.memset(Pre[:, 0, :], 1.0), writes=[("Ppow", 0)])
        p.op("dve", lambda h: h.memset(Pim[:, 0, :], 0.0), writes=[("Ppow", 0)])
        cp(Pre[:, 1, :], sl(LBR), reads=[sk(LBR)], writes=[("Ppow", 1)])
        cp(Pim[:, 1, :], sl(LBI), reads=[sk(LBI)], writes=[("Ppow", 1)])
        for n in range(1, 8):
            rk = [("Ppow", n), sk(LBR), sk(LBI)]
            tt(sl(TA), Pre[:, n, :], sl(LBR), ALU.mult, reads=rk, writes=[sk(TA)])
            tt(sl(TB), Pim[:, n, :], sl(LBI), ALU.mult, reads=rk, writes=[sk(TB)])
            tt(Pre[:, n + 1, :], sl(TA), sl(TB), ALU.subtract, reads=[sk(TA), sk(TB)], writes=[("Ppow", n + 1)])
            tt(sl(TA), Pre[:, n, :], sl(LBI), ALU.mult, reads=rk, writes=[sk(TA)])
            tt(sl(TB), Pim[:, n, :], sl(LBR), ALU.mult, reads=rk, writes=[sk(TB)])
            tt(Pim[:, n + 1, :], sl(TA), sl(TB), ALU.add, reads=[sk(TA), sk(TB), ("Ppow", n + 1)], writes=[("Ppow", n + 1)])
        ts(sl(FF), sl(TRN), 8.0, None, ALU.mult, None, reads=[sk(TRN)], writes=[sk(FF)])
        ts(sl(TA), sl(FF), 1024.0, None, ALU.mult, None, reads=[sk(FF)], writes=[sk(TA)])
        cp(tiv, sl(TA), reads=[sk(TA)], writes=[sk(TI)])
        cp(sl(TF), tiv, reads=[sk(TI)], writes=[sk(TF)])
        ts(sl(FHI), sl(TF), 1.0 / 1024.0, None, ALU.mult, None, reads=[sk(TF)], writes=[sk(FHI)])
        tt(sl(FLO), sl(FF), sl(FHI), ALU.subtract, reads=[sk(FF), sk(FHI)], writes=[sk(FLO)])
        dma("sp", tab_small[l], smallb[:], reads=["smallb"], writes=[("tabd", l, "small")], key="tsmall")

        p.op("dve", lambda h: h.memset(BX[:], 0.0), writes=[("BX", 0), ("BX", 1)])
        p.op("dve", lambda h: h.memset(CX[:], 0.0), writes=[("CX", 0), ("CX", 1)])
        for hp in range(2):
            rs = slice(64 * hp, 64 * hp + 64)
            cs = slice(16 * hp, 16 * hp + 16)
            krb = spf[rs, KR, :].unsqueeze(2).broadcast_to([64, 16, 16])
            kib = spf[rs, KI, :].unsqueeze(2).broadcast_to([64, 16, 16])
            u1 = bt1[rs, 0:256].rearrange("p (q c) -> p q c", q=16)
            u2 = bt2[rs, 0:256].rearrange("p (q c) -> p q c", q=16)
            rk = [("Bs", 0), ("Bs", 1), sk(KR), sk(KI)]
            tt(u1, Bs[rs, 0, :, :], krb, ALU.mult, reads=rk, writes=["bt1"])
            tt(u2, Bs[rs, 1, :, :], kib, ALU.mult, reads=rk, writes=["bt2"])
            tt(BX[rs, 0, :, cs], u1, u2, ALU.subtract, reads=["bt1", "bt2", ("BX", 0)], writes=[("BX", 0)])
            tt(u1, Bs[rs, 1, :, :], krb, ALU.mult, reads=rk, writes=["bt1"])
            tt(u2, Bs[rs, 0, :, :], kib, ALU.mult, reads=rk, writes=["bt2"])
            tt(BX[rs, 1, :, cs], u1, u2, ALU.add, reads=["bt1", "bt2", ("BX", 1)], writes=[("BX", 1)])
        cp(BXb[:, 0, :, :], BX[:, 0, :, :], reads=[("BX", 0)], writes=["BXb"])
        ts(BXb[:, 1, :, :], BX[:, 1, :, :], -1.0, None, ALU.mult, None, reads=[("BX", 1), "BXb"], writes=["BXb"])
        for c in range(2):
            tr_group([(psb[6][:, 64 * o:64 * o + 64], Cld[0:64, c, o, :], ident_f[0:64, 0:64]) for o in range(4)],
                     reads=[("Cld", c), "ident_f"], writes=[("ps", 6)])
            for hp in range(2):
                rs = slice(64 * hp, 64 * hp + 64)
                cs = slice(16 * hp, 16 * hp + 16)
                cp(CX[rs, c, :, cs], psb[6][rs, 0:256].rearrange("p (q c) -> p q c", q=16), reads=[("ps", 6), ("CX", c)], writes=[("CX", c)])

        def pb(tab, n):
            return tab[:, n, :].unsqueeze(2).broadcast_to([128, 16, 32])

        def v16(ap2):
            return ap2.rearrange("p (q c) -> p q c", q=16)
        u1 = bt1[:, 0:512].rearrange("p (q c) -> p q c", q=16)
        u2 = bt2[:, 0:512].rearrange("p (q c) -> p q c", q=16)
        bxk = [("BX", 0), ("BX", 1)]
        cxk = [("CX", 0), ("CX", 1)]
        for n in range(8):
            rk = bxk + [("Ppow", n)]
            tt(u1, BX[:, 0, :, :], pb(Pre, n), ALU.mult, reads=rk, writes=["bt1"])
            tt(u2, BX[:, 1, :, :], pb(Pim, n), ALU.mult, reads=rk, writes=["bt2"])
            tt(v16(BPall[:, n, 0, :]), u1, u2, ALU.subtract, reads=["bt1", "bt2"], writes=[("BPall", n)])
            tt(u1, BX[:, 1, :, :], pb(Pre, n), ALU.mult, reads=rk, writes=["bt1"])
            tt(u2, BX[:, 0, :, :], pb(Pim, n), ALU.mult, reads=rk, writes=["bt2"])
            tt(v16(BPall[:, n, 1, :]), u1, u2, ALU.add, reads=["bt1", "bt2", ("BPall", n)], writes=[("BPall", n)])
        for o in range(4):
            for c in range(2):
                bank = 6 + (o * 2 + c) % 2
                pT = psb_bf(bank)
                tr_group([(pT[:, 128 * s8:128 * s8 + 128], BPall[:, 7 - s8, c, 128 * o:128 * o + 128], ident_b[:]) for s8 in range(8)],
                         reads=[("BPall", n) for n in range(8)] + ["ident_b"], writes=[("ps", bank)])
                cp(BT8all[:, o, c, :, :], pT[:, :].rearrange("p (s m) -> p s m", s=8), reads=[("ps", bank)], writes=[("BT8all", o)])
            dma("sp", tab_bt8[l, o], BT8all[:, o, :, :, :].rearrange("p c s m -> p (c s m)"), reads=[("BT8all", o)],
                writes=[("tabd", l, "bt8", o)], key="tbt8")
        w1 = bftmp[:, 0:512].rearrange("p (q c) -> p q c", q=16)
        w2 = bftmp[:, 512:1024].rearrange("p (q c) -> p q c", q=16)
        for tau in range(9):
            rk = cxk + [("Ppow", tau)]
            tt(w1, CX[:, 0, :, :], pb(Pre, tau), ALU.mult, reads=rk, writes=["bftmp"], eng="pool")
            tt(w2, CX[:, 1, :, :], pb(Pim, tau), ALU.mult, reads=rk, writes=["bftmp"], eng="pool")
            tt(v16(CPall[:, tau, 0, :]), w1, w2, ALU.subtract, reads=["bftmp"], writes=[("CPall", tau)], eng="pool")
            tt(w1, CX[:, 1, :, :], pb(Pre, tau), ALU.mult, reads=rk, writes=["bftmp"], eng="pool")
            tt(w2, CX[:, 0, :, :], pb(Pim, tau), ALU.mult, reads=rk, writes=["bftmp"], eng="pool")
            tt(v16(CPall[:, tau, 1, :]), w1, w2, ALU.add, reads=["bftmp", ("CPall", tau)], writes=[("CPall", tau)], eng="pool")
        for c in range(2):
            cp(CTrall[:, :, c, :, :].rearrange("p q t c -> p t q c"),
               CPall[:, 1:9, c, :].rearrange("p t (q c) -> p t q c", q=16),
               reads=[("CPall", t) for t in range(1, 9)], writes=[("CTrall", c)])
        for o in range(4):
            dma("sp", tab_ctr[l, o], CTrall[:, 4 * o:4 * o + 4, :, :, :].rearrange("p q c t e -> p (q c t e)"),
                reads=[("CTrall", 0), ("CTrall", 1)], writes=[("tabd", l, "ctr", o)], key="tctr")
        for o in range(4):
            for hb in range(2):
                mms = []
                for tl in range(4):
                    tau = 4 * hb + tl
                    mms.append((psb[4 + hb][:, 128 * tl:128 * tl + 128], BXb[:, 0, 4 * o:4 * o + 4, :].rearrange("p q c -> p (q c)"),
                                CPall[:, tau, 0, 128 * o:128 * o + 128], tl == 0, False, {"skip_group_check": True}))
                    mms.append((psb[4 + hb][:, 128 * tl:128 * tl + 128], BXb[:, 1, 4 * o:4 * o + 4, :].rearrange("p q c -> p (q c)"),
                                CPall[:, tau, 1, 128 * o:128 * o + 128], False, True, {"skip_group_check": True}))
                mm_group(mms, reads=["BXb"] + [("CPall", 4 * hb + tl) for tl in range(4)], writes=[("ps", 4 + hb)])
            tt(bt1[:, 0:128], psb[4][:, 0:128], bmask_f[:], ALU.mult, reads=[("ps", 4), "bmask_f"], writes=["bt1"])
            stt(KTall[:, o, 0, :], ident_f[:], spf[:, DCOL, o:o + 1], bt1[:, 0:128], ALU.mult, ALU.add,
                reads=["ident_f", sk(DCOL), "bt1"], writes=[("KTall", o)])
            tt(KTall[:, o, 1:4, :], psb[4][:, 128:512].rearrange("p (t c) -> p t c", t=3),
               bmask_f[:, :].unsqueeze(1).broadcast_to([128, 3, 128]), ALU.mult, reads=[("ps", 4), "bmask_f", ("KTall", o)], writes=[("KTall", o)])
            tt(KTall[:, o, 4:8, :], psb[5][:, :].rearrange("p (t c) -> p t c", t=4),
               bmask_f[:, :].unsqueeze(1).broadcast_to([128, 4, 128]), ALU.mult, reads=[("ps", 5), "bmask_f", ("KTall", o)], writes=[("KTall", o)])
            dma("sp", tab_kt[l, o], KTall[:, o, :, :].rearrange("p t c -> p (t c)"), reads=[("KTall", o)],
                writes=[("tabd", l, "kt", o)], key="tkt")

        p.handoff(["BPall"], ["bcs"])
        jb4 = jidx_f[:, :].unsqueeze(1).broadcast_to([128, 4, 256])
        v3 = lambda t: t[:, :].rearrange("p (q j) -> p q j", q=4)
        for o in range(4):
            fhi = spf[:, FHI, 4 * o:4 * o + 4].unsqueeze(2).broadcast_to([128, 4, 256])
            flo = spf[:, FLO, 4 * o:4 * o + 4].unsqueeze(2).broadcast_to([128, 4, 256])
            tt(v3(bt1), jb4, fhi, ALU.mult, reads=["jidx_f", sk(FHI)], writes=["bt1"])
            cp(bitmp[:], bt1[:], reads=["bt1"], writes=["bitmp"])
            cp(bftmp[:], bitmp[:], reads=["bitmp"], writes=["bftmp"])
            tt(bt1[:], bt1[:], bftmp[:], ALU.subtract, reads=["bt1", "bftmp"], writes=["bt1"])
            tt(v3(bt2), jb4, flo, ALU.mult, reads=["jidx_f", sk(FLO)], writes=["bt2"])
            tt(bt1[:], bt1[:], bt2[:], ALU.add, reads=["bt1", "bt2"], writes=["bt1"])
            cp(bitmp[:], bt1[:], reads=["bt1"], writes=["bitmp"])
            cp(bftmp[:], bitmp[:], reads=["bitmp"], writes=["bftmp"])
            tt(bt2[:], bt1[:], bftmp[:], ALU.subtract, reads=["bt1", "bftmp"], writes=["bt2"])
            act(bcs[:, 1, :, :], v3(bt2), AF.Sin, reads=["bt2"], writes=[("bcs", 1)], scale=TWO_PI)
            ts(bt2[:], bt1[:], 0.25, None, ALU.add, None, reads=["bt1"], writes=["bt2"])
            cp(bitmp[:], bt2[:], reads=["bt2"], writes=["bitmp"])
            cp(bftmp[:], bitmp[:], reads=["bitmp"], writes=["bftmp"])
            tt(bt2[:], bt2[:], bftmp[:], ALU.subtract, reads=["bt2", "bftmp"], writes=["bt2"])
            act(bcs[:, 0, :, :], v3(bt2), AF.Sin, reads=["bt2"], writes=[("bcs", 0)], scale=TWO_PI)
            dma("sp", tab_rot[l, o], bcs[:, :, :, :].rearrange("p c q j -> p (c q j)"), reads=[("bcs", 0), ("bcs", 1)],
                writes=[("tabd", l, "rot", o)], key="trot")

    def ssm_main(l, uT, mixT):
        SB = 48 * KB
        p.handoff(["qT", "kT", "vA", "vA1", "qraw", "rq1", "ebuf", "ep_o", "ep_o1", "ep_ob", "accS", "mixS"], S5_MAIN_BUFS + ["mixS"])
        tabsB = carve(SB, [128, 2048], BF16)
        tabsC = [carve(SB + 4096 + i * 6144, [128, 3072], BF16) for i in range(2)]
        SprevB = [carve(SB + 49408, [128, 4, 2, 258], BF16), carve(SB + 16384, [128, 4, 2, 258], BF16)]
        rotb = carve(SB + 20736, [128, 2, 4, 256], F32)
        r8tab = carve(SB + 28928, [128, 4, 256], F32)
        t1 = carve(SB + 33024, [128, 4, 256], F32)
        t2 = carve(SB + 37120, [128, 4, 256], F32)
        b1 = carve(SB + 41216, [128, 4, 256], F32)
        b2 = carve(SB + 45312, [128, 4, 256], F32)
        yact = [carve(SB + 53536, [128, 8, 128], BF16)]
        gluw = carve(SB + 55584, [128, 4, 512], BF16)
        sg = [carve(SB + 59680 + i * 2048, [128, 512], F32) for i in range(2)]
        small = carve(SB + 63776, [128, 32], F32)

        dma("sp", small[:], tab_small[l], reads=[("tabd", l, "small")], writes=["small"], key="lsmall")
        dma("pool", gluw[:], din["glu_w"][l].rearrange("(o p) n -> p o n", p=128), writes=["gluw"], key="gluw")
        for i_ in range(2):
            p.op("dve", lambda h, i_=i_: h.memset(SprevB[i_][:, :, :, 0:1], 0.0), writes=[("Sprev0", i_)])

        def s5_front(o):
            tb = tabsC[o % 2]
            tk = ("tabsC", o % 2)
            Sprev = SprevB[o % 2]
            spk = ("Sprev", o % 2)
            BT8 = tabsB[:, :].rearrange("p (c s m) -> p c s m", c=2, s=8)
            CTr = tb[:, 0:2048].rearrange("p (q c t e) -> p q c t e", q=4, c=2, t=8)
            KT = tb[:, 2048:3072].rearrange("p (t c) -> p t c", t=8)
            dma("sp", tabsB[:, :], tab_bt8[l, o], reads=[("tabd", l, "bt8", o)], writes=["tabsB"], key="ltabsB")
            p.op("sp", lambda h, inc, tb=tb, o=o, l=l: (
                inc(h.dma_start(out=tb[:, 0:2048], in_=tab_ctr[l, o])),
                inc(h.dma_start(out=tb[:, 2048:3072], in_=tab_kt[l, o]))),
                reads=[("tabd", l, "ctr", o), ("tabd", l, "kt", o)], writes=[tk], dma="ltabs%d" % (o % 2), dma_n=2)
            dma("sp", rotb[:, :, :, :].rearrange("p c q j -> p (c q j)"), tab_rot[l, o], reads=[("tabd", l, "rot", o)], writes=["rotb"], key="lrot")
            cosJ = rotb[:, 0, :, :]
            sinJ = rotb[:, 1, :, :]
            cp(r8tab[:], small[:, 4 * o:4 * o + 4].unsqueeze(2).broadcast_to([128, 4, 256]), reads=["small"], writes=["r8tab"])
            p.op("dve", lambda h: h.memset(r8tab[:, :, 0:1], 0.0), writes=["r8tab"])
            mms = []
            for c in range(2):
                for s8 in range(8):
                    for q4 in range(4):
                        prs = slice(32 * q4, 32 * q4 + 32)
                        mms.append((psb[q4][:, 256 * c:256 * c + 256], BT8[prs, c, s8, :], uT[prs, o, s8:S:8],
                                    c == 0 and s8 == 0, s8 == 7, {"tile_position": (32 * q4, 0), "skip_group_check": True}))
            mm_group(mms, reads=["tabsB"] + [("uT", o, mg) for mg in range(4)], writes=[("ps", q4) for q4 in range(4)])
            br = ps_all[:, 0:4, 0:256]
            bi = ps_all[:, 0:4, 256:512]
            pk = [("ps", i) for i in range(4)]
            tt(t1[:], br, cosJ, ALU.mult, reads=pk + ["rotb"], writes=["t1"])
            tt(t2[:], bi, sinJ, ALU.mult, reads=pk + ["rotb"], writes=["t2"])
            tt(b1[:], t1[:], t2[:], ALU.add, reads=["t1", "t2"], writes=["b1"])
            tt(t1[:], bi, cosJ, ALU.mult, reads=pk + ["rotb"], writes=["t1"])
            tt(t2[:], br, sinJ, ALU.mult, reads=pk + ["rotb"], writes=["t2"])
            tt(b2[:], t1[:], t2[:], ALU.subtract, reads=["t1", "t2"], writes=["b2"])
            f2 = lambda t: t[:, :, :].rearrange("p q j -> p (q j)")
            p.op("dve", lambda h: h.tensor_tensor_scan(f2(t1), f2(r8tab), f2(b1), 0.0, ALU.mult, ALU.add),
                 reads=["b1", "r8tab"], writes=["t1"])
            p.op("dve", lambda h: h.tensor_tensor_scan(f2(t2), f2(r8tab), f2(b2), 0.0, ALU.mult, ALU.add),
                 reads=["b2", "r8tab"], writes=["t2"])
            tt(b1[:], t1[:], cosJ, ALU.mult, reads=["t1", "rotb"], writes=["b1"])
            tt(b2[:], t2[:], sinJ, ALU.mult, reads=["t2", "rotb"], writes=["b2"])
            tt(Sprev[:, :, 0, 1:257], b1[:], b2[:], ALU.subtract, reads=["b1", "b2"], writes=[spk])
            tt(b1[:], t1[:], sinJ, ALU.mult, reads=["t1", "rotb"], writes=["b1"])
            tt(b2[:], t2[:], cosJ, ALU.mult, reads=["t2", "rotb"], writes=["b2"])
            stt(Sprev[:, :, 1, 1:257], b1[:], -1.0, b2[:], ALU.mult, ALU.subtract, reads=["b1", "b2", spk], writes=[spk])

            return tk, KT, CTr, Sprev, spk

        def s5_back(o, tk, KT, CTr, Sprev, spk):
            for jb in range(2):
                bA = 4
                bB = 5
                yi = 0
                mms = []
                sg_ = {"skip_group_check": True}
                for s8 in range(8):
                    lt = uT[:, o, 1024 * jb + s8:1024 * jb + 1024:8]
                    if s8 <= 3:
                        mms.append((psb[bA][:, 128 * s8:512], lt, KT[:, 0:4 - s8, :], s8 == 0, False, sg_))
                    lo = max(0, s8 - 4)
                    mms.append((psb[bB][:, 128 * lo:512], lt, KT[:, max(4 - s8, 0):8 - s8, :], s8 == 0, False, sg_))
                for q4 in range(4):
                    for c in range(2):
                        lt = Sprev[:, q4, c, 128 * jb:128 * jb + 128]
                        last = (q4 == 3 and c == 1)
                        mms.append((psb[bA].rearrange("p (t c) -> p t c", t=4)[:, :, 32 * q4:32 * q4 + 32], lt,
                                    CTr[:, q4, c, 0:4, :], False, last, sg_))
                        mms.append((psb[bB].rearrange("p (t c) -> p t c", t=4)[:, :, 32 * q4:32 * q4 + 32], lt,
                                    CTr[:, q4, c, 4:8, :], False, last, sg_))
                mm_group(mms, reads=[tk, ("Sprev0", o % 2), spk, ("uT", o, 2 * jb), ("uT", o, 2 * jb + 1)],
                         writes=[("ps", bA), ("ps", bB)])
                act(yact[yi][:, 0:4, :], psb[bA].rearrange("p (t c) -> p t c", t=4), AF.Gelu_apprx_tanh,
                    reads=[("ps", bA)], writes=[("yact", yi)])
                act(yact[yi][:, 4:8, :], psb[bB].rearrange("p (t c) -> p t c", t=4), AF.Gelu_apprx_tanh,
                    reads=[("ps", bB)], writes=[("yact", yi)])
                pT = psb_bf(6 + jb)
                tr_group([(pT[:, 128 * t8:128 * t8 + 128], yact[yi][:, t8, :], ident_b[:]) for t8 in range(8)],
                         reads=[("yact", yi), "ident_b"], writes=[("ps", 6 + jb)])
                act(mixT[:, o, 1024 * jb:1024 * jb + 1024].rearrange("p (j t) -> p t j", t=8),
                    pT[:, :].rearrange("p (t j) -> p t j", t=8), AF.Copy, reads=[("ps", 6 + jb)],
                    writes=[("mixS", o, 2 * jb), ("mixS", o, 2 * jb + 1)])


        prev = None
        for o in range(4):
            cur = s5_front(o)
            if prev is not None:
                s5_back(o - 1, *prev)
            prev = cur
        s5_back(3, *prev)

        for m_ in range(4):
            tsl = slice(512 * m_, 512 * m_ + 512)
            for ct in range(4):
                mm_group([(psb[ct][:, :], gluw[:, oo, 128 * ct:128 * ct + 128], mixT[:, oo, tsl], oo == 0, oo == 3, {}) for oo in range(4)],
                         reads=["gluw"] + [("mixS", oo, m_) for oo in range(4)], writes=[("ps", ct)])
            for ct in range(4):
                si = ct % 2
                act(sg[si][:], psb[ct][:, :], AF.Sigmoid, reads=[("ps", ct), "small"], writes=[("sg", si)],
                    bias=small[:, 16 + ct:17 + ct])
                tt(mixT[:, ct, tsl], mixT[:, ct, tsl], sg[si][:], ALU.mult, reads=[("mixS", ct, m_), ("sg", si)], writes=[("mixS", ct, m_)])

    tab_tokens = {}


    for sq_ in range(nseq):
        dma("sp", ltmp[0:8, 0:128], din["c"][sq_].rearrange("(k p) -> k p", p=128), writes=["ltmp"], key="cc")
        tr_group([(psb[7][:, 0:8], ltmp[0:8, 0:128], ident_f[0:8, 0:8])], reads=["ltmp", "ident_f"], writes=[("ps", 7)])
        act(cond_col[:], psb[7][:, 0:8], AF.Silu, reads=[("ps", 7)], writes=["cond_col"])
        cp(cond_all[:, sq_, :], cond_col[:], reads=["cond_col"], writes=["cond_all"])

    mstate = {"n": 0}

    def _modw_load(l_, ch):
        i_ = mstate["n"] % 2
        mstate["n"] += 1
        dma("pool", modw_buf[i_][:], din["mod_w"][l_][:, ch * 128:(ch + 1) * 128].rearrange("(k p) n -> p k n", p=128),
            writes=[("modw", i_)], key="modw%d" % i_)
        return modw_buf[i_], ("modw", i_)

    def modb_prep(l_, par):
        mbt = ltmp[0:48, 0:128]
        dma("sp", mbt, din["mod_b"][l_].rearrange("(c p) -> c p", p=128), writes=["ltmp"], key="mb")
        tr_group([(psb[7][:, 0:48], mbt, ident_f[0:48, 0:48])], reads=["ltmp", "ident_f"], writes=[("ps", 7)])
        cp(modbB[par][:], psb[7][:, 0:48], reads=[("ps", 7)], writes=[("modb", par)])

    def col_compute(lb, sq_, ch, base):
        buf, bkey = lb
        mm_group([(psb[7][:, base + ch:base + ch + 1], buf[:, k, :], cond_all[:, sq_, k:k + 1], k == 0, k == 7, {"skip_group_check": True})
                  for k in range(8)], reads=["cond_all", bkey], writes=[("ps", 7)])

    def gate_init(l_, kind):
        dma("sp", gate_b[:, 0, :], din["mod_b"][l_][kind * D:(kind + 1) * D].unsqueeze(0).broadcast_to([128, D]), writes=["gate_b"], key="gb")

    def gate_compute(lb, ch):
        buf, bkey = lb
        c0 = (ch % 8) * 128
        mm_group([(psb[7][:, 104:232], cond_rep[:, k, :], buf[:, k, :], k == 0, k == 7, {"skip_group_check": True}) for k in range(8)],
                 reads=["cond_rep", bkey], writes=[("ps", 7)])
        stt(gate_b[:, 0, c0:c0 + 128], psb[7][:, 104:232], 1.0, gate_b[:, 0, c0:c0 + 128], ALU.add, ALU.add,
            reads=[("ps", 7), "gate_b"], writes=["gate_b"])

    def fin_cols(par, base, lo, hi, p1lo, p1hi):
        tt(modcolB[par][:, lo:hi], psb[7][:, base + lo:base + hi], modbB[par][:, lo:hi], ALU.add,
           reads=[("ps", 7), ("modb", par)], writes=[("modcol", par)])
        if p1hi > p1lo:
            ts(modcolB[par][:, p1lo:p1hi], modcolB[par][:, p1lo:p1hi], 1.0, None, ALU.add, None, reads=[("modcol", par)], writes=[("modcol", par)])

    def _q_prefetch(q):
        cnt = 0
        for t_ in q:
            if t_[0] in ("col", "gate"):
                if cnt >= 2:
                    break
                cnt += 1
                if "lb" not in t_[-1]:
                    t_[-1]["lb"] = _modw_load(t_[1], t_[3] if t_[0] == "col" else t_[2])

    def run_tasks(q, n):
        while n > 0 and q:
            _q_prefetch(q)
            t_ = q.pop(0)
            if t_[0] == "col":
                col_compute(t_[-1]["lb"], t_[2], t_[3], t_[4])
            elif t_[0] == "gate":
                gate_compute(t_[-1]["lb"], t_[2])
            else:
                t_[1]()
            _q_prefetch(q)
            n -= 1

    def run_until(q, marker_len):
        while len(q) > marker_len:
            run_tasks(q, 1)

    def T_col(l_, sq_, ch, base):
        return ("col", l_, sq_, ch, base, {})

    def T_gate(l_, ch):
        return ("gate", l_, ch, {})

    def T_fn(f):
        return ("fn", f)

    steps = [(sq_, l_) for sq_ in range(nseq) for l_ in range(nl)]
    nchunk = 0
    for sq in range(nseq):
        xv = din["x"][sq].rearrange("(tt p) d -> p tt d", p=128)
        for t4 in range(4):
            dma("sp", x_sb[:, 4 * t4:4 * t4 + 4, :], xv[:, 4 * t4:4 * t4 + 4, :],
                writes=[("x", 4 * t4 + i) for i in range(4)], key="xld%d" % t4)
        cp(cond_rep[:], cond_all[:, sq, :].unsqueeze(2).broadcast_to([128, 8, 128]), reads=["cond_all"], writes=["cond_rep"])

        for l in range(nl):
            lam_init = 0.8 - 0.6 * math.exp(-0.3 * l)
            si_ = steps.index((sq, l))
            par = si_ % 2
            modcol = modcolB[par]
            mck = ("modcol", par)
            if si_ == 0:
                modb_prep(l, par)
                q0 = [T_col(l, sq, ch_, 48) for ch_ in range(16)]
                run_until(q0, 0)
                fin_cols(par, 48, 0, 16, 8, 16)
                if do_ssm:
                    for l_ in range(nl):
                        ssm_build(l_)
            nxt = steps[si_ + 1] if si_ + 1 < len(steps) else None
            qA = [T_fn(lambda: gate_init(l, 2))] + [T_gate(l, ch_) for ch_ in range(16, 24)]
            qA += [T_col(l, sq, ch_, 48) for ch_ in range(24, 40)] + [T_fn(lambda: fin_cols(par, 48, 24, 40, 32, 40))]
            qB = [T_fn(lambda: gate_init(l, 5))] + [T_gate(l, ch_) for ch_ in range(40, 48)]
            if nxt is not None:
                nsq, nl_ = nxt
                qA += [T_fn(lambda: modb_prep(nl_, 1 - par))] + [T_col(nl_, nsq, ch_, 88) for ch_ in range(0, 8)]
                qA += [T_fn(lambda: fin_cols(1 - par, 88, 0, 8, 0, 0))]
                qB += [T_col(nl_, nsq, ch_, 88) for ch_ in range(8, 16)] + [T_fn(lambda: fin_cols(1 - par, 88, 8, 16, 8, 16))]
            nA_gate = len(qA) - 9
            nB_gate = len(qB) - 9

            def build_hT(hT, hkey, tok0, ntile, sc_col, sh_col):
                for t4 in range(ntile // 4):
                    for k in range(8):
                        bank = (t4 * 8 + k) % 2
                        tr_group([(psb[bank][:, 128 * i:128 * i + 128], x_sb[:, tok0 + 4 * t4 + i, 128 * k:128 * k + 128], ident_f[:])
                                  for i in range(4)],
                                 reads=[("x", tok0 + 4 * t4 + i) for i in range(4)] + ["ident_f"], writes=[("ps", bank)])
                        act(hT[:, k, 512 * t4:512 * t4 + 512], psb[bank][:, :], AF.Identity,
                            reads=[("ps", bank), mck], writes=[(hkey, k, t4)],
                            scale=modcol[:, sc_col + k:sc_col + k + 1], bias=modcol[:, sh_col + k:sh_col + k + 1])

            uT = carve(0, [128, 4, S], BF16)
            mixT = carve(16 * KB, [128, 8, S], BF16)
            qT = carve(48 * KB, [128, 4, S], BF16)
            kT = carve(64 * KB, [128, 4, S], BF16)
            vA = carve(80 * KB, [128, NT, 4, 130], BF16)
            hT = carve(16 * KB, [128, 8, 1024], BF16)
            wch = [carve(32 * KB + i * 2 * KB, [128, 8, 128], BF16) for i in range(2)] + \
                  [carve(102 * KB + 256 + i * 2 * KB, [128, 8, 128], BF16) for i in range(4)]
            rotc = carve(36 * KB, [128, 1024], F32)
            rots = carve(40 * KB, [128, 1024], F32)
            rtmp_i = carve(44 * KB, [128, 1024], F32).bitcast(I32)
            qraw = [carve(96 * KB + 256 + i * KB, [128, 512], BF16) for i in range(2)]
            rq1 = [carve(98 * KB + 256 + i * 2 * KB, [128, 512], F32) for i in range(2)]
            p.handoff(["actT", "hT2", "wd", "gu"] + S5_BUILD_BUFS, ["uT", "mixT", "qT", "kT", "vA", "vA1", "hT", "wch", "rotc", "rots", "rtmp",
                                                     "qraw", "rq1"])

            p.op("pool", lambda h: h.memset(vA[:, :, :, 128:130], 1.0), writes=[("vA1",)])

            NWB = len(wch)
            NPRE = 4
            wlist = [(hf, ct_) for hf in range(2) for ct_ in range(16)
                     if not ((ct_ < 4 and not do_ssm) or (ct_ >= 4 and not do_attn))]
            wissued = [0]

            def issue_w(upto):
                while wissued[0] < min(upto, len(wlist)):
                    i_ = wissued[0]
                    ct_ = wlist[i_][1]
                    dma("pool", wch[i_ % NWB][:], din["w_in"][l][:, ct_ * 128:(ct_ + 1) * 128].rearrange("(k p) n -> p k n", p=128),
                        writes=[("wch", i_ % NWB)], key="wch%d" % (i_ % NWB))
                    wissued[0] += 1
            issue_w(NPRE)
            wchn = 0
            for half in range(2):
                tok0 = 8 * half
                build_hT(hT, "hT", tok0, 8, 8, 0)
                if do_attn:
                    posb = din["positions"][sq][1024 * half:1024 * half + 1024].unsqueeze(0).broadcast_to([128, 1024])
                    dma("pool", rotc[:], posb, writes=["rotc"], key="rotc")
                    ts(rots[:], rotc[:], ropef[:, 1:2], None, ALU.mult, None, reads=["rotc", "ropef"], writes=["rots"])
                    ts(rotc[:], rotc[:], ropef[:, 0:1], 0.25, ALU.mult, ALU.add, reads=["rotc", "ropef"], writes=["rotc"])
                    for nm, tb in (("rotc", rotc), ("rots", rots)):
                        cp(rtmp_i[:], tb[:], reads=[nm], writes=["rtmp"])
                        cp(ltmp[:], rtmp_i[:], reads=["rtmp"], writes=["ltmp"])
                        tt(tb[:], tb[:], ltmp[:], ALU.subtract, reads=[nm, "ltmp"], writes=[nm])
                        act(tb[:], tb[:], AF.Sin, reads=[nm], writes=[nm], scale=TWO_PI)
                for ct in range(16):
                    if (ct < 4 and not do_ssm) or (ct >= 4 and not do_attn):
                        continue
                    assert wlist[wchn] == (half, ct)
                    wb = wch[wchn % NWB]
                    wkey = ("wch", wchn % NWB)
                    issue_w(wchn + 1 + NPRE)
                    wchn += 1
                    if ct < 12:
                        for mm in range(2):
                            bank = 2 + (ct * 2 + mm) % 2
                            mm_group([(psb[bank][:, :], wb[:, k, :], hT[:, k, 512 * mm:512 * mm + 512], k == 0, k == 7, {})
                                      for k in range(8)],
                                     reads=[wkey] + [("hT", k, mm) for k in range(8)], writes=[("ps", bank)])
                            tsl = slice(1024 * half + 512 * mm, 1024 * half + 512 * mm + 512)
                            mg = 2 * half + mm
                            if ct < 4:
                                act(uT[:, ct, tsl], psb[bank][:, :], AF.Copy, reads=[("ps", bank)], writes=[("uT", ct, mg)])
                            else:
                                isq = ct < 8
                                dst = qT if isq else kT
                                dkey = ("qT" if isq else "kT", ct % 4, mg)
                                qi = (ct * 2 + mm) % 2
                                act(qraw[qi][:], psb[bank][:, :], AF.Copy, reads=[("ps", bank)], writes=[("qraw", qi)],
                                    scale=(0.125 if isq else 1.0))
                                rb = 4 + qi
                                mm_group([(psb[rb][:, :], rmat_b[:], qraw[qi][:], True, True, {})],
                                         reads=["rmat_b", ("qraw", qi)], writes=[("ps", rb)])
                                rsl = slice(512 * mm, 512 * mm + 512)
                                tt(rq1[qi][:], qraw[qi][:], rotc[:, rsl], ALU.mult, reads=[("qraw", qi), "rotc"], writes=[("rq1", qi)], eng="pool")
                                tt(dst[:, ct % 4, tsl], psb[rb][:, :], rots[:, rsl], ALU.mult, reads=[("ps", rb), "rots"], writes=[dkey])
                                tt(dst[:, ct % 4, tsl], dst[:, ct % 4, tsl], rq1[qi][:], ALU.add, reads=[dkey, ("rq1", qi)], writes=[dkey])
                    else:
                        hd = ct - 12
                        for tl in range(8):
                            bank = 2 + tl % 2
                            mm_group([(psb[bank][:, 0:128], hT[:, k, 128 * tl:128 * tl + 128], wb[:, k, :], k == 0, k == 7, {})
                                      for k in range(8)],
                                     reads=[wkey] + [("hT", k, tl // 4) for k in range(8)], writes=[("ps", bank)])
                            act(vA[:, tok0 + tl, hd, 0:128], psb[bank][:, 0:128], AF.Copy, reads=[("ps", bank)],
                                writes=[("vA", tok0 + tl, hd)])

            if dbg and "dbg_q" in dbg_out and sq == 0 and l == 0 and do_attn:
                for nm, tb, kn in (("dbg_q", qT, "qT"), ("dbg_k", kT, "kT")):
                    cp(ltmp[:, 0:512], tb[:, 0, 0:512], reads=[(kn, 0, 0)], writes=["ltmp"])
                    fin.append(dma("sp", dbg_out[nm], ltmp[:, 0:512], reads=["ltmp"], key="dbgq"))
                cp(ltmp[:, 0:130], vA[:, 0, 0, :], reads=[("vA", 0, 0), ("vA1",)], writes=["ltmp"])
                fin.append(dma("sp", dbg_out["dbg_v"], ltmp[:, 0:130], reads=["ltmp"], key="dbgq"))
            p.handoff(["hT", "wch", "rotc", "rots", "rtmp", "mixT"], ["ebuf", "ep_o", "ep_o1", "ep_ob", "accS", "mixT", "mixS"])
            if do_attn:
                ebuf = [carve(16 * KB + i * KB, [128, 512], BF16) for i in range(4)]
                ep_o = carve(20 * KB, [128, 4, 128], F32)
                ep_o1 = carve(22 * KB, [128, 4, 128], F32)
                ep_ob = carve(24 * KB, [128, 4, 128], BF16)
                accS = carve(25 * KB, [128, 2, 4, 130], F32)
                for i, nm in enumerate(("lam_q1", "lam_k1", "lam_q2", "lam_k2")):
                    dma("sp", lamv[:, i, :], din[nm][l].unsqueeze(0).broadcast_to([128, 64]), writes=[("lamv", i)], key="lamv%d" % i)
                dma("sp", gain_b[:], din["subln_w"][l].unsqueeze(0).broadcast_to([128, 128]), writes=["gain_b"], key="gain")
                ts(gain_b[:], gain_b[:], float(1.0 - lam_init), None, ALU.mult, None, reads=["gain_b"], writes=["gain_b"])
                tt(lamv[:, 0, :], lamv[:, 0, :], lamv[:, 1, :], ALU.mult, reads=[("lamv", 0), ("lamv", 1)], writes=[("lamv", 0)])
                tt(lamv[:, 2, :], lamv[:, 2, :], lamv[:, 3, :], ALU.mult, reads=[("lamv", 2), ("lamv", 3)], writes=[("lamv", 2)])
                p.op("dve", lambda h: h.reduce_sum(lams[:, 0:1], lamv[:, 0, :], AX.X), reads=[("lamv", 0)], writes=[("lams", 0)])
                p.op("dve", lambda h: h.reduce_sum(lams[:, 1:2], lamv[:, 2, :], AX.X), reads=[("lamv", 2)], writes=[("lams", 1)])
                act(lams[:, 2:4], lams[:, 0:2], AF.Exp, reads=[("lams", 0), ("lams", 1)], writes=[("lams", 2)])
                tt(lams[:, 4:5], lams[:, 2:3], lams[:, 3:4], ALU.subtract, reads=[("lams", 2)], writes=[("lams", 4)])
                ts(lams[:, 5:6], lams[:, 4:5], -1.0, float(-lam_init), ALU.mult, ALU.add, reads=[("lams", 4)], writes=[("lams", 5)])

                qz = carve(96 * KB + 256, [128, 4, S], BF16)
                p.handoff(["qraw", "rq1", "wch"], ["qz"])
                p.op("pool", lambda h: h.memset(qz[0:64, :, :], 0.0), writes=[("qz", hd_) for hd_ in range(4)])
                for hd_ in range(4):
                    qk_ = [("qT", hd_, Q_) for Q_ in range(4)]
                    act(qz[64:128, hd_, :], qT[64:128, hd_, :], AF.Copy, reads=qk_ + [("qz", hd_)], writes=[("qz", hd_)])
                    p.op("pool", lambda h, hd_=hd_: h.memset(qT[64:128, hd_, :], 0.0), reads=qk_, writes=qk_)
                ecnt = 0
                pending_tail = []
                for hd in range(4):
                    for Q in range(4):
                        nkt = 4 * (Q + 1)
                        for m in range(2):
                            acc = [psb[2 + 2 * m], psb[3 + 2 * m]]
                            acck = [("ps", 2 + 2 * m), ("ps", 3 + 2 * m)]
                            prs = slice(64 * m, 64 * m + 64)

                            def accv(qs):
                                return acc[qs // 2][:, 256 * (qs % 2):256 * (qs % 2) + 130]

                            def issue_s(kt):
                                nonlocal ecnt
                                o = kt - 4 * Q
                                c0 = 128 * o if o > 0 else 0
                                sb_i = (0, 1, 6)[kt % 3]
                                mms_ = []
                                if o >= 0:
                                    mms_.append((psb[sb_i][:, c0:c0 + 128], ident_b[:], negtri_b[:], True, False, {"skip_group_check": True}))
                                qsrc = qT if m == 0 else qz
                                mms_.append((psb[sb_i][:, c0:512], kT[:, hd, 128 * kt:128 * kt + 128],
                                             qsrc[:, hd, 512 * Q + c0:512 * Q + 512], o < 0, True, {"skip_group_check": True}))
                                mm_group(mms_, reads=[("kT", hd, kt // 4), ("qT", hd, Q), ("qz", hd), "ident_b", "negtri_b"], writes=[("ps", sb_i)])
                                ei = ecnt % 4
                                ecnt += 1
                                act(ebuf[ei][:, c0:512], psb[sb_i][:, c0:512], AF.Exp, reads=[("ps", sb_i)], writes=[("ebuf", ei)])
                                return ei, o

                            def issue_pv(kt, ei, o):
                                qs0 = max(o, 0)
                                mms = []
                                for qs in range(qs0, 4):
                                    last_kt = 4 * Q + qs
                                    mms.append((accv(qs), ebuf[ei][:, 128 * qs:128 * qs + 128], vA[:, kt, hd, :],
                                                kt == 0 and qs % 2 == 0, kt == last_kt, {"skip_group_check": True}))
                                mm_group(mms, reads=[("ebuf", ei), ("vA", kt, hd), ("vA1",)], writes=acck)

                            run_tasks(qA, 1)
                            pendq = [issue_s(kt_) for kt_ in range(min(2, nkt))]
                            for kt in range(nkt):
                                if kt + 2 < nkt:
                                    pendq.append(issue_s(kt + 2))
                                issue_pv(kt, *pendq.pop(0))
                        while len(pending_tail) > 0:
                            pending_tail.pop(0)()
                        for bi_ in range(4):
                            cp(accS[:, bi_ // 2, 2 * (bi_ % 2):2 * (bi_ % 2) + 2, :],
                               psb[2 + bi_][:, :].rearrange("p (a b) -> p a b", a=2)[:, :, 0:130],
                               reads=[("ps", 2 + bi_)], writes=[("accS", bi_)])
                        z0 = accS[:, 0, :, 128]
                        z1 = accS[:, 1, :, 128]
                        ak = [("accS", i_) for i_ in range(4)]
                        p.op("dve", lambda h, z0=z0: h.reciprocal(ep_z[:, 0, :], z0), reads=ak, writes=[("ep_z", 0)])
                        p.op("dve", lambda h, z1=z1: h.reciprocal(ep_z[:, 1, :], z1), reads=ak, writes=[("ep_z", 1)])
                        ts(ep_z[:, 1, :], ep_z[:, 1, :], lams[:, 5:6], None, ALU.mult, None,
                           reads=[("ep_z", 1), ("lams", 5)], writes=[("ep_z", 1)])
                        tt(ep_o1[:], accS[:, 1, :, 0:128], ep_z[:, 1, :].unsqueeze(2).broadcast_to([128, 4, 128]), ALU.mult,
                           reads=ak + [("ep_z", 1)], writes=[("ep_o1", 0), ("ep_o1", 1)])
                        tt(ep_o[:], accS[:, 0, :, 0:128], ep_z[:, 0, :].unsqueeze(2).broadcast_to([128, 4, 128]), ALU.mult,
                           reads=ak + [("ep_z", 0)], writes=[("ep_o", 0), ("ep_o", 1)])
                        tt(ep_o[:], ep_o[:], ep_o1[:], ALU.add, reads=[("ep_o", 0), ("ep_o", 1), ("ep_o1", 0), ("ep_o1", 1)],
                           writes=[("ep_o", 0), ("ep_o", 1)])
                        tt(ep_o1[:], ep_o[:], ep_o[:], ALU.mult, reads=[("ep_o", 0), ("ep_o", 1)], writes=[("ep_o1", 0), ("ep_o1", 1)])
                        p.op("dve", lambda h: h.reduce_sum(ep_ss[:], ep_o1[:], AX.X), reads=[("ep_o1", 0), ("ep_o1", 1)], writes=["ep_ss"])
                        def ep_tail(hd=hd, Q=Q):
                            act(ep_ss[:], ep_ss[:], AF.Sqrt, reads=["ep_ss"], writes=["ep_ss"], scale=1.0 / 128.0, bias=1e-5)
                            p.op("dve", lambda h: h.reciprocal(ep_ss[:], ep_ss[:]), reads=["ep_ss"], writes=["ep_ss"])
                            tt(ep_o[:], ep_o[:], ep_ss[:, :].unsqueeze(2).broadcast_to([128, 4, 128]), ALU.mult,
                               reads=[("ep_o", 0), ("ep_o", 1), "ep_ss"], writes=[("ep_o", 0), ("ep_o", 1)])
                            tt(ep_ob[:], ep_o[:], gain_b[:, :].unsqueeze(1).broadcast_to([128, 4, 128]), ALU.mult,
                               reads=[("ep_o", 0), ("ep_o", 1), "gain_b"], writes=["ep_ob"])
                            pT = psb[7][:, 256:512].bitcast(BF16)
                            tr_group([(pT[:, 128 * qs:128 * qs + 128], ep_ob[:, qs, :], ident_b[:]) for qs in range(4)],
                                     reads=["ep_ob", "ident_b"], writes=[("ps", 7)])
                            act(mixT[:, 4 + hd, 512 * Q:512 * Q + 512], pT[:, 0:512], AF.Copy, reads=[("ps", 7)], writes=[("mixT", 4 + hd, Q)])
                        pending_tail.append(ep_tail)
                        if (hd * 4 + Q) < 8:
                            run_tasks(qA, 1)
                while len(pending_tail) > 0:
                    pending_tail.pop(0)()
                run_until(qA, 0)
            else:
                p.op("pool", lambda h: h.memset(mixT[:, 4:8, :], 0.0), writes=[("mixT", 4 + hd, Q) for hd in range(4) for Q in range(4)])

            if do_ssm:
                ssm_main(l, uT, mixT)
            else:
                p.handoff(["ebuf", "ep_o", "ep_o1", "ep_ob", "accS", "mixS"], ["mixS"])
                p.op("pool", lambda h: h.memset(mixT[:, 0:4, :], 0.0), writes=[("mixS", ct, Q) for ct in range(4) for Q in range(4)])

            if dbg and "dbg_mix" in dbg_out and sq == 0 and l == 0:
                for kk in range(8):
                    cp(ltmp[:, 0:512], mixT[:, kk, 0:512], reads=[("mixT" if kk >= 4 else "mixS", kk, 0)], writes=["ltmp"])
                    fin.append(dma("sp", dbg_out["dbg_mix"][kk * 128:(kk + 1) * 128, :], ltmp[:, 0:512], reads=["ltmp"], key="dbgm"))

            run_until(qA, nA_gate)
            wo = carve(96 * KB + 256, [128, 8, D], BF16)
            p.handoff(["qraw", "rq1"] + S5_MAIN_BUFS, ["wo"])
            dma("pool", wo[:], din["w_out"][l].rearrange("(k p) n -> p k n", p=128), writes=["wo"], key="wo")
            dma("sp", lngb[:, 0, :], din["ln1_g"][l].unsqueeze(0).broadcast_to([128, D]), writes=[("lngb", 0)], key="lng")
            dma("sp", lngb[:, 1, :], din["ln1_b"][l].unsqueeze(0).broadcast_to([128, D]), writes=[("lngb", 1)], key="lnb")

            def resid_ln(tt_i, banks, gi):
                xk = ("x", tt_i)
                xs = x_sb[:, tt_i, :]
                for hb in range(2):
                    tt(ltmp[:, 512 * hb:512 * hb + 512], psb[banks[hb]][:, :], gate_b[:, 0, 512 * hb:512 * hb + 512], ALU.mult,
                       reads=[("ps", banks[hb]), "gate_b"], writes=["ltmp"])
                stt(xs, xs, ALPHA, ltmp[:], ALU.mult, ALU.add, reads=[xk, "ltmp"], writes=[xk])
                for hb in range(2):
                    p.op("dve", lambda h, hb=hb: h.bn_stats(st_sb[:, 6 * hb:6 * hb + 6], x_sb[:, tt_i, 512 * hb:512 * hb + 512]),
                         reads=[xk], writes=[("st", hb)])
                p.op("dve", lambda h: h.bn_aggr(mv_sb[:], st_sb[:]), reads=[("st", 0), ("st", 1)], writes=["mv"])
                act(rs_sb[:, 0:1], mv_sb[:, 1:2], AF.Sqrt, reads=["mv"], writes=["rs"], bias=1e-5, scale=1.0)
                p.op("dve", lambda h: h.reciprocal(rs_sb[:, 0:1], rs_sb[:, 0:1]), reads=["rs"], writes=["rs"])
                stt(xs, xs, mv_sb[:, 0:1], lngb[:, 0, :], ALU.subtract, ALU.mult, reads=[xk, "mv", ("lngb", 0)], writes=[xk])
                stt(xs, xs, rs_sb[:, 0:1], lngb[:, 1, :], ALU.mult, ALU.add, reads=[xk, "rs", ("lngb", 1)], writes=[xk])

            for tt_i in range(NT):
                banks = (0, 1) if tt_i % 2 == 0 else (6, 7)
                for hb in range(2):
                    mm_group([(psb[banks[hb]][:, :], mixT[:, k, 128 * tt_i:128 * tt_i + 128], wo[:, k, 512 * hb:512 * hb + 512],
                               k == 0, k == 7, {}) for k in range(8)],
                             reads=["wo"] + [("mixT" if k >= 4 else "mixS", k, tt_i // 4) for k in range(8)], writes=[("ps", banks[hb])])
                resid_ln(tt_i, banks, 0)

            if dbg and "dbg_x1" in dbg_out and sq == 0 and l == 0:
                fin.append(dma("sp", dbg_out["dbg_x1"], x_sb[:, 0, :], reads=[("x", 0)], key="dbgx1"))

            hT2 = carve(0, [128, 8, 1024], BF16)
            actT = carve(16 * KB, [128, NF, 1024], BF16)
            wd = carve(60 * KB, [128, NF, D], BF16)
            gu = [carve(104 * KB + i * 4 * KB, [128, 2, 8, 128], BF16) for i in range(2)]
            p.handoff(["uT", "mixT", "mixS", "qT", "kT", "vA", "ebuf", "ep_o", "ep_o1", "ep_ob", "accS", "wo", "qraw", "rq1", "vA1"] + S5_MAIN_BUFS + [
                       "hT", "wch", "rotc", "rots", "rtmp"],
                      ["hT2", "actT", "wd", "gu"])
            dma("pool", wd[:], din["ffn_w_down"][l].rearrange("(f p) n -> p f n", p=128), writes=["wd"], key="wd")
            run_until(qA, 0)
            dma("sp", lngb[:, 0, :], din["ln2_g"][l].unsqueeze(0).broadcast_to([128, D]), writes=[("lngb", 0)], key="lng")
            dma("sp", lngb[:, 1, :], din["ln2_b"][l].unsqueeze(0).broadcast_to([128, D]), writes=[("lngb", 1)], key="lnb")
            gun = 0

            def issue_gu(i_):
                if i_ >= 2 * NF:
                    return
                gb_ = gu[i_ % 2]
                fs_ = slice(128 * (i_ % NF), 128 * (i_ % NF) + 128)
                p.op("pool", lambda h, inc, gb_=gb_, fs_=fs_, l=l: (
                    inc(h.dma_start(out=gb_[:, 0, :, :], in_=din["ffn_w_gate"][l][:, fs_].rearrange("(k p) n -> p k n", p=128))),
                    inc(h.dma_start(out=gb_[:, 1, :, :], in_=din["ffn_w_up"][l][:, fs_].rearrange("(k p) n -> p k n", p=128)))),
                    writes=[("gu", i_ % 2)], dma="gu%d" % (i_ % 2), dma_n=2)
            issue_gu(0)
            issue_gu(1)
            for half in range(2):
                tok0 = 8 * half
                build_hT(hT2, "hT2", tok0, 8, 32, 24)
                for f in range(NF):
                    gb = gu[gun % 2]
                    gkey = ("gu", gun % 2)
                    gun += 1
                    for mm in range(2):
                        bg = (f * 2 + mm) % 2
                        bu = 2 + bg
                        mm_group([(psb[bg][:, :], gb[:, 0, k, :], hT2[:, k, 512 * mm:512 * mm + 512], k == 0, k == 7, {}) for k in range(8)],
                                 reads=[gkey] + [("hT2", k, mm) for k in range(8)], writes=[("ps", bg)])
                        mm_group([(psb[bu][:, :], gb[:, 1, k, :], hT2[:, k, 512 * mm:512 * mm + 512], k == 0, k == 7, {}) for k in range(8)],
                                 reads=[gkey] + [("hT2", k, mm) for k in range(8)], writes=[("ps", bu)])
                        asl = actT[:, f, 512 * mm:512 * mm + 512]
                        act(stmp[bg][:], psb[bg][:, :], AF.Silu, reads=[("ps", bg)], writes=[("stmp", bg)])
                        tt(asl, stmp[bg][:], psb[bu][:, :], ALU.mult, reads=[("stmp", bg), ("ps", bu)], writes=[("actT", f, mm)])
                    issue_gu(gun + 1)
                    run_tasks(qB, 1)
                if half == 0:
                    run_until(qB, nB_gate)
                for tl in range(8):
                    tt_i = tok0 + tl
                    banks = (4, 5) if tl % 2 == 0 else (6, 7)
                    for hb in range(2):
                        mm_group([(psb[banks[hb]][:, :], actT[:, f, 128 * tl:128 * tl + 128], wd[:, f, 512 * hb:512 * hb + 512],
                                   f == 0, f == NF - 1, {}) for f in range(NF)],
                                 reads=["wd"] + [("actT", f, tl // 4) for f in range(NF)], writes=[("ps", banks[hb])])
                    resid_ln(tt_i, banks, 1)
            run_until(qB, 0)

        yv = y_out[sq].rearrange("(tt p) d -> p tt d", p=128)
        for t4 in range(4):
            fin.append(dma("sp", yv[:, 4 * t4:4 * t4 + 4, :], x_sb[:, 4 * t4:4 * t4 + 4, :],
                           reads=[("x", 4 * t4 + i) for i in range(4)], key="yst%d" % t4))

    p.emit(fin)
    p.close()
    return nc


def ssm_block(p, din, l, env):
    raise NotImplementedError


_NC_CACHE = {}


def kernel(**inputs):
    n = 8
    if "nc" not in _NC_CACHE:
        _NC_CACHE["nc"] = build_program()
    nc = _NC_CACHE["nc"]
    hc = host_consts()
    in_maps = []
    for c in range(n):
        m = {"x": np.ascontiguousarray(inputs["x"][2 * c:2 * c + 2], dtype=np.float32),
             "c": np.ascontiguousarray(inputs["c"][2 * c:2 * c + 2], dtype=np.float32),
             "positions": np.ascontiguousarray(inputs["positions"][2 * c:2 * c + 2], dtype=np.int32)}
        for name, _ in PARAM_SPECS:
            m[name] = np.ascontiguousarray(inputs[name], dtype=np.float32)
        for k_, v_ in hc.items():
            m[k_] = v_
        in_maps.append(m)
    res = run_bass_kernel_spmd(nc, in_maps, core_ids=list(range(n)))
    out = np.concatenate([np.asarray(r["y"]) for r in res.results], axis=0)
    return out.astype(np.float32)
```

```python
import contextlib
import math
import numpy as np
import ml_dtypes
import concourse.bass as bass
import concourse.mybir as mybir
from concourse.bass_utils import run_bass_kernel_spmd

F32 = mybir.dt.float32
BF16 = mybir.dt.bfloat16
I32 = mybir.dt.int32
ALU = mybir.AluOpType
AF = mybir.ActivationFunctionType
AX = mybir.AxisListType

D = 1024
S = 2048
NT = 16
DFF = 2816
NF = 22
DEPTH = 4
ALPHA = float((2 * DEPTH) ** 0.25)
TWO_PI = float(2 * np.pi)
ENG = ["pe", "act", "dve", "pool", "sp"]
EPOCH = 8192


class Prog:
    def __init__(self, nc):
        self.nc = nc
        self.stack = contextlib.ExitStack()
        self.ops = {e: [] for e in ENG}
        self.count = {e: 0 for e in ENG}
        self.seen = {e: {} for e in ENG}
        self.lastw = {}
        self.readers = {}
        self.dma_val = {}
        self.sems = {}
        self.pending = {}
        self.bufkeys = {}

    def sb(self, name, shape, dt):
        return self.stack.enter_context(self.nc.sbuf_tensor("s_" + name, list(shape), dt))

    def ps(self, name, shape, dt):
        return self.stack.enter_context(self.nc.psum_tensor(name, list(shape), dt))

    def sem(self, name):
        if name not in self.sems:
            self.sems[name] = self.stack.enter_context(self.nc.semaphore(name))
        return self.sems[name]

    def _touch(self, k):
        b = k[0] if isinstance(k, tuple) else k
        ks = self.bufkeys.setdefault(b, set())
        if k not in ks:
            ks.add(k)
            pend = self.pending.get(b)
            if pend and k not in self.lastw and k not in self.readers:
                self.readers[k] = [(s, v) for s, v in pend.items()]

    def handoff(self, old_bufs, new_bufs):
        t = {}
        for b in old_bufs:
            for k in self.bufkeys.get(b, ()):
                toks = list(self.readers.get(k, ()))
                if self.lastw.get(k) is not None:
                    toks.append(self.lastw[k])
                for s, v in toks:
                    if t.get(s, 0) < v:
                        t[s] = v
            for s, v in self.pending.get(b, {}).items():
                if t.get(s, 0) < v:
                    t[s] = v
        for b in new_bufs:
            for k in self.bufkeys.get(b, ()):
                self.lastw.pop(k, None)
                self.readers.pop(k, None)
            self.bufkeys[b] = set()
            self.pending[b] = dict(t)

    def _need(self, eng, tok, waits):
        if tok is None:
            return
        stream, val = tok
        if self.seen[eng].get(stream, 0) >= val:
            return
        self.seen[eng][stream] = val
        waits.append(tok)

    def op(self, eng, fn, reads=(), writes=(), dma=None, dma_n=1):
        waits = []
        is_dma = dma is not None
        for k in reads:
            self._touch(k)
        for k in writes:
            self._touch(k)
        for k in reads:
            tok = self.lastw.get(k)
            if tok is not None:
                if (not is_dma) and tok[0] == ("eng", eng) and eng == "pe":
                    pass
                else:
                    self._need(eng, tok, waits)
            else:
                for tok2 in self.readers.get(k, ()):
                    pass
        for k in writes:
            tok = self.lastw.get(k)
            if tok is not None:
                if (not is_dma) and tok[0] == ("eng", eng):
                    pass
                else:
                    self._need(eng, tok, waits)
            for tok in self.readers.get(k, ()):
                if (not is_dma) and tok[0] == ("eng", eng):
                    continue
                self._need(eng, tok, waits)
        if is_dma:
            prev = self.dma_val.get(dma, 0)
            if prev:
                self._need(eng, (("dma", dma), prev), waits)
            val = prev + 16 * dma_n
            self.dma_val[dma] = val
            mytok = (("dma", dma), val)
            self.ops[eng].append((waits, fn, ("dma", dma)))
        else:
            self.count[eng] += 1
            mytok = (("eng", eng), self.count[eng])
            self.ops[eng].append((waits, fn, ("eng", self.count[eng])))
        for k in writes:
            self.lastw[k] = mytok
            self.readers[k] = []
        for k in reads:
            self.readers.setdefault(k, []).append(mytok)
        return mytok

    def _semfor(self, stream, val):
        if stream[0] == "dma":
            return self.sem("d_" + str(stream[1])), val
        e = stream[1]
        ep = (val - 1) // EPOCH
        return self.sem("e_%s_%d" % (e, ep)), (val - 1) % EPOCH + 1

    def emit(self, final_tokens):
        nc = self.nc
        for e in ENG:
            for waits, fn, inc in self.ops[e]:
                for stream, val in waits:
                    self._semfor(stream, val)
                if inc[0] == "dma":
                    self.sem("d_" + str(inc[1]))
                else:
                    self._semfor(("eng", e), inc[1])
        for stream, val in final_tokens:
            self._semfor(stream, val)
        prog = self
        with nc.Block() as block:
            def replay(e, handle):
                for waits, fn, inc in prog.ops[e]:
                    for stream, val in waits:
                        s, v = prog._semfor(stream, val)
                        handle.wait_ge(s, v)
                    if inc[0] == "dma":
                        s = prog.sem("d_" + str(inc[1]))
                        fn(handle, lambda ins, s=s: ins.then_inc(s, 16))
                    else:
                        ins = fn(handle)
                        s, v = prog._semfor(("eng", e), inc[1])
                        ins.then_inc(s, 1)
                if e == "sp":
                    for stream, val in final_tokens:
                        s, v = prog._semfor(stream, val)
                        handle.wait_ge(s, v)

            @block.tensor
            def _(h):
                replay("pe", h)

            @block.scalar
            def _(h):
                replay("act", h)

            @block.vector
            def _(h):
                replay("dve", h)

            @block.gpsimd
            def _(h):
                replay("pool", h)

            @block.sync
            def _(h):
                replay("sp", h)

    def close(self):
        self.stack.close()


PARAM_SPECS = [
    ("mod_w", [DEPTH, D, 6 * D]), ("mod_b", [DEPTH, 6 * D]), ("w_in", [DEPTH, D, 2048]),
    ("ssm_a_re", [DEPTH, 32, 64]), ("ssm_a_im", [DEPTH, 32, 64]), ("ssm_log_step", [DEPTH, 32]),
    ("ssm_b_re", [DEPTH, 32, 64, 16]), ("ssm_b_im", [DEPTH, 32, 64, 16]),
    ("ssm_c_re", [DEPTH, 32, 16, 64]), ("ssm_c_im", [DEPTH, 32, 16, 64]), ("ssm_d", [DEPTH, 32, 16]),
    ("glu_w", [DEPTH, 512, 512]), ("glu_b", [DEPTH, 512]),
    ("lam_q1", [DEPTH, 64]), ("lam_k1", [DEPTH, 64]), ("lam_q2", [DEPTH, 64]), ("lam_k2", [DEPTH, 64]),
    ("subln_w", [DEPTH, 128]), ("w_out", [DEPTH, D, D]), ("ln1_g", [DEPTH, D]), ("ln1_b", [DEPTH, D]),
    ("ffn_w_gate", [DEPTH, D, DFF]), ("ffn_w_up", [DEPTH, D, DFF]), ("ffn_w_down", [DEPTH, DFF, D]),
    ("ln2_g", [DEPTH, D]), ("ln2_b", [DEPTH, D]),
]


def host_consts():
    c = {}
    c["ident"] = np.eye(128, dtype=np.float32)
    R = np.zeros((128, 128), np.float32)
    for i in range(128):
        d = i % 64
        if d < 8:
            R[i + 8, i] = 1.0
        elif d < 16:
            R[i - 8, i] = 1.0
    c["rmat"] = R
    c["negtri"] = np.where(np.arange(128)[:, None] <= np.arange(128)[None, :], 0.0, -30000.0).astype(np.float32)
    c["bmask"] = (np.arange(128)[:, None] // 16 == np.arange(128)[None, :] // 16).astype(np.float32)
    c["jidx"] = np.arange(256, dtype=np.float32)[None, :]
    freqs = 500000.0 ** (-np.arange(0, 16, 2, dtype=np.float64) / 16.0)
    fc = np.zeros((128, 2), np.float32)
    for i in range(128):
        d = i % 64
        if d < 8:
            fc[i, 0] = freqs[d] / (2 * np.pi)
            fc[i, 1] = -freqs[d] / (2 * np.pi)
        elif d < 16:
            fc[i, 0] = freqs[d - 8] / (2 * np.pi)
            fc[i, 1] = freqs[d - 8] / (2 * np.pi)
    c["ropef"] = fc
    return c


def build_program(nseq=2, nl=DEPTH, dbg=None, do_attn=True, do_ssm=True):
    nc = bass.Bass("TRN2", target_bir_lowering=False)
    din = {}
    din["x"] = nc.dram_tensor("x", [nseq, S, D], F32, kind="ExternalInput").ap()
    din["c"] = nc.dram_tensor("c", [nseq, D], F32, kind="ExternalInput").ap()
    din["positions"] = nc.dram_tensor("positions", [nseq, S], I32, kind="ExternalInput").ap()
    for name, shp in PARAM_SPECS:
        din[name] = nc.dram_tensor(name, shp, F32, kind="ExternalInput").ap()
    hc = host_consts()
    for name, arr in hc.items():
        din[name] = nc.dram_tensor(name, list(arr.shape), F32, kind="ExternalInput").ap()
    y_out = nc.dram_tensor("y", [nseq, S, D], F32, kind="ExternalOutput").ap()
    dbg_out = {}
    if dbg:
        for name, shp in dbg.items():
            dbg_out[name] = nc.dram_tensor(name, shp, F32, kind="ExternalOutput").ap()

    tab_bt8 = nc.dram_tensor("tab_bt8", [DEPTH, 4, 128, 2 * 8 * 128], BF16).ap()
    tab_ctr = nc.dram_tensor("tab_ctr", [DEPTH, 4, 128, 4 * 2 * 8 * 32], BF16).ap()
    tab_kt = nc.dram_tensor("tab_kt", [DEPTH, 4, 128, 8 * 128], BF16).ap()
    tab_rot = nc.dram_tensor("tab_rot", [DEPTH, 4, 128, 2 * 4 * 256], F32).ap()
    tab_small = nc.dram_tensor("tab_small", [DEPTH, 128, 32], F32).ap()

    p = Prog(nc)
    fin = []

    def dma(q, out, in_, reads=(), writes=(), key=None):
        return p.op(q, lambda h, inc: inc(h.dma_start(out=out, in_=in_)), reads=reads, writes=writes, dma=key)

    def act(out, in_, func, reads, writes, **kw):
        return p.op("act", lambda h: h.activation(out, in_, func, **kw), reads=reads, writes=writes)

    def tt(out, a, b, op, reads, writes, eng="dve"):
        return p.op(eng, lambda h: h.tensor_tensor(out, a, b, op), reads=reads, writes=writes)

    def ts(out, a, s1, s2, op0, op1, reads, writes, eng="dve"):
        if s2 is None:
            return p.op(eng, lambda h: h.tensor_scalar(out, a, s1, None, op0), reads=reads, writes=writes)
        return p.op(eng, lambda h: h.tensor_scalar(out, a, s1, s2, op0, op1), reads=reads, writes=writes)

    def stt(out, a, sc, b, op0, op1, reads, writes, eng="dve"):
        return p.op(eng, lambda h: h.scalar_tensor_tensor(out, a, sc, b, op0, op1), reads=reads, writes=writes)

    def cp(out, a, reads, writes, eng="dve"):
        return p.op(eng, lambda h: h.tensor_copy(out, a), reads=reads, writes=writes)

    def mm_group(mms, reads, writes):
        def fn(h):
            ins = None
            for (o, l, r, st, sp_, kw) in mms:
                ins = h.matmul(o, l, r, start=st, stop=sp_, **kw)
            return ins
        return p.op("pe", fn, reads=reads, writes=writes)

    def tr_group(trs, reads, writes):
        def fn(h):
            ins = None
            for (o, i, idn) in trs:
                ins = h.transpose(o, i, idn)
            return ins
        return p.op("pe", fn, reads=reads, writes=writes)

    x_sb = p.sb("x_sb", [128, NT, D], F32)
    ARENA_F32 = 112 * 256 + 64
    arena = p.sb("arena", [128, ARENA_F32], F32)

    def carve(off_bytes, shape, dt):
        size = 2 if dt == BF16 else 4
        n = int(np.prod(shape[1:]))
        assert off_bytes % 4 == 0 and (n * size) % 4 == 0
        assert off_bytes + n * size <= ARENA_F32 * 4, (off_bytes, shape)
        v = arena[:, off_bytes // 4:(off_bytes + n * size) // 4]
        if dt != F32:
            v = v.bitcast(dt)
        if len(shape) == 2:
            return v
        names = " ".join("a%d" % i for i in range(len(shape) - 1))
        kw = {"a%d" % i: shape[i + 1] for i in range(len(shape) - 1)}
        return v.rearrange("p (%s) -> p %s" % (names, names), **kw)

    KB = 1024
    ident_f = p.sb("ident_f", [128, 128], F32)
    ident_b = p.sb("ident_b", [128, 128], BF16)
    rmat_b = p.sb("rmat_b", [128, 128], BF16)
    negtri_b = p.sb("negtri_b", [128, 128], BF16)
    bmask_f = p.sb("bmask_f", [128, 128], F32)
    jidx_f = p.sb("jidx_f", [128, 256], F32)
    ropef = p.sb("ropef", [128, 2], F32)
    modcolB = [p.sb("modcol%d" % i, [128, 48], F32) for i in range(2)]
    modbB = [p.sb("modb_col%d" % i, [128, 48], F32) for i in range(2)]
    cond_all = p.sb("cond_all", [128, 2, 8], BF16)
    gate_b = p.sb("gate_b", [128, 1, D], F32)
    lngb = p.sb("lngb", [128, 2, D], F32)
    modw_buf = [p.sb("modw%d" % i, [128, 8, 128], BF16) for i in range(2)]
    cond_col = p.sb("cond_col", [128, 8], F32)
    cond_rep = p.sb("cond_rep", [128, 8, 128], BF16)
    gain_b = p.sb("gain_b", [128, 128], F32)
    lamv = p.sb("lamv", [128, 4, 64], F32)
    lams = p.sb("lams", [128, 8], F32)
    st_sb = p.sb("st_sb", [128, 12], F32)
    mv_sb = p.sb("mv_sb", [128, 2], F32)
    rs_sb = p.sb("rs_sb", [128, 2], F32)
    ltmp = p.sb("ltmp", [128, D], F32)
    stmp = [p.sb("stmp%d" % i, [128, 512], F32) for i in range(2)]
    ep_z = p.sb("ep_z", [128, 2, 4], F32)
    ep_ss = p.sb("ep_ss", [128, 4], F32)

    ps_all = p.ps("ps_all", [128, 8, 512], F32)
    psb = [ps_all[:, i, :] for i in range(8)]

    def psb_bf(i):
        return psb[i].bitcast(BF16)

    dma("sp", ident_f[:], din["ident"], writes=["ident_f"], key="c0")
    dma("pool", ident_b[:], din["ident"], writes=["ident_b"], key="c1")
    dma("pool", rmat_b[:], din["rmat"], writes=["rmat_b"], key="c2")
    dma("pool", negtri_b[:], din["negtri"], writes=["negtri_b"], key="c3")
    dma("sp", bmask_f[:], din["bmask"], writes=["bmask_f"], key="c4")
    dma("sp", jidx_f[:], din["jidx"].broadcast_to([128, 256]), writes=["jidx_f"], key="c5")
    dma("sp", ropef[:], din["ropef"], writes=["ropef"], key="c6")

    def frac_round(dst, src, itmp, ftmp, reads, writes, eng="dve"):
        cp(itmp, src, reads=reads, writes=["_frac_i"], eng=eng)
        cp(ftmp, itmp, reads=["_frac_i"], writes=["_frac_f"], eng=eng)
        tt(dst, src, ftmp, ALU.subtract, reads=list(reads) + ["_frac_f"], writes=writes, eng=eng)


    S5_MAIN_BUFS = ["tabsB", "tabsC", "rotb", "small", "r8tab", "t1", "t2", "b1", "b2", "Sprev", "Sprev0", "yact", "gluw", "sg"]
    S5_BUILD_BUFS = ["spf", "Ppow", "aq", "Bs", "BX", "Cld", "CX", "BXb", "BPall", "CPall", "BT8all", "CTrall", "KTall",
                     "bt1", "bt2", "bitmp", "bftmp", "bcs", "smallb"]

    def ssm_build(l):
        p.handoff(S5_BUILD_BUFS, S5_BUILD_BUFS)
        spf = carve(0, [128, 40, 16], F32)
        Pre = carve(2560, [128, 9, 16], F32)
        Pim = carve(3328, [128, 9, 16], F32)
        aq = carve(4 * KB, [128, 3, 128], F32)
        Bs = carve(5632, [128, 2, 16, 16], F32)
        BX = carve(7680, [128, 2, 16, 32], F32)
        Cld = carve(11776, [128, 2, 4, 128], F32)
        CX = carve(15872, [128, 2, 16, 32], F32)
        BXb = carve(19968, [128, 2, 16, 32], BF16)
        smallb = carve(22016, [128, 32], F32)
        BPall = carve(22 * KB, [128, 8, 2, 512], BF16)
        CPall = carve(38 * KB, [128, 9, 2, 512], BF16)
        BT8all = carve(56 * KB, [128, 4, 2, 8, 128], BF16)
        CTrall = carve(72 * KB, [128, 16, 2, 8, 32], BF16)
        KTall = carve(88 * KB, [128, 4, 8, 128], BF16)
        bt1 = carve(96 * KB, [128, 1024], F32)
        bt2 = carve(100 * KB, [128, 1024], F32)
        bitmp = carve(104 * KB, [128, 1024], F32).bitcast(I32)
        bftmp = carve(108 * KB, [128, 1024], F32)
        bcs = carve(22 * KB, [128, 2, 4, 256], F32)

        def sl(i):
            return spf[:, i, :]
        L_RE, L_IM, DL, LR, TRN, MAG, COSA, SINA, LBR, LBI, DEN, RDEN, NRE, KR, KI, TA, TB, R8, FF, FHI, FLO, TI, TF, TC, GLUB, DCOL = range(26)
        sk = lambda i: ("spf", i)

        dma("sp", aq[0:16, 0, :], din["ssm_a_re"][l].rearrange("(q g) p -> q (g p)", g=2), writes=[("aq", 0)], key="aq0")
        dma("sp", aq[0:16, 1, :], din["ssm_a_im"][l].rearrange("(q g) p -> q (g p)", g=2), writes=[("aq", 1)], key="aq1")
        dma("sp", ltmp[0:16, 0:2], din["ssm_log_step"][l].rearrange("(q g) -> q g", g=2), writes=["ltmp"], key="aq2")
        act(ltmp[0:16, 0:2], ltmp[0:16, 0:2], AF.Exp, reads=["ltmp"], writes=["ltmp"])
        cp(aq[0:16, 2, :].rearrange("q (g p) -> q g p", g=2), ltmp[0:16, 0:2].unsqueeze(2).broadcast_to([16, 2, 64]),
           reads=["ltmp"], writes=[("aq", 2)])
        tr_group([(psb[7][:, 16 * i:16 * i + 16], aq[0:16, i, :], ident_f[0:16, 0:16]) for i in range(3)],
                 reads=[("aq", 0), ("aq", 1), ("aq", 2), "ident_f"], writes=[("ps", 7)])
        ts(sl(L_RE), psb[7][:, 0:16], -1e-4, None, ALU.min, None, reads=[("ps", 7)], writes=[sk(L_RE)])
        cp(sl(L_IM), psb[7][:, 16:32], reads=[("ps", 7)], writes=[sk(L_IM)])
        cp(sl(DL), psb[7][:, 32:48], reads=[("ps", 7)], writes=[sk(DL)])
        dma("sp", ltmp[0:4, 0:128], din["ssm_d"][l].rearrange("g c -> (g c)").rearrange("(o p) -> o p", p=128), writes=["ltmp"], key="aq3")
        dma("sp", ltmp[4:8, 0:128], din["glu_b"][l].rearrange("(o p) -> o p", p=128), writes=["ltmp"], key="aq3")
        tr_group([(psb[7][:, 64:72], ltmp[0:8, 0:128], ident_f[0:8, 0:8])], reads=["ltmp", "ident_f"], writes=[("ps", 7)])
        cp(spf[:, DCOL, 0:4], psb[7][:, 64:68], reads=[("ps", 7)], writes=[sk(DCOL)])
        cp(smallb[:, 16:20], psb[7][:, 68:72], reads=[("ps", 7)], writes=["smallb"])
        for c, nm in enumerate(("ssm_b_re", "ssm_b_im")):
            dma("sp", Bs[:, c, :, :], din[nm][l].rearrange("(q g) p c -> (g p) q c", g=2), writes=[("Bs", c)], key="bs%d" % c)
        for c, nm in enumerate(("ssm_c_re", "ssm_c_im")):
            csrc = din[nm][l].rearrange("(o q g) c s -> q c o g s", o=4, q=4, g=2)
            for q4_ in range(4):
                for o_ in range(4):
                    dma("sp", Cld[16 * q4_:16 * q4_ + 16, c, o_, :].rearrange("p (g s) -> p g s", g=2), csrc[q4_][:, o_, :, :],
                        writes=[("Cld", c)], key="cld%d_%d" % (q4_, o_))

        tt(sl(LR), sl(L_RE), sl(DL), ALU.mult, reads=[sk(L_RE), sk(DL)], writes=[sk(LR)])
        stt(sl(TRN), sl(L_IM), 1.0 / TWO_PI, sl(DL), ALU.mult, ALU.mult, reads=[sk(L_IM), sk(DL)], writes=[sk(TRN)])
        act(sl(MAG), sl(LR), AF.Exp, reads=[sk(LR)], writes=[sk(MAG)])
        act(smallb[:, 0:16], sl(LR), AF.Exp, reads=[sk(LR)], writes=["smallb"], scale=8.0)
        tiv = spf[:, TI, :].bitcast(I32)

        def frac16(dst, src):
            cp(tiv, sl(src), reads=[sk(src)], writes=[sk(TI)])
            cp(sl(TF), tiv, reads=[sk(TI)], writes=[sk(TF)])
            tt(sl(dst), sl(src), sl(TF), ALU.subtract, reads=[sk(src), sk(TF)], writes=[sk(dst)])
        frac16(TA, TRN)
        act(sl(SINA), sl(TA), AF.Sin, reads=[sk(TA)], writes=[sk(SINA)], scale=TWO_PI)
        ts(sl(TC), sl(TRN), 0.25, None, ALU.add, None, reads=[sk(TRN)], writes=[sk(TC)])
        frac16(TA, TC)
        act(sl(COSA), sl(TA), AF.Sin, reads=[sk(TA)], writes=[sk(COSA)], scale=TWO_PI)
        tt(sl(LBR), sl(MAG), sl(COSA), ALU.mult, reads=[sk(MAG), sk(COSA)], writes=[sk(LBR)])
        tt(sl(LBI), sl(MAG), sl(SINA), ALU.mult, reads=[sk(MAG), sk(SINA)], writes=[sk(LBI)])
        tt(sl(TA), sl(L_RE), sl(L_RE), ALU.mult, reads=[sk(L_RE)], writes=[sk(TA)])
        tt(sl(TB), sl(L_IM), sl(L_IM), ALU.mult, reads=[sk(L_IM)], writes=[sk(TB)])
        tt(sl(DEN), sl(TA), sl(TB), ALU.add, reads=[sk(TA), sk(TB)], writes=[sk(DEN)])
        p.op("dve", lambda h: h.reciprocal(spf[:, RDEN, :], spf[:, DEN, :]), reads=[sk(DEN)], writes=[sk(RDEN)])
        ts(sl(NRE), sl(LBR), -1.0, None, ALU.add, None, reads=[sk(LBR)], writes=[sk(NRE)])
        tt(sl(TA), sl(NRE), sl(L_RE), ALU.mult, reads=[sk(NRE), sk(L_RE)], writes=[sk(TA)])
        tt(sl(TB), sl(LBI), sl(L_IM), ALU.mult, reads=[sk(LBI), sk(L_IM)], writes=[sk(TB)])
        tt(sl(TA), sl(TA), sl(TB), ALU.add, reads=[sk(TA), sk(TB)], writes=[sk(TA)])
        tt(sl(KR), sl(TA), sl(RDEN), ALU.mult, reads=[sk(TA), sk(RDEN)], writes=[sk(KR)])
        tt(sl(TA), sl(LBI), sl(L_RE), ALU.mult, reads=[sk(LBI), sk(L_RE)], writes=[sk(TA)])
        tt(sl(TB), sl(NRE), sl(L_IM), ALU.mult, reads=[sk(NRE), sk(L_IM)], writes=[sk(TB)])
        tt(sl(TA), sl(TA), sl(TB), ALU.subtract, reads=[sk(TA), sk(TB)], writes=[sk(TA)])
        tt(sl(KI), sl(TA), sl(RDEN), ALU.mult, reads=[sk(TA), sk(RDEN)], writes=[sk(KI)])
        p.op("dve", lambda h: h.memset(Pre[:, 0, :], 1.0), writes=[("Ppow", 0)])
        p.op("dve", lambda h: h.memset(Pim[:, 0, :], 0.0), writes=[("Ppow", 0)])
        cp(Pre[:, 1, :], sl(LBR), reads=[sk(LBR)], writes=[("Ppow", 1)])
        cp(Pim[:, 1, :], sl(LBI), reads=[sk(LBI)], writes=[("Ppow", 1)])
        for n in range(1, 8):
            rk = [("Ppow", n), sk(LBR), sk(LBI)]
            tt(sl(TA), Pre[:, n, :], sl(LBR), ALU.mult, reads=rk, writes=[sk(TA)])
            tt(sl(TB), Pim[:, n, :], sl(LBI), ALU.mult, reads=rk, writes=[sk(TB)])
            tt(Pre[:, n + 1, :], sl(TA), sl(TB), ALU.subtract, reads=[sk(TA), sk(TB)], writes=[("Ppow", n + 1)])
            tt(sl(TA), Pre[:, n, :], sl(LBI), ALU.mult, reads=rk, writes=[sk(TA)])
            tt(sl(TB), Pim[:, n, :], sl(LBR), ALU.mult, reads=rk, writes=[sk(TB)])
            tt(Pim[:, n + 1, :], sl(TA), sl(TB), ALU.add, reads=[sk(TA), sk(TB), ("Ppow", n + 1)], writes=[("Ppow", n + 1)])
        ts(sl(FF), sl(TRN), 8.0, None, ALU.mult, None, reads=[sk(TRN)], writes=[sk(FF)])
        ts(sl(TA), sl(FF), 1024.0, None, ALU.mult, None, reads=[sk(FF)], writes=[sk(TA)])
        cp(tiv, sl(TA), reads=[sk(TA)], writes=[sk(TI)])
        cp(sl(TF), tiv, reads=[sk(TI)], writes=[sk(TF)])
        ts(sl(FHI), sl(TF), 1.0 / 1024.0, None, ALU.mult, None, reads=[sk(TF)], writes=[sk(FHI)])
        tt(sl(FLO), sl(FF), sl(FHI), ALU.subtract, reads=[sk(FF), sk(FHI)], writes=[sk(FLO)])
        dma("sp", tab_small[l], smallb[:], reads=["smallb"], writes=[("tabd", l, "small")], key="tsmall")

        p.op("dve", lambda h: h.memset(BX[:], 0.0), writes=[("BX", 0), ("BX", 1)])
        p.op("dve", lambda h: h.memset(CX[:], 0.0), writes=[("CX", 0), ("CX", 1)])
        for hp in range(2):
            rs = slice(64 * hp, 64 * hp + 64)
            cs = slice(16 * hp, 16 * hp + 16)
            krb = spf[rs, KR, :].unsqueeze(2).broadcast_to([64, 16, 16])
            kib = spf[rs, KI, :].unsqueeze(2).broadcast_to([64, 16, 16])
            u1 = bt1[rs, 0:256].rearrange("p (q c) -> p q c", q=16)
            u2 = bt2[rs, 0:256].rearrange("p (q c) -> p q c", q=16)
            rk = [("Bs", 0), ("Bs", 1), sk(KR), sk(KI)]
            tt(u1, Bs[rs, 0, :, :], krb, ALU.mult, reads=rk, writes=["bt1"])
            tt(u2, Bs[rs, 1, :, :], kib, ALU.mult, reads=rk, writes=["bt2"])
            tt(BX[rs, 0, :, cs], u1, u2, ALU.subtract, reads=["bt1", "bt2", ("BX", 0)], writes=[("BX", 0)])
            tt(u1, Bs[rs, 1, :, :], krb, ALU.mult, reads=rk, writes=["bt1"])
            tt(u2, Bs[rs, 0, :, :], kib, ALU.mult, reads=rk, writes=["bt2"])
            tt(BX[rs, 1, :, cs], u1, u2, ALU.add, reads=["bt1", "bt2", ("BX", 1)], writes=[("BX", 1)])
        cp(BXb[:, 0, :, :], BX[:, 0, :, :], reads=[("BX", 0)], writes=["BXb"])
        ts(BXb[:, 1, :, :], BX[:, 1, :, :], -1.0, None, ALU.mult, None, reads=[("BX", 1), "BXb"], writes=["BXb"])
        for c in range(2):
            tr_group([(psb[6][:, 64 * o:64 * o + 64], Cld[0:64, c, o, :], ident_f[0:64, 0:64]) for o in range(4)],
                     reads=[("Cld", c), "ident_f"], writes=[("ps", 6)])
            for hp in range(2):
                rs = slice(64 * hp, 64 * hp + 64)
                cs = slice(16 * hp, 16 * hp + 16)
                cp(CX[rs, c, :, cs], psb[6][rs, 0:256].rearrange("p (q c) -> p q c", q=16), reads=[("ps", 6), ("CX", c)], writes=[("CX", c)])

        def pb(tab, n):
            return tab[:, n, :].unsqueeze(2).broadcast_to([128, 16, 32])

        def v16(ap2):
            return ap2.rearrange("p (q c) -> p q c", q=16)
        u1 = bt1[:, 0:512].rearrange("p (q c) -> p q c", q=16)
        u2 = bt2[:, 0:512].rearrange("p (q c) -> p q c", q=16)
        bxk = [("BX", 0), ("BX", 1)]
        cxk = [("CX", 0), ("CX", 1)]
        for n in range(8):
            rk = bxk + [("Ppow", n)]
            tt(u1, BX[:, 0, :, :], pb(Pre, n), ALU.mult, reads=rk, writes=["bt1"])
            tt(u2, BX[:, 1, :, :], pb(Pim, n), ALU.mult, reads=rk, writes=["bt2"])
            tt(v16(BPall[:, n, 0, :]), u1, u2, ALU.subtract, reads=["bt1", "bt2"], writes=[("BPall", n)])
            tt(u1, BX[:, 1, :, :], pb(Pre, n), ALU.mult, reads=rk, writes=["bt1"])
            tt(u2, BX[:, 0, :, :], pb(Pim, n), ALU.mult, reads=rk, writes=["bt2"])
            tt(v16(BPall[:, n, 1, :]), u1, u2, ALU.add, reads=["bt1", "bt2", ("BPall", n)], writes=[("BPall", n)])
        for o in range(4):
            for c in range(2):
                bank = 6 + (o * 2 + c) % 2
                pT = psb_bf(bank)
                tr_group([(pT[:, 128 * s8:128 * s8 + 128], BPall[:, 7 - s8, c, 128 * o:128 * o + 128], ident_b[:]) for s8 in range(8)],
                         reads=[("BPall", n) for n in range(8)] + ["ident_b"], writes=[("ps", bank)])
                cp(BT8all[:, o, c, :, :], pT[:, :].rearrange("p (s m) -> p s m", s=8), reads=[("ps", bank)], writes=[("BT8all", o)])
            dma("sp", tab_bt8[l, o], BT8all[:, o, :, :, :].rearrange("p c s m -> p (c s m)"), reads=[("BT8all", o)],
                writes=[("tabd", l, "bt8", o)], key="tbt8")
        w1 = bftmp[:, 0:512].rearrange("p (q c) -> p q c", q=16)
        w2 = bftmp[:, 512:1024].rearrange("p (q c) -> p q c", q=16)
        for tau in range(9):
            rk = cxk + [("Ppow", tau)]
            tt(w1, CX[:, 0, :, :], pb(Pre, tau), ALU.mult, reads=rk, writes=["bftmp"], eng="pool")
            tt(w2, CX[:, 1, :, :], pb(Pim, tau), ALU.mult, reads=rk, writes=["bftmp"], eng="pool")
            tt(v16(CPall[:, tau, 0, :]), w1, w2, ALU.subtract, reads=["bftmp"], writes=[("CPall", tau)], eng="pool")
            tt(w1, CX[:, 1, :, :], pb(Pre, tau), ALU.mult, reads=rk, writes=["bftmp"], eng="pool")
            tt(w2, CX[:, 0, :, :], pb(Pim, tau), ALU.mult, reads=rk, writes=["bftmp"], eng="pool")
            tt(v16(CPall[:, tau, 1, :]), w1, w2, ALU.add, reads=["bftmp", ("CPall", tau)], writes=[("CPall", tau)], eng="pool")
        for c in range(2):
            cp(CTrall[:, :, c, :, :].rearrange("p q t c -> p t q c"),
               CPall[:, 1:9, c, :].rearrange("p t (q c) -> p t q c", q=16),
               reads=[("CPall", t) for t in range(1, 9)], writes=[("CTrall", c)])
        for o in range(4):
            dma("sp", tab_ctr[l, o], CTrall[:, 4 * o:4 * o + 4, :, :, :].rearrange("p q c t e -> p (q c t e)"),
                reads=[("CTrall", 0), ("CTrall", 1)], writes=[("tabd", l, "ctr", o)], key="tctr")
        for o in range(4):
            for hb in range(2):
                mms = []
                for tl in range(4):
                    tau = 4 * hb + tl
                    mms.append((psb[4 + hb][:, 128 * tl:128 * tl + 128], BXb[:, 0, 4 * o:4 * o + 4, :].rearrange("p q c -> p (q c)"),
                                CPall[:, tau, 0, 128 * o:128 * o + 128], tl == 0, False, {"skip_group_check": True}))
                    mms.append((psb[4 + hb][:, 128 * tl:128 * tl + 128], BXb[:, 1, 4 * o:4 * o + 4, :].rearrange("p q c -> p (q c)"),
                                CPall[:, tau, 1, 128 * o:128 * o + 128], False, True, {"skip_group_check": True}))
                mm_group(mms, reads=["BXb"] + [("CPall", 4 * hb + tl) for tl in range(4)], writes=[("ps", 4 + hb)])
            tt(bt1[:, 0:128], psb[4][:, 0:128], bmask_f[:], ALU.mult, reads=[("ps", 4), "bmask_f"], writes=["bt1"])
            stt(KTall[:, o, 0, :], ident_f[:], spf[:, DCOL, o:o + 1], bt1[:, 0:128], ALU.mult, ALU.add,
                reads=["ident_f", sk(DCOL), "bt1"], writes=[("KTall", o)])
            tt(KTall[:, o, 1:4, :], psb[4][:, 128:512].rearrange("p (t c) -> p t c", t=3),
               bmask_f[:, :].unsqueeze(1).broadcast_to([128, 3, 128]), ALU.mult, reads=[("ps", 4), "bmask_f", ("KTall", o)], writes=[("KTall", o)])
            tt(KTall[:, o, 4:8, :], psb[5][:, :].rearrange("p (t c) -> p t c", t=4),
               bmask_f[:, :].unsqueeze(1).broadcast_to([128, 4, 128]), ALU.mult, reads=[("ps", 5), "bmask_f", ("KTall", o)], writes=[("KTall", o)])
            dma("sp", tab_kt[l, o], KTall[:, o, :, :].rearrange("p t c -> p (t c)"), reads=[("KTall", o)],
                writes=[("tabd", l, "kt", o)], key="tkt")

        p.handoff(["BPall"], ["bcs"])
        jb4 = jidx_f[:, :].unsqueeze(1).broadcast_to([128, 4, 256])
        v3 = lambda t: t[:, :].rearrange("p (q j) -> p q j", q=4)
        for o in range(4):
            fhi = spf[:, FHI, 4 * o:4 * o + 4].unsqueeze(2).broadcast_to([128, 4, 256])
            flo = spf[:, FLO, 4 * o:4 * o + 4].unsqueeze(2).broadcast_to([128, 4, 256])
            tt(v3(bt1), jb4, fhi, ALU.mult, reads=["jidx_f", sk(FHI)], writes=["bt1"])
            cp(bitmp[:], bt1[:], reads=["bt1"], writes=["bitmp"])
            cp(bftmp[:], bitmp[:], reads=["bitmp"], writes=["bftmp"])
            tt(bt1[:], bt1[:], bftmp[:], ALU.subtract, reads=["bt1", "bftmp"], writes=["bt1"])
            tt(v3(bt2), jb4, flo, ALU.mult, reads=["jidx_f", sk(FLO)], writes=["bt2"])
            tt(bt1[:], bt1[:], bt2[:], ALU.add, reads=["bt1", "bt2"], writes=["bt1"])
            cp(bitmp[:], bt1[:], reads=["bt1"], writes=["bitmp"])
            cp(bftmp[:], bitmp[:], reads=["bitmp"], writes=["bftmp"])
            tt(bt2[:], bt1[:], bftmp[:], ALU.subtract, reads=["bt1", "bftmp"], writes=["bt2"])
            act(bcs[:, 1, :, :], v3(bt2), AF.Sin, reads=["bt2"], writes=[("bcs", 1)], scale=TWO_PI)
            ts(bt2[:], bt1[:], 0.25, None, ALU.add, None, reads=["bt1"], writes=["bt2"])
            cp(bitmp[:], bt2[:], reads=["bt2"], writes=["bitmp"])
            cp(bftmp[:], bitmp[:], reads=["bitmp"], writes=["bftmp"])
            tt(bt2[:], bt2[:], bftmp[:], ALU.subtract, reads=["bt2", "bftmp"], writes=["bt2"])
            act(bcs[:, 0, :, :], v3(bt2), AF.Sin, reads=["bt2"], writes=[("bcs", 0)], scale=TWO_PI)
            dma("sp", tab_rot[l, o], bcs[:, :, :, :].rearrange("p c q j -> p (c q j)"), reads=[("bcs", 0), ("bcs", 1)],
                writes=[("tabd", l, "rot", o)], key="trot")

    def ssm_main(l, uT, mixT, pre_glu=None):
        SB = 48 * KB
        p.handoff(["qT", "kT", "vA", "vA1", "qraw", "rq1", "ebuf", "ep_o", "ep_o1", "ep_ob", "accS", "mixS"], S5_MAIN_BUFS + ["mixS"])
        tabsB = carve(SB, [128, 2048], BF16)
        tabsC = [carve(SB + 4096 + i * 6144, [128, 3072], BF16) for i in range(2)]
        SprevB = [carve(SB + 49408, [128, 4, 2, 258], BF16), carve(SB + 16384, [128, 4, 2, 258], BF16)]
        rotb = carve(SB + 20736, [128, 2, 4, 256], F32)
        r8tab = carve(SB + 28928, [128, 4, 256], F32)
        t1 = carve(SB + 33024, [128, 4, 256], F32)
        t2 = carve(SB + 37120, [128, 4, 256], F32)
        b1 = carve(SB + 41216, [128, 4, 256], F32)
        b2 = carve(SB + 45312, [128, 4, 256], F32)
        yact = [carve(SB + 53536, [128, 8, 128], BF16)]
        gluw = carve(SB + 55584, [128, 4, 512], BF16)
        sg = [carve(SB + 59680 + i * 2048, [128, 512], F32) for i in range(2)]
        small = carve(SB + 63776, [128, 32], F32)

        dma("sp", small[:], tab_small[l], reads=[("tabd", l, "small")], writes=["small"], key="lsmall")
        dma("pool", gluw[:], din["glu_w"][l].rearrange("(o p) n -> p o n", p=128), writes=["gluw"], key="gluw")
        for i_ in range(2):
            p.op("dve", lambda h, i_=i_: h.memset(SprevB[i_][:, :, :, 0:1], 0.0), writes=[("Sprev0", i_)])

        def s5_front(o):
            tb = tabsC[o % 2]
            tk = ("tabsC", o % 2)
            Sprev = SprevB[o % 2]
            spk = ("Sprev", o % 2)
            BT8 = tabsB[:, :].rearrange("p (c s m) -> p c s m", c=2, s=8)
            CTr = tb[:, 0:2048].rearrange("p (q c t e) -> p q c t e", q=4, c=2, t=8)
            KT = tb[:, 2048:3072].rearrange("p (t c) -> p t c", t=8)
            dma("sp", tabsB[:, :], tab_bt8[l, o], reads=[("tabd", l, "bt8", o)], writes=["tabsB"], key="ltabsB")
            p.op("sp", lambda h, inc, tb=tb, o=o, l=l: (
                inc(h.dma_start(out=tb[:, 0:2048], in_=tab_ctr[l, o])),
                inc(h.dma_start(out=tb[:, 2048:3072], in_=tab_kt[l, o]))),
                reads=[("tabd", l, "ctr", o), ("tabd", l, "kt", o)], writes=[tk], dma="ltabs%d" % (o % 2), dma_n=2)
            dma("sp", rotb[:, :, :, :].rearrange("p c q j -> p (c q j)"), tab_rot[l, o], reads=[("tabd", l, "rot", o)], writes=["rotb"], key="lrot")
            cosJ = rotb[:, 0, :, :]
            sinJ = rotb[:, 1, :, :]
            cp(r8tab[:], small[:, 4 * o:4 * o + 4].unsqueeze(2).broadcast_to([128, 4, 256]), reads=["small"], writes=["r8tab"])
            p.op("dve", lambda h: h.memset(r8tab[:, :, 0:1], 0.0), writes=["r8tab"])
            mms = []
            for c in range(2):
                for s8 in range(8):
                    for q4 in range(4):
                        prs = slice(32 * q4, 32 * q4 + 32)
                        mms.append((psb[q4][:, 256 * c:256 * c + 256], BT8[prs, c, s8, :], uT[prs, o, s8:S:8],
                                    c == 0 and s8 == 0, s8 == 7, {"tile_position": (32 * q4, 0), "skip_group_check": True}))
            mm_group(mms, reads=["tabsB"] + [("uT", o, mg) for mg in range(4)], writes=[("ps", q4) for q4 in range(4)])
            br = ps_all[:, 0:4, 0:256]
            bi = ps_all[:, 0:4, 256:512]
            pk = [("ps", i) for i in range(4)]
            tt(t1[:], br, cosJ, ALU.mult, reads=pk + ["rotb"], writes=["t1"])
            tt(t2[:], bi, sinJ, ALU.mult, reads=pk + ["rotb"], writes=["t2"])
            tt(b1[:], t1[:], t2[:], ALU.add, reads=["t1", "t2"], writes=["b1"])
            tt(t1[:], bi, cosJ, ALU.mult, reads=pk + ["rotb"], writes=["t1"])
            tt(t2[:], br, sinJ, ALU.mult, reads=pk + ["rotb"], writes=["t2"])
            tt(b2[:], t1[:], t2[:], ALU.subtract, reads=["t1", "t2"], writes=["b2"])
            f2 = lambda t: t[:, :, :].rearrange("p q j -> p (q j)")
            p.op("dve", lambda h: h.tensor_tensor_scan(f2(t1), f2(r8tab), f2(b1), 0.0, ALU.mult, ALU.add),
                 reads=["b1", "r8tab"], writes=["t1"])
            p.op("dve", lambda h: h.tensor_tensor_scan(f2(t2), f2(r8tab), f2(b2), 0.0, ALU.mult, ALU.add),
                 reads=["b2", "r8tab"], writes=["t2"])
            tt(b1[:], t1[:], cosJ, ALU.mult, reads=["t1", "rotb"], writes=["b1"])
            tt(b2[:], t2[:], sinJ, ALU.mult, reads=["t2", "rotb"], writes=["b2"])
            tt(Sprev[:, :, 0, 1:257], b1[:], b2[:], ALU.subtract, reads=["b1", "b2"], writes=[spk])
            tt(b1[:], t1[:], sinJ, ALU.mult, reads=["t1", "rotb"], writes=["b1"])
            tt(b2[:], t2[:], cosJ, ALU.mult, reads=["t2", "rotb"], writes=["b2"])
            stt(Sprev[:, :, 1, 1:257], b1[:], -1.0, b2[:], ALU.mult, ALU.subtract, reads=["b1", "b2", spk], writes=[spk])

            return tk, KT, CTr, Sprev, spk

        def s5_back(o, tk, KT, CTr, Sprev, spk):
            for jb in range(2):
                bA = 4
                bB = 5
                yi = 0
                mms = []
                sg_ = {"skip_group_check": True}
                for s8 in range(8):
                    lt = uT[:, o, 1024 * jb + s8:1024 * jb + 1024:8]
                    if s8 <= 3:
                        mms.append((psb[bA][:, 128 * s8:512], lt, KT[:, 0:4 - s8, :], s8 == 0, False, sg_))
                    lo = max(0, s8 - 4)
                    mms.append((psb[bB][:, 128 * lo:512], lt, KT[:, max(4 - s8, 0):8 - s8, :], s8 == 0, False, sg_))
                for q4 in range(4):
                    for c in range(2):
                        lt = Sprev[:, q4, c, 128 * jb:128 * jb + 128]
                        last = (q4 == 3 and c == 1)
                        mms.append((psb[bA].rearrange("p (t c) -> p t c", t=4)[:, :, 32 * q4:32 * q4 + 32], lt,
                                    CTr[:, q4, c, 0:4, :], False, last, sg_))
                        mms.append((psb[bB].rearrange("p (t c) -> p t c", t=4)[:, :, 32 * q4:32 * q4 + 32], lt,
                                    CTr[:, q4, c, 4:8, :], False, last, sg_))
                mm_group(mms, reads=[tk, ("Sprev0", o % 2), spk, ("uT", o, 2 * jb), ("uT", o, 2 * jb + 1)],
                         writes=[("ps", bA), ("ps", bB)])
                act(yact[yi][:, 0:4, :], psb[bA].rearrange("p (t c) -> p t c", t=4), AF.Gelu_apprx_tanh,
                    reads=[("ps", bA)], writes=[("yact", yi)])
                act(yact[yi][:, 4:8, :], psb[bB].rearrange("p (t c) -> p t c", t=4), AF.Gelu_apprx_tanh,
                    reads=[("ps", bB)], writes=[("yact", yi)])
                pT = psb_bf(6 + jb)
                tr_group([(pT[:, 128 * t8:128 * t8 + 128], yact[yi][:, t8, :], ident_b[:]) for t8 in range(8)],
                         reads=[("yact", yi), "ident_b"], writes=[("ps", 6 + jb)])
                act(mixT[:, o, 1024 * jb:1024 * jb + 1024].rearrange("p (j t) -> p t j", t=8),
                    pT[:, :].rearrange("p (t j) -> p t j", t=8), AF.Copy, reads=[("ps", 6 + jb)],
                    writes=[("mixS", o, 2 * jb), ("mixS", o, 2 * jb + 1)])


        prev = None
        for o in range(4):
            cur = s5_front(o)
            if prev is not None:
                s5_back(o - 1, *prev)
            prev = cur
        s5_back(3, *prev)

        if pre_glu is not None:
            pre_glu()
        for m_ in range(4):
            tsl = slice(512 * m_, 512 * m_ + 512)
            for ct in range(4):
                mm_group([(psb[ct][:, :], gluw[:, oo, 128 * ct:128 * ct + 128], mixT[:, oo, tsl], oo == 0, oo == 3, {}) for oo in range(4)],
                         reads=["gluw"] + [("mixS", oo, m_) for oo in range(4)], writes=[("ps", ct)])
            for ct in range(4):
                si = ct % 2
                act(sg[si][:], psb[ct][:, :], AF.Sigmoid, reads=[("ps", ct), "small"], writes=[("sg", si)],
                    bias=small[:, 16 + ct:17 + ct])
                tt(mixT[:, ct, tsl], mixT[:, ct, tsl], sg[si][:], ALU.mult, reads=[("mixS", ct, m_), ("sg", si)], writes=[("mixS", ct, m_)])

    tab_tokens = {}


    for sq_ in range(nseq):
        dma("sp", ltmp[0:8, 0:128], din["c"][sq_].rearrange("(k p) -> k p", p=128), writes=["ltmp"], key="cc")
        tr_group([(psb[7][:, 0:8], ltmp[0:8, 0:128], ident_f[0:8, 0:8])], reads=["ltmp", "ident_f"], writes=[("ps", 7)])
        act(cond_col[:], psb[7][:, 0:8], AF.Silu, reads=[("ps", 7)], writes=["cond_col"])
        cp(cond_all[:, sq_, :], cond_col[:], reads=["cond_col"], writes=["cond_all"])

    mstate = {"n": 0}

    def _modw_load(l_, ch):
        i_ = mstate["n"] % 2
        mstate["n"] += 1
        dma("pool", modw_buf[i_][:], din["mod_w"][l_][:, ch * 128:(ch + 1) * 128].rearrange("(k p) n -> p k n", p=128),
            writes=[("modw", i_)], key="modw%d" % i_)
        return modw_buf[i_], ("modw", i_)

    def modb_prep(l_, par):
        mbt = ltmp[0:48, 0:128]
        dma("sp", mbt, din["mod_b"][l_].rearrange("(c p) -> c p", p=128), writes=["ltmp"], key="mb")
        tr_group([(psb[7][:, 0:48], mbt, ident_f[0:48, 0:48])], reads=["ltmp", "ident_f"], writes=[("ps", 7)])
        cp(modbB[par][:], psb[7][:, 0:48], reads=[("ps", 7)], writes=[("modb", par)])

    def col_compute(lb, sq_, ch, base):
        buf, bkey = lb
        mm_group([(psb[7][:, base + ch:base + ch + 1], buf[:, k, :], cond_all[:, sq_, k:k + 1], k == 0, k == 7, {"skip_group_check": True})
                  for k in range(8)], reads=["cond_all", bkey], writes=[("ps", 7)])

    def gate_init(l_, kind):
        dma("sp", gate_b[:, 0, :], din["mod_b"][l_][kind * D:(kind + 1) * D].unsqueeze(0).broadcast_to([128, D]), writes=["gate_b"], key="gb")

    def gate_compute(lb, ch):
        buf, bkey = lb
        c0 = (ch % 8) * 128
        mm_group([(psb[7][:, 104:232], cond_rep[:, k, :], buf[:, k, :], k == 0, k == 7, {"skip_group_check": True}) for k in range(8)],
                 reads=["cond_rep", bkey], writes=[("ps", 7)])
        stt(gate_b[:, 0, c0:c0 + 128], psb[7][:, 104:232], 1.0, gate_b[:, 0, c0:c0 + 128], ALU.add, ALU.add,
            reads=[("ps", 7), "gate_b"], writes=["gate_b"])

    def fin_cols(par, base, lo, hi, p1lo, p1hi):
        tt(modcolB[par][:, lo:hi], psb[7][:, base + lo:base + hi], modbB[par][:, lo:hi], ALU.add,
           reads=[("ps", 7), ("modb", par)], writes=[("modcol", par)])
        if p1hi > p1lo:
            ts(modcolB[par][:, p1lo:p1hi], modcolB[par][:, p1lo:p1hi], 1.0, None, ALU.add, None, reads=[("modcol", par)], writes=[("modcol", par)])

    def _q_prefetch(q):
        cnt = 0
        for t_ in q:
            if t_[0] in ("col", "gate"):
                if cnt >= 2:
                    break
                cnt += 1
                if "lb" not in t_[-1]:
                    t_[-1]["lb"] = _modw_load(t_[1], t_[3] if t_[0] == "col" else t_[2])

    def run_tasks(q, n):
        while n > 0 and q:
            _q_prefetch(q)
            t_ = q.pop(0)
            if t_[0] == "col":
                col_compute(t_[-1]["lb"], t_[2], t_[3], t_[4])
            elif t_[0] == "gate":
                gate_compute(t_[-1]["lb"], t_[2])
            else:
                t_[1]()
            _q_prefetch(q)
            n -= 1

    def run_until(q, marker_len):
        while len(q) > marker_len:
            run_tasks(q, 1)

    def T_col(l_, sq_, ch, base):
        return ("col", l_, sq_, ch, base, {})

    def T_gate(l_, ch):
        return ("gate", l_, ch, {})

    def T_fn(f):
        return ("fn", f)

    steps = [(sq_, l_) for sq_ in range(nseq) for l_ in range(nl)]
    nchunk = 0
    for sq in range(nseq):
        xv = din["x"][sq].rearrange("(tt p) d -> p tt d", p=128)
        for t4 in range(4):
            dma("sp", x_sb[:, 4 * t4:4 * t4 + 4, :], xv[:, 4 * t4:4 * t4 + 4, :],
                writes=[("x", 4 * t4 + i) for i in range(4)], key="xld%d" % t4)
        cp(cond_rep[:], cond_all[:, sq, :].unsqueeze(2).broadcast_to([128, 8, 128]), reads=["cond_all"], writes=["cond_rep"])

        for l in range(nl):
            lam_init = 0.8 - 0.6 * math.exp(-0.3 * l)
            si_ = steps.index((sq, l))
            par = si_ % 2
            modcol = modcolB[par]
            mck = ("modcol", par)
            if si_ == 0:
                modb_prep(l, par)
                q0 = [T_col(l, sq, ch_, 48) for ch_ in range(16)]
                run_until(q0, 0)
                fin_cols(par, 48, 0, 16, 8, 16)
                if do_ssm:
                    for l_ in range(nl):
                        ssm_build(l_)
            nxt = steps[si_ + 1] if si_ + 1 < len(steps) else None
            qA = [T_fn(lambda: gate_init(l, 2))] + [T_gate(l, ch_) for ch_ in range(16, 24)]
            qA += [T_col(l, sq, ch_, 48) for ch_ in range(24, 40)] + [T_fn(lambda: fin_cols(par, 48, 24, 40, 32, 40))]
            qB = [T_fn(lambda: gate_init(l, 5))] + [T_gate(l, ch_) for ch_ in range(40, 48)]
            if nxt is not None:
                nsq, nl_ = nxt
                qA += [T_fn(lambda: modb_prep(nl_, 1 - par))] + [T_col(nl_, nsq, ch_, 88) for ch_ in range(0, 8)]
                qA += [T_fn(lambda: fin_cols(1 - par, 88, 0, 8, 0, 0))]
                qB += [T_col(nl_, nsq, ch_, 88) for ch_ in range(8, 16)] + [T_fn(lambda: fin_cols(1 - par, 88, 8, 16, 8, 16))]
            nA_gate = len(qA) - 9
            nB_gate = len(qB) - 9

            def build_hT(hT, hkey, tok0, ntile, sc_col, sh_col):
                for t4 in range(ntile // 4):
                    for k in range(8):
                        bank = (t4 * 8 + k) % 2
                        tr_group([(psb[bank][:, 128 * i:128 * i + 128], x_sb[:, tok0 + 4 * t4 + i, 128 * k:128 * k + 128], ident_f[:])
                                  for i in range(4)],
                                 reads=[("x", tok0 + 4 * t4 + i) for i in range(4)] + ["ident_f"], writes=[("ps", bank)])
                        act(hT[:, k, 512 * t4:512 * t4 + 512], psb[bank][:, :], AF.Identity,
                            reads=[("ps", bank), mck], writes=[(hkey, k, t4)],
                            scale=modcol[:, sc_col + k:sc_col + k + 1], bias=modcol[:, sh_col + k:sh_col + k + 1])

            uT = carve(0, [128, 4, S], BF16)
            mixT = carve(16 * KB, [128, 8, S], BF16)
            qT = carve(48 * KB, [128, 4, S], BF16)
            kT = carve(64 * KB, [128, 4, S], BF16)
            vA = carve(80 * KB, [128, NT, 4, 130], BF16)
            hT = carve(16 * KB, [128, 8, 1024], BF16)
            wch = [carve(32 * KB + i * 2 * KB, [128, 8, 128], BF16) for i in range(2)] + \
                  [carve(102 * KB + 256 + i * 2 * KB, [128, 8, 128], BF16) for i in range(4)]
            rotc = carve(36 * KB, [128, 1024], F32)
            rots = carve(40 * KB, [128, 1024], F32)
            rtmp_i = carve(44 * KB, [128, 1024], F32).bitcast(I32)
            qraw = [carve(96 * KB + 256 + i * KB, [128, 512], BF16) for i in range(2)]
            rq1 = [carve(98 * KB + 256 + i * 2 * KB, [128, 512], F32) for i in range(2)]
            p.handoff(["actT", "hT2", "wd", "gu"] + S5_BUILD_BUFS, ["uT", "mixT", "qT", "kT", "vA", "vA1", "hT", "wch", "rotc", "rots", "rtmp",
                                                     "qraw", "rq1"])

            p.op("pool", lambda h: h.memset(vA[:, :, :, 128:130], 1.0), writes=[("vA1",)])

            NWB = len(wch)
            NPRE = 4
            wlist = [(hf, ct_) for hf in range(2) for ct_ in range(16)
                     if not ((ct_ < 4 and not do_ssm) or (ct_ >= 4 and not do_attn))]
            wissued = [0]

            def issue_w(upto):
                while wissued[0] < min(upto, len(wlist)):
                    i_ = wissued[0]
                    ct_ = wlist[i_][1]
                    dma("pool", wch[i_ % NWB][:], din["w_in"][l][:, ct_ * 128:(ct_ + 1) * 128].rearrange("(k p) n -> p k n", p=128),
                        writes=[("wch", i_ % NWB)], key="wch%d" % (i_ % NWB))
                    wissued[0] += 1
            issue_w(NPRE)
            wchn = 0
            for half in range(2):
                tok0 = 8 * half
                build_hT(hT, "hT", tok0, 8, 8, 0)
                if do_attn:
                    posb = din["positions"][sq][1024 * half:1024 * half + 1024].unsqueeze(0).broadcast_to([128, 1024])
                    dma("pool", rotc[:], posb, writes=["rotc"], key="rotc")
                    ts(rots[:], rotc[:], ropef[:, 1:2], None, ALU.mult, None, reads=["rotc", "ropef"], writes=["rots"])
                    ts(rotc[:], rotc[:], ropef[:, 0:1], 0.25, ALU.mult, ALU.add, reads=["rotc", "ropef"], writes=["rotc"])
                    for nm, tb in (("rotc", rotc), ("rots", rots)):
                        cp(rtmp_i[:], tb[:], reads=[nm], writes=["rtmp"])
                        cp(ltmp[:], rtmp_i[:], reads=["rtmp"], writes=["ltmp"])
                        tt(tb[:], tb[:], ltmp[:], ALU.subtract, reads=[nm, "ltmp"], writes=[nm])
                        act(tb[:], tb[:], AF.Sin, reads=[nm], writes=[nm], scale=TWO_PI)
                for ct in range(16):
                    if (ct < 4 and not do_ssm) or (ct >= 4 and not do_attn):
                        continue
                    assert wlist[wchn] == (half, ct)
                    wb = wch[wchn % NWB]
                    wkey = ("wch", wchn % NWB)
                    issue_w(wchn + 1 + NPRE)
                    wchn += 1
                    if ct < 12:
                        for mm in range(2):
                            bank = 2 + (ct * 2 + mm) % 2
                            mm_group([(psb[bank][:, :], wb[:, k, :], hT[:, k, 512 * mm:512 * mm + 512], k == 0, k == 7, {})
                                      for k in range(8)],
                                     reads=[wkey] + [("hT", k, mm) for k in range(8)], writes=[("ps", bank)])
                            tsl = slice(1024 * half + 512 * mm, 1024 * half + 512 * mm + 512)
                            mg = 2 * half + mm
                            if ct < 4:
                                act(uT[:, ct, tsl], psb[bank][:, :], AF.Copy, reads=[("ps", bank)], writes=[("uT", ct, mg)])
                            else:
                                isq = ct < 8
                                dst = qT if isq else kT
                                dkey = ("qT" if isq else "kT", ct % 4, mg)
                                qi = (ct * 2 + mm) % 2
                                act(qraw[qi][:], psb[bank][:, :], AF.Copy, reads=[("ps", bank)], writes=[("qraw", qi)],
                                    scale=(0.125 if isq else 1.0))
                                rb = 4 + qi
                                mm_group([(psb[rb][:, :], rmat_b[:], qraw[qi][:], True, True, {})],
                                         reads=["rmat_b", ("qraw", qi)], writes=[("ps", rb)])
                                rsl = slice(512 * mm, 512 * mm + 512)
                                tt(rq1[qi][:], qraw[qi][:], rotc[:, rsl], ALU.mult, reads=[("qraw", qi), "rotc"], writes=[("rq1", qi)], eng="pool")
                                tt(dst[:, ct % 4, tsl], psb[rb][:, :], rots[:, rsl], ALU.mult, reads=[("ps", rb), "rots"], writes=[dkey])
                                tt(dst[:, ct % 4, tsl], dst[:, ct % 4, tsl], rq1[qi][:], ALU.add, reads=[dkey, ("rq1", qi)], writes=[dkey])
                    else:
                        hd = ct - 12
                        for tl in range(8):
                            bank = 2 + tl % 2
                            mm_group([(psb[bank][:, 0:128], hT[:, k, 128 * tl:128 * tl + 128], wb[:, k, :], k == 0, k == 7, {})
                                      for k in range(8)],
                                     reads=[wkey] + [("hT", k, tl // 4) for k in range(8)], writes=[("ps", bank)])
                            act(vA[:, tok0 + tl, hd, 0:128], psb[bank][:, 0:128], AF.Copy, reads=[("ps", bank)],
                                writes=[("vA", tok0 + tl, hd)])

            if dbg and "dbg_q" in dbg_out and sq == 0 and l == 0 and do_attn:
                for nm, tb, kn in (("dbg_q", qT, "qT"), ("dbg_k", kT, "kT")):
                    cp(ltmp[:, 0:512], tb[:, 0, 0:512], reads=[(kn, 0, 0)], writes=["ltmp"])
                    fin.append(dma("sp", dbg_out[nm], ltmp[:, 0:512], reads=["ltmp"], key="dbgq"))
                cp(ltmp[:, 0:130], vA[:, 0, 0, :], reads=[("vA", 0, 0), ("vA1",)], writes=["ltmp"])
                fin.append(dma("sp", dbg_out["dbg_v"], ltmp[:, 0:130], reads=["ltmp"], key="dbgq"))
            p.handoff(["hT", "wch", "rotc", "rots", "rtmp", "mixT"], ["ebuf", "ep_o", "ep_o1", "ep_ob", "accS", "mixT", "mixS"])
            if do_attn:
                ebuf = [carve(16 * KB + i * KB, [128, 512], BF16) for i in range(4)]
                ep_o = carve(20 * KB, [128, 4, 128], F32)
                ep_o1 = carve(22 * KB, [128, 4, 128], F32)
                ep_ob = carve(24 * KB, [128, 4, 128], BF16)
                accS = carve(25 * KB, [128, 2, 4, 130], F32)
                for i, nm in enumerate(("lam_q1", "lam_k1", "lam_q2", "lam_k2")):
                    dma("sp", lamv[:, i, :], din[nm][l].unsqueeze(0).broadcast_to([128, 64]), writes=[("lamv", i)], key="lamv%d" % i)
                dma("sp", gain_b[:], din["subln_w"][l].unsqueeze(0).broadcast_to([128, 128]), writes=["gain_b"], key="gain")
                ts(gain_b[:], gain_b[:], float(1.0 - lam_init), None, ALU.mult, None, reads=["gain_b"], writes=["gain_b"])
                tt(lamv[:, 0, :], lamv[:, 0, :], lamv[:, 1, :], ALU.mult, reads=[("lamv", 0), ("lamv", 1)], writes=[("lamv", 0)])
                tt(lamv[:, 2, :], lamv[:, 2, :], lamv[:, 3, :], ALU.mult, reads=[("lamv", 2), ("lamv", 3)], writes=[("lamv", 2)])
                p.op("dve", lambda h: h.reduce_sum(lams[:, 0:1], lamv[:, 0, :], AX.X), reads=[("lamv", 0)], writes=[("lams", 0)])
                p.op("dve", lambda h: h.reduce_sum(lams[:, 1:2], lamv[:, 2, :], AX.X), reads=[("lamv", 2)], writes=[("lams", 1)])
                act(lams[:, 2:4], lams[:, 0:2], AF.Exp, reads=[("lams", 0), ("lams", 1)], writes=[("lams", 2)])
                tt(lams[:, 4:5], lams[:, 2:3], lams[:, 3:4], ALU.subtract, reads=[("lams", 2)], writes=[("lams", 4)])
                ts(lams[:, 5:6], lams[:, 4:5], -1.0, float(-lam_init), ALU.mult, ALU.add, reads=[("lams", 4)], writes=[("lams", 5)])

                qz = carve(96 * KB + 256, [128, 4, S], BF16)
                p.handoff(["qraw", "rq1", "wch"], ["qz"])
                p.op("pool", lambda h: h.memset(qz[0:64, :, :], 0.0), writes=[("qz", hd_) for hd_ in range(4)])
                for hd_ in range(4):
                    qk_ = [("qT", hd_, Q_) for Q_ in range(4)]
                    act(qz[64:128, hd_, :], qT[64:128, hd_, :], AF.Copy, reads=qk_ + [("qz", hd_)], writes=[("qz", hd_)])
                    p.op("pool", lambda h, hd_=hd_: h.memset(qT[64:128, hd_, :], 0.0), reads=qk_, writes=qk_)
                ecnt = 0
                pending_tail = []
                for hd in range(4):
                    for Q in range(4):
                        nkt = 4 * (Q + 1)
                        for m in range(2):
                            acc = [psb[2 + 2 * m], psb[3 + 2 * m]]
                            acck = [("ps", 2 + 2 * m), ("ps", 3 + 2 * m)]
                            prs = slice(64 * m, 64 * m + 64)

                            def accv(qs):
                                return acc[qs // 2][:, 256 * (qs % 2):256 * (qs % 2) + 130]

                            def issue_s(kt):
                                nonlocal ecnt
                                o = kt - 4 * Q
                                c0 = 128 * o if o > 0 else 0
                                sb_i = (0, 1, 6)[kt % 3]
                                mms_ = []
                                if o >= 0:
                                    mms_.append((psb[sb_i][:, c0:c0 + 128], ident_b[:], negtri_b[:], True, False, {"skip_group_check": True}))
                                qsrc = qT if m == 0 else qz
                                mms_.append((psb[sb_i][:, c0:512], kT[:, hd, 128 * kt:128 * kt + 128],
                                             qsrc[:, hd, 512 * Q + c0:512 * Q + 512], o < 0, True, {"skip_group_check": True}))
                                mm_group(mms_, reads=[("kT", hd, kt // 4), ("qT", hd, Q), ("qz", hd), "ident_b", "negtri_b"], writes=[("ps", sb_i)])
                                ei = ecnt % 4
                                ecnt += 1
                                act(ebuf[ei][:, c0:512], psb[sb_i][:, c0:512], AF.Exp, reads=[("ps", sb_i)], writes=[("ebuf", ei)])
                                return ei, o

                            def issue_pv(kt, ei, o):
                                qs0 = max(o, 0)
                                mms = []
                                for qs in range(qs0, 4):
                                    last_kt = 4 * Q + qs
                                    mms.append((accv(qs), ebuf[ei][:, 128 * qs:128 * qs + 128], vA[:, kt, hd, :],
                                                kt == 0 and qs % 2 == 0, kt == last_kt, {"skip_group_check": True}))
                                mm_group(mms, reads=[("ebuf", ei), ("vA", kt, hd), ("vA1",)], writes=acck)

                            run_tasks(qA, 1)
                            pendq = [issue_s(kt_) for kt_ in range(min(2, nkt))]
                            for kt in range(nkt):
                                if kt + 2 < nkt:
                                    pendq.append(issue_s(kt + 2))
                                issue_pv(kt, *pendq.pop(0))
                        while len(pending_tail) > 0:
                            pending_tail.pop(0)()
                        for bi_ in range(4):
                            cp(accS[:, bi_ // 2, 2 * (bi_ % 2):2 * (bi_ % 2) + 2, :],
                               psb[2 + bi_][:, :].rearrange("p (a b) -> p a b", a=2)[:, :, 0:130],
                               reads=[("ps", 2 + bi_)], writes=[("accS", bi_)])
                        z0 = accS[:, 0, :, 128]
                        z1 = accS[:, 1, :, 128]
                        ak = [("accS", i_) for i_ in range(4)]
                        p.op("dve", lambda h, z0=z0: h.reciprocal(ep_z[:, 0, :], z0), reads=ak, writes=[("ep_z", 0)])
                        p.op("dve", lambda h, z1=z1: h.reciprocal(ep_z[:, 1, :], z1), reads=ak, writes=[("ep_z", 1)])
                        ts(ep_z[:, 1, :], ep_z[:, 1, :], lams[:, 5:6], None, ALU.mult, None,
                           reads=[("ep_z", 1), ("lams", 5)], writes=[("ep_z", 1)])
                        tt(ep_o1[:], accS[:, 1, :, 0:128], ep_z[:, 1, :].unsqueeze(2).broadcast_to([128, 4, 128]), ALU.mult,
                           reads=ak + [("ep_z", 1)], writes=[("ep_o1", 0), ("ep_o1", 1)])
                        tt(ep_o[:], accS[:, 0, :, 0:128], ep_z[:, 0, :].unsqueeze(2).broadcast_to([128, 4, 128]), ALU.mult,
                           reads=ak + [("ep_z", 0)], writes=[("ep_o", 0), ("ep_o", 1)])
                        tt(ep_o[:], ep_o[:], ep_o1[:], ALU.add, reads=[("ep_o", 0), ("ep_o", 1), ("ep_o1", 0), ("ep_o1", 1)],
                           writes=[("ep_o", 0), ("ep_o", 1)])
                        tt(ep_o1[:], ep_o[:], ep_o[:], ALU.mult, reads=[("ep_o", 0), ("ep_o", 1)], writes=[("ep_o1", 0), ("ep_o1", 1)])
                        p.op("dve", lambda h: h.reduce_sum(ep_ss[:], ep_o1[:], AX.X), reads=[("ep_o1", 0), ("ep_o1", 1)], writes=["ep_ss"])
                        def ep_tail(hd=hd, Q=Q):
                            act(ep_ss[:], ep_ss[:], AF.Sqrt, reads=["ep_ss"], writes=["ep_ss"], scale=1.0 / 128.0, bias=1e-5)
                            p.op("dve", lambda h: h.reciprocal(ep_ss[:], ep_ss[:]), reads=["ep_ss"], writes=["ep_ss"])
                            tt(ep_o[:], ep_o[:], ep_ss[:, :].unsqueeze(2).broadcast_to([128, 4, 128]), ALU.mult,
                               reads=[("ep_o", 0), ("ep_o", 1), "ep_ss"], writes=[("ep_o", 0), ("ep_o", 1)])
                            tt(ep_ob[:], ep_o[:], gain_b[:, :].unsqueeze(1).broadcast_to([128, 4, 128]), ALU.mult,
                               reads=[("ep_o", 0), ("ep_o", 1), "gain_b"], writes=["ep_ob"])
                            pT = psb[7][:, 256:512].bitcast(BF16)
                            tr_group([(pT[:, 128 * qs:128 * qs + 128], ep_ob[:, qs, :], ident_b[:]) for qs in range(4)],
                                     reads=["ep_ob", "ident_b"], writes=[("ps", 7)])
                            act(mixT[:, 4 + hd, 512 * Q:512 * Q + 512], pT[:, 0:512], AF.Copy, reads=[("ps", 7)], writes=[("mixT", 4 + hd, Q)])
                        pending_tail.append(ep_tail)
                        if (hd * 4 + Q) < 8:
                            run_tasks(qA, 1)
                while len(pending_tail) > 0:
                    pending_tail.pop(0)()
                run_until(qA, 0)
            else:
                p.op("pool", lambda h: h.memset(mixT[:, 4:8, :], 0.0), writes=[("mixT", 4 + hd, Q) for hd in range(4) for Q in range(4)])

            wo = carve(48 * KB, [128, 8, D], BF16)

            def load_wo():
                p.handoff(["qT", "tabsB", "tabsC", "Sprev", "Sprev0", "rotb"], ["wo"])
                dma("pool", wo[:], din["w_out"][l].rearrange("(k p) n -> p k n", p=128), writes=["wo"], key="wo")
            if do_ssm:
                ssm_main(l, uT, mixT, pre_glu=load_wo)
            else:
                p.handoff(["ebuf", "ep_o", "ep_o1", "ep_ob", "accS", "mixS"], ["mixS"])
                p.op("pool", lambda h: h.memset(mixT[:, 0:4, :], 0.0), writes=[("mixS", ct, Q) for ct in range(4) for Q in range(4)])

            if dbg and "dbg_mix" in dbg_out and sq == 0 and l == 0:
                for kk in range(8):
                    cp(ltmp[:, 0:512], mixT[:, kk, 0:512], reads=[("mixT" if kk >= 4 else "mixS", kk, 0)], writes=["ltmp"])
                    fin.append(dma("sp", dbg_out["dbg_mix"][kk * 128:(kk + 1) * 128, :], ltmp[:, 0:512], reads=["ltmp"], key="dbgm"))

            run_until(qA, nA_gate)
            if not do_ssm:
                load_wo()
            dma("sp", lngb[:, 0, :], din["ln1_g"][l].unsqueeze(0).broadcast_to([128, D]), writes=[("lngb", 0)], key="lng")
            dma("sp", lngb[:, 1, :], din["ln1_b"][l].unsqueeze(0).broadcast_to([128, D]), writes=[("lngb", 1)], key="lnb")

            def resid_ln(tt_i, banks, gi):
                xk = ("x", tt_i)
                xs = x_sb[:, tt_i, :]
                for hb in range(2):
                    tt(ltmp[:, 512 * hb:512 * hb + 512], psb[banks[hb]][:, :], gate_b[:, 0, 512 * hb:512 * hb + 512], ALU.mult,
                       reads=[("ps", banks[hb]), "gate_b"], writes=["ltmp"])
                stt(xs, xs, ALPHA, ltmp[:], ALU.mult, ALU.add, reads=[xk, "ltmp"], writes=[xk])
                for hb in range(2):
                    p.op("dve", lambda h, hb=hb: h.bn_stats(st_sb[:, 6 * hb:6 * hb + 6], x_sb[:, tt_i, 512 * hb:512 * hb + 512]),
                         reads=[xk], writes=[("st", hb)])
                p.op("dve", lambda h: h.bn_aggr(mv_sb[:], st_sb[:]), reads=[("st", 0), ("st", 1)], writes=["mv"])
                act(rs_sb[:, 0:1], mv_sb[:, 1:2], AF.Sqrt, reads=["mv"], writes=["rs"], bias=1e-5, scale=1.0)
                p.op("dve", lambda h: h.reciprocal(rs_sb[:, 0:1], rs_sb[:, 0:1]), reads=["rs"], writes=["rs"])
                ts(rs_sb[:, 1:2], mv_sb[:, 0:1], -1.0, rs_sb[:, 0:1], ALU.mult, ALU.mult, reads=["mv", "rs"], writes=["rs1"])
                act(xs, xs, AF.Identity, reads=[xk, "rs", "rs1"], writes=[xk], scale=rs_sb[:, 0:1], bias=rs_sb[:, 1:2])
                tt(xs, xs, lngb[:, 0, :], ALU.mult, reads=[xk, ("lngb", 0)], writes=[xk], eng="pool")
                tt(xs, xs, lngb[:, 1, :], ALU.add, reads=[xk, ("lngb", 1)], writes=[xk], eng="pool")

            for tt_i in range(NT):
                banks = (0, 1) if tt_i % 2 == 0 else (6, 7)
                for hb in range(2):
                    mm_group([(psb[banks[hb]][:, :], mixT[:, k, 128 * tt_i:128 * tt_i + 128], wo[:, k, 512 * hb:512 * hb + 512],
                               k == 0, k == 7, {}) for k in range(8)],
                             reads=["wo"] + [("mixT" if k >= 4 else "mixS", k, tt_i // 4) for k in range(8)], writes=[("ps", banks[hb])])
                resid_ln(tt_i, banks, 0)

            if dbg and "dbg_x1" in dbg_out and sq == 0 and l == 0:
                fin.append(dma("sp", dbg_out["dbg_x1"], x_sb[:, 0, :], reads=[("x", 0)], key="dbgx1"))

            hT2 = carve(0, [128, 8, 1024], BF16)
            actT = carve(16 * KB, [128, NF, 1024], BF16)
            wd = carve(60 * KB, [128, NF, D], BF16)
            gu = [carve(104 * KB + i * 4 * KB, [128, 2, 8, 128], BF16) for i in range(2)]
            p.handoff(["uT", "mixT", "mixS", "qT", "kT", "vA", "ebuf", "ep_o", "ep_o1", "ep_ob", "accS", "wo", "qraw", "rq1", "vA1"] + S5_MAIN_BUFS + [
                       "hT", "wch", "rotc", "rots", "rtmp"],
                      ["hT2", "actT", "wd", "gu"])
            run_until(qA, 0)
            dma("sp", lngb[:, 0, :], din["ln2_g"][l].unsqueeze(0).broadcast_to([128, D]), writes=[("lngb", 0)], key="lng")
            dma("sp", lngb[:, 1, :], din["ln2_b"][l].unsqueeze(0).broadcast_to([128, D]), writes=[("lngb", 1)], key="lnb")
            gun = 0

            def issue_gu(i_):
                if i_ >= 2 * NF:
                    return
                gb_ = gu[i_ % 2]
                fs_ = slice(128 * (i_ % NF), 128 * (i_ % NF) + 128)
                p.op("pool", lambda h, inc, gb_=gb_, fs_=fs_, l=l: (
                    inc(h.dma_start(out=gb_[:, 0, :, :], in_=din["ffn_w_gate"][l][:, fs_].rearrange("(k p) n -> p k n", p=128))),
                    inc(h.dma_start(out=gb_[:, 1, :, :], in_=din["ffn_w_up"][l][:, fs_].rearrange("(k p) n -> p k n", p=128)))),
                    writes=[("gu", i_ % 2)], dma="gu%d" % (i_ % 2), dma_n=2)
            issue_gu(0)
            issue_gu(1)
            wd_issued = [False]
            for half in range(2):
                tok0 = 8 * half
                build_hT(hT2, "hT2", tok0, 8, 32, 24)
                for f in range(NF):
                    gb = gu[gun % 2]
                    gkey = ("gu", gun % 2)
                    gun += 1
                    for mm in range(2):
                        bg = (f * 2 + mm) % 2
                        bu = 2 + bg
                        mm_group([(psb[bg][:, :], gb[:, 0, k, :], hT2[:, k, 512 * mm:512 * mm + 512], k == 0, k == 7, {}) for k in range(8)],
                                 reads=[gkey] + [("hT2", k, mm) for k in range(8)], writes=[("ps", bg)])
                        mm_group([(psb[bu][:, :], gb[:, 1, k, :], hT2[:, k, 512 * mm:512 * mm + 512], k == 0, k == 7, {}) for k in range(8)],
                                 reads=[gkey] + [("hT2", k, mm) for k in range(8)], writes=[("ps", bu)])
                        asl = actT[:, f, 512 * mm:512 * mm + 512]
                        act(stmp[bg][:], psb[bg][:, :], AF.Silu, reads=[("ps", bg)], writes=[("stmp", bg)])
                        tt(asl, stmp[bg][:], psb[bu][:, :], ALU.mult, reads=[("stmp", bg), ("ps", bu)], writes=[("actT", f, mm)])
                    issue_gu(gun + 1)
                    if not wd_issued[0] and f >= 2:
                        wd_issued[0] = True
                        dma("pool", wd[:], din["ffn_w_down"][l].rearrange("(f p) n -> p f n", p=128), writes=["wd"], key="wd")
                    run_tasks(qB, 1)
                if half == 0:
                    run_until(qB, nB_gate)
                for tl in range(8):
                    tt_i = tok0 + tl
                    banks = (4, 5) if tl % 2 == 0 else (6, 7)
                    for hb in range(2):
                        mm_group([(psb[banks[hb]][:, :], actT[:, f, 128 * tl:128 * tl + 128], wd[:, f, 512 * hb:512 * hb + 512],
                                   f == 0, f == NF - 1, {}) for f in range(NF)],
                                 reads=["wd"] + [("actT", f, tl // 4) for f in range(NF)], writes=[("ps", banks[hb])])
                    resid_ln(tt_i, banks, 1)
            run_until(qB, 0)

        yv = y_out[sq].rearrange("(tt p) d -> p tt d", p=128)
        for t4 in range(4):
            fin.append(dma("sp", yv[:, 4 * t4:4 * t4 + 4, :], x_sb[:, 4 * t4:4 * t4 + 4, :],
                           reads=[("x", 4 * t4 + i) for i in range(4)], key="yst%d" % t4))

    p.emit(fin)
    p.close()
    return nc


def ssm_block(p, din, l, env):
    raise NotImplementedError


_NC_CACHE = {}


def kernel(**inputs):
    n = 8
    if "nc" not in _NC_CACHE:
        _NC_CACHE["nc"] = build_program()
    nc = _NC_CACHE["nc"]
    hc = host_consts()
    in_maps = []
    for c in range(n):
        m = {"x": np.ascontiguousarray(inputs["x"][2 * c:2 * c + 2], dtype=np.float32),
             "c": np.ascontiguousarray(inputs["c"][2 * c:2 * c + 2], dtype=np.float32),
             "positions": np.ascontiguousarray(inputs["positions"][2 * c:2 * c + 2], dtype=np.int32)}
        for name, _ in PARAM_SPECS:
            m[name] = np.ascontiguousarray(inputs[name], dtype=np.float32)
        for k_, v_ in hc.items():
            m[k_] = v_
        in_maps.append(m)
    res = run_bass_kernel_spmd(nc, in_maps, core_ids=list(range(n)))
    out = np.concatenate([np.asarray(r["y"]) for r in res.results], axis=0)
    return out.astype(np.float32)
```

```python
import contextlib
import math
import numpy as np
import ml_dtypes
import concourse.bass as bass
import concourse.mybir as mybir
from concourse.bass_utils import run_bass_kernel_spmd

F32 = mybir.dt.float32
BF16 = mybir.dt.bfloat16
I32 = mybir.dt.int32
ALU = mybir.AluOpType
AF = mybir.ActivationFunctionType
AX = mybir.AxisListType

D = 1024
S = 2048
NT = 16
DFF = 2816
NF = 22
DEPTH = 4
ALPHA = float((2 * DEPTH) ** 0.25)
TWO_PI = float(2 * np.pi)
ENG = ["pe", "act", "dve", "pool", "sp"]
EPOCH = 8192


class Prog:
    def __init__(self, nc):
        self.nc = nc
        self.stack = contextlib.ExitStack()
        self.ops = {e: [] for e in ENG}
        self.count = {e: 0 for e in ENG}
        self.seen = {e: {} for e in ENG}
        self.lastw = {}
        self.readers = {}
        self.dma_val = {}
        self.sems = {}
        self.pending = {}
        self.bufkeys = {}

    def sb(self, name, shape, dt):
        return self.stack.enter_context(self.nc.sbuf_tensor("s_" + name, list(shape), dt))

    def ps(self, name, shape, dt):
        return self.stack.enter_context(self.nc.psum_tensor(name, list(shape), dt))

    def sem(self, name):
        if name not in self.sems:
            self.sems[name] = self.stack.enter_context(self.nc.semaphore(name))
        return self.sems[name]

    def _touch(self, k):
        b = k[0] if isinstance(k, tuple) else k
        ks = self.bufkeys.setdefault(b, set())
        if k not in ks:
            ks.add(k)
            pend = self.pending.get(b)
            if pend and k not in self.lastw and k not in self.readers:
                self.readers[k] = [(s, v) for s, v in pend.items()]

    def handoff(self, old_bufs, new_bufs):
        t = {}
        for b in old_bufs:
            for k in self.bufkeys.get(b, ()):
                toks = list(self.readers.get(k, ()))
                if self.lastw.get(k) is not None:
                    toks.append(self.lastw[k])
                for s, v in toks:
                    if t.get(s, 0) < v:
                        t[s] = v
            for s, v in self.pending.get(b, {}).items():
                if t.get(s, 0) < v:
                    t[s] = v
        for b in new_bufs:
            for k in self.bufkeys.get(b, ()):
                self.lastw.pop(k, None)
                self.readers.pop(k, None)
            self.bufkeys[b] = set()
            self.pending[b] = dict(t)

    def _need(self, eng, tok, waits):
        if tok is None:
            return
        stream, val = tok
        if self.seen[eng].get(stream, 0) >= val:
            return
        self.seen[eng][stream] = val
        waits.append(tok)

    def op(self, eng, fn, reads=(), writes=(), dma=None, dma_n=1):
        waits = []
        is_dma = dma is not None
        for k in reads:
            self._touch(k)
        for k in writes:
            self._touch(k)
        for k in reads:
            tok = self.lastw.get(k)
            if tok is not None:
                if (not is_dma) and tok[0] == ("eng", eng) and eng == "pe":
                    pass
                else:
                    self._need(eng, tok, waits)
            else:
                for tok2 in self.readers.get(k, ()):
                    pass
        for k in writes:
            tok = self.lastw.get(k)
            if tok is not None:
                if (not is_dma) and tok[0] == ("eng", eng):
                    pass
                else:
                    self._need(eng, tok, waits)
            for tok in self.readers.get(k, ()):
                if (not is_dma) and tok[0] == ("eng", eng):
                    continue
                self._need(eng, tok, waits)
        if is_dma:
            prev = self.dma_val.get(dma, 0)
            if prev:
                self._need(eng, (("dma", dma), prev), waits)
            val = prev + 16 * dma_n
            self.dma_val[dma] = val
            mytok = (("dma", dma), val)
            self.ops[eng].append((waits, fn, ("dma", dma)))
        else:
            self.count[eng] += 1
            mytok = (("eng", eng), self.count[eng])
            self.ops[eng].append((waits, fn, ("eng", self.count[eng])))
        for k in writes:
            self.lastw[k] = mytok
            self.readers[k] = []
        for k in reads:
            self.readers.setdefault(k, []).append(mytok)
        return mytok

    def _semfor(self, stream, val):
        if stream[0] == "dma":
            return self.sem("d_" + str(stream[1])), val
        e = stream[1]
        ep = (val - 1) // EPOCH
        return self.sem("e_%s_%d" % (e, ep)), (val - 1) % EPOCH + 1

    def emit(self, final_tokens):
        nc = self.nc
        for e in ENG:
            for waits, fn, inc in self.ops[e]:
                for stream, val in waits:
                    self._semfor(stream, val)
                if inc[0] == "dma":
                    self.sem("d_" + str(inc[1]))
                else:
                    self._semfor(("eng", e), inc[1])
        for stream, val in final_tokens:
            self._semfor(stream, val)
        prog = self
        with nc.Block() as block:
            def replay(e, handle):
                for waits, fn, inc in prog.ops[e]:
                    for stream, val in waits:
                        s, v = prog._semfor(stream, val)
                        handle.wait_ge(s, v)
                    if inc[0] == "dma":
                        s = prog.sem("d_" + str(inc[1]))
                        fn(handle, lambda ins, s=s: ins.then_inc(s, 16))
                    else:
                        ins = fn(handle)
                        s, v = prog._semfor(("eng", e), inc[1])
                        ins.then_inc(s, 1)
                if e == "sp":
                    for stream, val in final_tokens:
                        s, v = prog._semfor(stream, val)
                        handle.wait_ge(s, v)

            @block.tensor
            def _(h):
                replay("pe", h)

            @block.scalar
            def _(h):
                replay("act", h)

            @block.vector
            def _(h):
                replay("dve", h)

            @block.gpsimd
            def _(h):
                replay("pool", h)

            @block.sync
            def _(h):
                replay("sp", h)

    def close(self):
        self.stack.close()


PARAM_SPECS = [
    ("mod_w", [DEPTH, D, 6 * D]), ("mod_b", [DEPTH, 6 * D]), ("w_in", [DEPTH, D, 2048]),
    ("ssm_a_re", [DEPTH, 32, 64]), ("ssm_a_im", [DEPTH, 32, 64]), ("ssm_log_step", [DEPTH, 32]),
    ("ssm_b_re", [DEPTH, 32, 64, 16]), ("ssm_b_im", [DEPTH, 32, 64, 16]),
    ("ssm_c_re", [DEPTH, 32, 16, 64]), ("ssm_c_im", [DEPTH, 32, 16, 64]), ("ssm_d", [DEPTH, 32, 16]),
    ("glu_w", [DEPTH, 512, 512]), ("glu_b", [DEPTH, 512]),
    ("lam_q1", [DEPTH, 64]), ("lam_k1", [DEPTH, 64]), ("lam_q2", [DEPTH, 64]), ("lam_k2", [DEPTH, 64]),
    ("subln_w", [DEPTH, 128]), ("w_out", [DEPTH, D, D]), ("ln1_g", [DEPTH, D]), ("ln1_b", [DEPTH, D]),
    ("ffn_w_gate", [DEPTH, D, DFF]), ("ffn_w_up", [DEPTH, D, DFF]), ("ffn_w_down", [DEPTH, DFF, D]),
    ("ln2_g", [DEPTH, D]), ("ln2_b", [DEPTH, D]),
]


def host_consts():
    c = {}
    c["ident"] = np.eye(128, dtype=np.float32)
    R = np.zeros((128, 128), np.float32)
    for i in range(128):
        d = i % 64
        if d < 8:
            R[i + 8, i] = 1.0
        elif d < 16:
            R[i - 8, i] = 1.0
    c["rmat"] = R
    c["negtri"] = np.where(np.arange(128)[:, None] <= np.arange(128)[None, :], 0.0, -30000.0).astype(np.float32)
    c["bmask"] = (np.arange(128)[:, None] // 16 == np.arange(128)[None, :] // 16).astype(np.float32)
    c["jidx"] = np.arange(256, dtype=np.float32)[None, :]
    freqs = 500000.0 ** (-np.arange(0, 16, 2, dtype=np.float64) / 16.0)
    fc = np.zeros((128, 2), np.float32)
    for i in range(128):
        d = i % 64
        if d < 8:
            fc[i, 0] = freqs[d] / (2 * np.pi)
            fc[i, 1] = -freqs[d] / (2 * np.pi)
        elif d < 16:
            fc[i, 0] = freqs[d - 8] / (2 * np.pi)
            fc[i, 1] = freqs[d - 8] / (2 * np.pi)
    c["ropef"] = fc
    return c


def build_program(nseq=2, nl=DEPTH, dbg=None, do_attn=True, do_ssm=True):
    nc = bass.Bass("TRN2", target_bir_lowering=False)
    din = {}
    din["x"] = nc.dram_tensor("x", [nseq, S, D], F32, kind="ExternalInput").ap()
    din["c"] = nc.dram_tensor("c", [nseq, D], F32, kind="ExternalInput").ap()
    din["positions"] = nc.dram_tensor("positions", [nseq, S], I32, kind="ExternalInput").ap()
    for name, shp in PARAM_SPECS:
        din[name] = nc.dram_tensor(name, shp, F32, kind="ExternalInput").ap()
    hc = host_consts()
    for name, arr in hc.items():
        din[name] = nc.dram_tensor(name, list(arr.shape), F32, kind="ExternalInput").ap()
    y_out = nc.dram_tensor("y", [nseq, S, D], F32, kind="ExternalOutput").ap()
    dbg_out = {}
    if dbg:
        for name, shp in dbg.items():
            dbg_out[name] = nc.dram_tensor(name, shp, F32, kind="ExternalOutput").ap()

    tab_bt8 = nc.dram_tensor("tab_bt8", [DEPTH, 4, 128, 2 * 8 * 128], BF16).ap()
    tab_ctr = nc.dram_tensor("tab_ctr", [DEPTH, 4, 128, 4 * 2 * 8 * 32], BF16).ap()
    tab_kt = nc.dram_tensor("tab_kt", [DEPTH, 4, 128, 8 * 128], BF16).ap()
    tab_rot = nc.dram_tensor("tab_rot", [DEPTH, 4, 128, 2 * 4 * 256], F32).ap()
    tab_small = nc.dram_tensor("tab_small", [DEPTH, 128, 32], F32).ap()

    p = Prog(nc)
    fin = []

    def dma(q, out, in_, reads=(), writes=(), key=None):
        return p.op(q, lambda h, inc: inc(h.dma_start(out=out, in_=in_)), reads=reads, writes=writes, dma=key)

    def act(out, in_, func, reads, writes, **kw):
        return p.op("act", lambda h: h.activation(out, in_, func, **kw), reads=reads, writes=writes)

    def tt(out, a, b, op, reads, writes, eng="dve"):
        return p.op(eng, lambda h: h.tensor_tensor(out, a, b, op), reads=reads, writes=writes)

    def ts(out, a, s1, s2, op0, op1, reads, writes, eng="dve"):
        if s2 is None:
            return p.op(eng, lambda h: h.tensor_scalar(out, a, s1, None, op0), reads=reads, writes=writes)
        return p.op(eng, lambda h: h.tensor_scalar(out, a, s1, s2, op0, op1), reads=reads, writes=writes)

    def stt(out, a, sc, b, op0, op1, reads, writes, eng="dve"):
        return p.op(eng, lambda h: h.scalar_tensor_tensor(out, a, sc, b, op0, op1), reads=reads, writes=writes)

    def cp(out, a, reads, writes, eng="dve"):
        return p.op(eng, lambda h: h.tensor_copy(out, a), reads=reads, writes=writes)

    def mm_group(mms, reads, writes):
        def fn(h):
            ins = None
            for (o, l, r, st, sp_, kw) in mms:
                ins = h.matmul(o, l, r, start=st, stop=sp_, **kw)
            return ins
        return p.op("pe", fn, reads=reads, writes=writes)

    def tr_group(trs, reads, writes):
        def fn(h):
            ins = None
            for (o, i, idn) in trs:
                ins = h.transpose(o, i, idn)
            return ins
        return p.op("pe", fn, reads=reads, writes=writes)

    x_sb = p.sb("x_sb", [128, NT, D], F32)
    ARENA_F32 = 112 * 256 + 64
    arena = p.sb("arena", [128, ARENA_F32], F32)

    def carve(off_bytes, shape, dt):
        size = 2 if dt == BF16 else 4
        n = int(np.prod(shape[1:]))
        assert off_bytes % 4 == 0 and (n * size) % 4 == 0
        assert off_bytes + n * size <= ARENA_F32 * 4, (off_bytes, shape)
        v = arena[:, off_bytes // 4:(off_bytes + n * size) // 4]
        if dt != F32:
            v = v.bitcast(dt)
        if len(shape) == 2:
            return v
        names = " ".join("a%d" % i for i in range(len(shape) - 1))
        kw = {"a%d" % i: shape[i + 1] for i in range(len(shape) - 1)}
        return v.rearrange("p (%s) -> p %s" % (names, names), **kw)

    KB = 1024
    ident_f = p.sb("ident_f", [128, 128], F32)
    ident_b = p.sb("ident_b", [128, 128], BF16)
    rmat_b = p.sb("rmat_b", [128, 128], BF16)
    negtri_b = p.sb("negtri_b", [128, 128], BF16)
    bmask_f = p.sb("bmask_f", [128, 128], F32)
    jidx_f = p.sb("jidx_f", [128, 256], F32)
    ropef = p.sb("ropef", [128, 2], F32)
    modcolB = [p.sb("modcol%d" % i, [128, 48], F32) for i in range(2)]
    modbB = [p.sb("modb_col%d" % i, [128, 48], F32) for i in range(2)]
    cond_all = p.sb("cond_all", [128, 2, 8], BF16)
    gate_b = p.sb("gate_b", [128, 1, D], F32)
    lngb = p.sb("lngb", [128, 2, D], F32)
    modw_buf = [p.sb("modw%d" % i, [128, 8, 128], BF16) for i in range(2)]
    cond_col = p.sb("cond_col", [128, 8], F32)
    cond_rep = p.sb("cond_rep", [128, 8, 128], BF16)
    gain_b = p.sb("gain_b", [128, 128], F32)
    lamv = p.sb("lamv", [128, 4, 64], F32)
    lams = p.sb("lams", [128, 8], F32)
    st_sb = p.sb("st_sb", [128, 12], F32)
    eps_col = p.sb("eps_col", [128, 1], F32)
    mv_sb = p.sb("mv_sb", [128, 2], F32)
    rs_sb = p.sb("rs_sb", [128, 2], F32)
    ltmp = p.sb("ltmp", [128, D], F32)
    stmp = [p.sb("stmp%d" % i, [128, 512], F32) for i in range(2)]
    ep_z = p.sb("ep_z", [128, 2, 4], F32)
    ep_ss = p.sb("ep_ss", [128, 4], F32)

    ps_all = p.ps("ps_all", [128, 8, 512], F32)
    psb = [ps_all[:, i, :] for i in range(8)]

    def psb_bf(i):
        return psb[i].bitcast(BF16)

    dma("sp", ident_f[:], din["ident"], writes=["ident_f"], key="c0")
    dma("pool", ident_b[:], din["ident"], writes=["ident_b"], key="c1")
    dma("pool", rmat_b[:], din["rmat"], writes=["rmat_b"], key="c2")
    dma("pool", negtri_b[:], din["negtri"], writes=["negtri_b"], key="c3")
    dma("sp", bmask_f[:], din["bmask"], writes=["bmask_f"], key="c4")
    dma("sp", jidx_f[:], din["jidx"].broadcast_to([128, 256]), writes=["jidx_f"], key="c5")
    dma("sp", ropef[:], din["ropef"], writes=["ropef"], key="c6")
    p.op("dve", lambda h: h.memset(eps_col[:], 1e-5), writes=["eps_col"])

    def frac_round(dst, src, itmp, ftmp, reads, writes, eng="dve"):
        cp(itmp, src, reads=reads, writes=["_frac_i"], eng=eng)
        cp(ftmp, itmp, reads=["_frac_i"], writes=["_frac_f"], eng=eng)
        tt(dst, src, ftmp, ALU.subtract, reads=list(reads) + ["_frac_f"], writes=writes, eng=eng)


    S5_MAIN_BUFS = ["tabsB", "tabsC", "rotb", "small", "r8tab", "t1", "t2", "b1", "b2", "Sprev", "Sprev0", "yact", "gluw", "sg"]
    S5_BUILD_BUFS = ["spf", "Ppow", "aq", "Bs", "BX", "Cld", "CX", "BXb", "BPall", "CPall", "BT8all", "CTrall", "KTall",
                     "bt1", "bt2", "bitmp", "bftmp", "bcs", "smallb"]

    def ssm_build(l):
        p.handoff(S5_BUILD_BUFS, S5_BUILD_BUFS)
        spf = carve(0, [128, 40, 16], F32)
        Pre = carve(2560, [128, 9, 16], F32)
        Pim = carve(3328, [128, 9, 16], F32)
        aq = carve(4 * KB, [128, 3, 128], F32)
        Bs = carve(5632, [128, 2, 16, 16], F32)
        BX = carve(7680, [128, 2, 16, 32], F32)
        Cld = carve(11776, [128, 2, 4, 128], F32)
        CX = carve(15872, [128, 2, 16, 32], F32)
        BXb = carve(19968, [128, 2, 16, 32], BF16)
        smallb = carve(22016, [128, 32], F32)
        BPall = carve(22 * KB, [128, 8, 2, 512], BF16)
        CPall = carve(38 * KB, [128, 9, 2, 512], BF16)
        BT8all = carve(56 * KB, [128, 4, 2, 8, 128], BF16)
        CTrall = carve(72 * KB, [128, 16, 2, 8, 32], BF16)
        KTall = carve(88 * KB, [128, 4, 8, 128], BF16)
        bt1 = carve(96 * KB, [128, 1024], F32)
        bt2 = carve(100 * KB, [128, 1024], F32)
        bitmp = carve(104 * KB, [128, 1024], F32).bitcast(I32)
        bftmp = carve(108 * KB, [128, 1024], F32)
        bcs = carve(22 * KB, [128, 2, 4, 256], F32)

        def sl(i):
            return spf[:, i, :]
        L_RE, L_IM, DL, LR, TRN, MAG, COSA, SINA, LBR, LBI, DEN, RDEN, NRE, KR, KI, TA, TB, R8, FF, FHI, FLO, TI, TF, TC, GLUB, DCOL = range(26)
        sk = lambda i: ("spf", i)

        dma("sp", aq[0:16, 0, :], din["ssm_a_re"][l].rearrange("(q g) p -> q (g p)", g=2), writes=[("aq", 0)], key="aq0")
        dma("sp", aq[0:16, 1, :], din["ssm_a_im"][l].rearrange("(q g) p -> q (g p)", g=2), writes=[("aq", 1)], key="aq1")
        dma("sp", ltmp[0:16, 0:2], din["ssm_log_step"][l].rearrange("(q g) -> q g", g=2), writes=["ltmp"], key="aq2")
        act(ltmp[0:16, 0:2], ltmp[0:16, 0:2], AF.Exp, reads=["ltmp"], writes=["ltmp"])
        cp(aq[0:16, 2, :].rearrange("q (g p) -> q g p", g=2), ltmp[0:16, 0:2].unsqueeze(2).broadcast_to([16, 2, 64]),
           reads=["ltmp"], writes=[("aq", 2)])
        tr_group([(psb[7][:, 16 * i:16 * i + 16], aq[0:16, i, :], ident_f[0:16, 0:16]) for i in range(3)],
                 reads=[("aq", 0), ("aq", 1), ("aq", 2), "ident_f"], writes=[("ps", 7)])
        ts(sl(L_RE), psb[7][:, 0:16], -1e-4, None, ALU.min, None, reads=[("ps", 7)], writes=[sk(L_RE)])
        cp(sl(L_IM), psb[7][:, 16:32], reads=[("ps", 7)], writes=[sk(L_IM)])
        cp(sl(DL), psb[7][:, 32:48], reads=[("ps", 7)], writes=[sk(DL)])
        dma("sp", ltmp[0:4, 0:128], din["ssm_d"][l].rearrange("g c -> (g c)").rearrange("(o p) -> o p", p=128), writes=["ltmp"], key="aq3")
        dma("sp", ltmp[4:8, 0:128], din["glu_b"][l].rearrange("(o p) -> o p", p=128), writes=["ltmp"], key="aq3")
        tr_group([(psb[7][:, 64:72], ltmp[0:8, 0:128], ident_f[0:8, 0:8])], reads=["ltmp", "ident_f"], writes=[("ps", 7)])
        cp(spf[:, DCOL, 0:4], psb[7][:, 64:68], reads=[("ps", 7)], writes=[sk(DCOL)])
        cp(smallb[:, 16:20], psb[7][:, 68:72], reads=[("ps", 7)], writes=["smallb"])
        for c, nm in enumerate(("ssm_b_re", "ssm_b_im")):
            dma("sp", Bs[:, c, :, :], din[nm][l].rearrange("(q g) p c -> (g p) q c", g=2), writes=[("Bs", c)], key="bs%d" % c)
        for c, nm in enumerate(("ssm_c_re", "ssm_c_im")):
            csrc = din[nm][l].rearrange("(o q g) c s -> q c o g s", o=4, q=4, g=2)
            for q4_ in range(4):
                for o_ in range(4):
                    dma("sp", Cld[16 * q4_:16 * q4_ + 16, c, o_, :].rearrange("p (g s) -> p g s", g=2), csrc[q4_][:, o_, :, :],
                        writes=[("Cld", c)], key="cld%d_%d" % (q4_, o_))

        tt(sl(LR), sl(L_RE), sl(DL), ALU.mult, reads=[sk(L_RE), sk(DL)], writes=[sk(LR)])
        stt(sl(TRN), sl(L_IM), 1.0 / TWO_PI, sl(DL), ALU.mult, ALU.mult, reads=[sk(L_IM), sk(DL)], writes=[sk(TRN)])
        act(sl(MAG), sl(LR), AF.Exp, reads=[sk(LR)], writes=[sk(MAG)])
        act(smallb[:, 0:16], sl(LR), AF.Exp, reads=[sk(LR)], writes=["smallb"], scale=8.0)
        tiv = spf[:, TI, :].bitcast(I32)

        def frac16(dst, src):
            cp(tiv, sl(src), reads=[sk(src)], writes=[sk(TI)])
            cp(sl(TF), tiv, reads=[sk(TI)], writes=[sk(TF)])
            tt(sl(dst), sl(src), sl(TF), ALU.subtract, reads=[sk(src), sk(TF)], writes=[sk(dst)])
        frac16(TA, TRN)
        act(sl(SINA), sl(TA), AF.Sin, reads=[sk(TA)], writes=[sk(SINA)], scale=TWO_PI)
        ts(sl(TC), sl(TRN), 0.25, None, ALU.add, None, reads=[sk(TRN)], writes=[sk(TC)])
        frac16(TA, TC)
        act(sl(COSA), sl(TA), AF.Sin, reads=[sk(TA)], writes=[sk(COSA)], scale=TWO_PI)
        tt(sl(LBR), sl(MAG), sl(COSA), ALU.mult, reads=[sk(MAG), sk(COSA)], writes=[sk(LBR)])
        tt(sl(LBI), sl(MAG), sl(SINA), ALU.mult, reads=[sk(MAG), sk(SINA)], writes=[sk(LBI)])
        tt(sl(TA), sl(L_RE), sl(L_RE), ALU.mult, reads=[sk(L_RE)], writes=[sk(TA)])
        tt(sl(TB), sl(L_IM), sl(L_IM), ALU.mult, reads=[sk(L_IM)], writes=[sk(TB)])
        tt(sl(DEN), sl(TA), sl(TB), ALU.add, reads=[sk(TA), sk(TB)], writes=[sk(DEN)])
        p.op("dve", lambda h: h.reciprocal(spf[:, RDEN, :], spf[:, DEN, :]), reads=[sk(DEN)], writes=[sk(RDEN)])
        ts(sl(NRE), sl(LBR), -1.0, None, ALU.add, None, reads=[sk(LBR)], writes=[sk(NRE)])
        tt(sl(TA), sl(NRE), sl(L_RE), ALU.mult, reads=[sk(NRE), sk(L_RE)], writes=[sk(TA)])
        tt(sl(TB), sl(LBI), sl(L_IM), ALU.mult, reads=[sk(LBI), sk(L_IM)], writes=[sk(TB)])
        tt(sl(TA), sl(TA), sl(TB), ALU.add, reads=[sk(TA), sk(TB)], writes=[sk(TA)])
        tt(sl(KR), sl(TA), sl(RDEN), ALU.mult, reads=[sk(TA), sk(RDEN)], writes=[sk(KR)])
        tt(sl(TA), sl(LBI), sl(L_RE), ALU.mult, reads=[sk(LBI), sk(L_RE)], writes=[sk(TA)])
        tt(sl(TB), sl(NRE), sl(L_IM), ALU.mult, reads=[sk(NRE), sk(L_IM)], writes=[sk(TB)])
        tt(sl(TA), sl(TA), sl(TB), ALU.subtract, reads=[sk(TA), sk(TB)], writes=[sk(TA)])
        tt(sl(KI), sl(TA), sl(RDEN), ALU.mult, reads=[sk(TA), sk(RDEN)], writes=[sk(KI)])
        p.op("dve", lambda h: h.memset(Pre[:, 0, :], 1.0), writes=[("Ppow", 0)])
        p.op("dve", lambda h: h.memset(Pim[:, 0, :], 0.0), writes=[("Ppow", 0)])
        cp(Pre[:, 1, :], sl(LBR), reads=[sk(LBR)], writes=[("Ppow", 1)])
        cp(Pim[:, 1, :], sl(LBI), reads=[sk(LBI)], writes=[("Ppow", 1)])
        for n in range(1, 8):
            rk = [("Ppow", n), sk(LBR), sk(LBI)]
            tt(sl(TA), Pre[:, n, :], sl(LBR), ALU.mult, reads=rk, writes=[sk(TA)])
            tt(sl(TB), Pim[:, n, :], sl(LBI), ALU.mult, reads=rk, writes=[sk(TB)])
            tt(Pre[:, n + 1, :], sl(TA), sl(TB), ALU.subtract, reads=[sk(TA), sk(TB)], writes=[("Ppow", n + 1)])
            tt(sl(TA), Pre[:, n, :], sl(LBI), ALU.mult, reads=rk, writes=[sk(TA)])
            tt(sl(TB), Pim[:, n, :], sl(LBR), ALU.mult, reads=rk, writes=[sk(TB)])
            tt(Pim[:, n + 1, :], sl(TA), sl(TB), ALU.add, reads=[sk(TA), sk(TB), ("Ppow", n + 1)], writes=[("Ppow", n + 1)])
        ts(sl(FF), sl(TRN), 8.0, None, ALU.mult, None, reads=[sk(TRN)], writes=[sk(FF)])
        ts(sl(TA), sl(FF), 1024.0, None, ALU.mult, None, reads=[sk(FF)], writes=[sk(TA)])
        cp(tiv, sl(TA), reads=[sk(TA)], writes=[sk(TI)])
        cp(sl(TF), tiv, reads=[sk(TI)], writes=[sk(TF)])
        ts(sl(FHI), sl(TF), 1.0 / 1024.0, None, ALU.mult, None, reads=[sk(TF)], writes=[sk(FHI)])
        tt(sl(FLO), sl(FF), sl(FHI), ALU.subtract, reads=[sk(FF), sk(FHI)], writes=[sk(FLO)])
        dma("sp", tab_small[l], smallb[:], reads=["smallb"], writes=[("tabd", l, "small")], key="tsmall")

        p.op("dve", lambda h: h.memset(BX[:], 0.0), writes=[("BX", 0), ("BX", 1)])
        p.op("dve", lambda h: h.memset(CX[:], 0.0), writes=[("CX", 0), ("CX", 1)])
        for hp in range(2):
            rs = slice(64 * hp, 64 * hp + 64)
            cs = slice(16 * hp, 16 * hp + 16)
            krb = spf[rs, KR, :].unsqueeze(2).broadcast_to([64, 16, 16])
            kib = spf[rs, KI, :].unsqueeze(2).broadcast_to([64, 16, 16])
            u1 = bt1[rs, 0:256].rearrange("p (q c) -> p q c", q=16)
            u2 = bt2[rs, 0:256].rearrange("p (q c) -> p q c", q=16)
            rk = [("Bs", 0), ("Bs", 1), sk(KR), sk(KI)]
            tt(u1, Bs[rs, 0, :, :], krb, ALU.mult, reads=rk, writes=["bt1"])
            tt(u2, Bs[rs, 1, :, :], kib, ALU.mult, reads=rk, writes=["bt2"])
            tt(BX[rs, 0, :, cs], u1, u2, ALU.subtract, reads=["bt1", "bt2", ("BX", 0)], writes=[("BX", 0)])
            tt(u1, Bs[rs, 1, :, :], krb, ALU.mult, reads=rk, writes=["bt1"])
            tt(u2, Bs[rs, 0, :, :], kib, ALU.mult, reads=rk, writes=["bt2"])
            tt(BX[rs, 1, :, cs], u1, u2, ALU.add, reads=["bt1", "bt2", ("BX", 1)], writes=[("BX", 1)])
        cp(BXb[:, 0, :, :], BX[:, 0, :, :], reads=[("BX", 0)], writes=["BXb"])
        ts(BXb[:, 1, :, :], BX[:, 1, :, :], -1.0, None, ALU.mult, None, reads=[("BX", 1), "BXb"], writes=["BXb"])
        for c in range(2):
            tr_group([(psb[6][:, 64 * o:64 * o + 64], Cld[0:64, c, o, :], ident_f[0:64, 0:64]) for o in range(4)],
                     reads=[("Cld", c), "ident_f"], writes=[("ps", 6)])
            for hp in range(2):
                rs = slice(64 * hp, 64 * hp + 64)
                cs = slice(16 * hp, 16 * hp + 16)
                cp(CX[rs, c, :, cs], psb[6][rs, 0:256].rearrange("p (q c) -> p q c", q=16), reads=[("ps", 6), ("CX", c)], writes=[("CX", c)])

        def pb(tab, n):
            return tab[:, n, :].unsqueeze(2).broadcast_to([128, 16, 32])

        def v16(ap2):
            return ap2.rearrange("p (q c) -> p q c", q=16)
        u1 = bt1[:, 0:512].rearrange("p (q c) -> p q c", q=16)
        u2 = bt2[:, 0:512].rearrange("p (q c) -> p q c", q=16)
        bxk = [("BX", 0), ("BX", 1)]
        cxk = [("CX", 0), ("CX", 1)]
        for n in range(8):
            rk = bxk + [("Ppow", n)]
            tt(u1, BX[:, 0, :, :], pb(Pre, n), ALU.mult, reads=rk, writes=["bt1"])
            tt(u2, BX[:, 1, :, :], pb(Pim, n), ALU.mult, reads=rk, writes=["bt2"])
            tt(v16(BPall[:, n, 0, :]), u1, u2, ALU.subtract, reads=["bt1", "bt2"], writes=[("BPall", n)])
            tt(u1, BX[:, 1, :, :], pb(Pre, n), ALU.mult, reads=rk, writes=["bt1"])
            tt(u2, BX[:, 0, :, :], pb(Pim, n), ALU.mult, reads=rk, writes=["bt2"])
            tt(v16(BPall[:, n, 1, :]), u1, u2, ALU.add, reads=["bt1", "bt2", ("BPall", n)], writes=[("BPall", n)])
        for o in range(4):
            for c in range(2):
                bank = 6 + (o * 2 + c) % 2
                pT = psb_bf(bank)
                tr_group([(pT[:, 128 * s8:128 * s8 + 128], BPall[:, 7 - s8, c, 128 * o:128 * o + 128], ident_b[:]) for s8 in range(8)],
                         reads=[("BPall", n) for n in range(8)] + ["ident_b"], writes=[("ps", bank)])
                cp(BT8all[:, o, c, :, :], pT[:, :].rearrange("p (s m) -> p s m", s=8), reads=[("ps", bank)], writes=[("BT8all", o)])
            dma("sp", tab_bt8[l, o], BT8all[:, o, :, :, :].rearrange("p c s m -> p (c s m)"), reads=[("BT8all", o)],
                writes=[("tabd", l, "bt8", o)], key="tbt8")
        w1 = bftmp[:, 0:512].rearrange("p (q c) -> p q c", q=16)
        w2 = bftmp[:, 512:1024].rearrange("p (q c) -> p q c", q=16)
        for tau in range(9):
            rk = cxk + [("Ppow", tau)]
            tt(w1, CX[:, 0, :, :], pb(Pre, tau), ALU.mult, reads=rk, writes=["bftmp"], eng="pool")
            tt(w2, CX[:, 1, :, :], pb(Pim, tau), ALU.mult, reads=rk, writes=["bftmp"], eng="pool")
            tt(v16(CPall[:, tau, 0, :]), w1, w2, ALU.subtract, reads=["bftmp"], writes=[("CPall", tau)], eng="pool")
            tt(w1, CX[:, 1, :, :], pb(Pre, tau), ALU.mult, reads=rk, writes=["bftmp"], eng="pool")
            tt(w2, CX[:, 0, :, :], pb(Pim, tau), ALU.mult, reads=rk, writes=["bftmp"], eng="pool")
            tt(v16(CPall[:, tau, 1, :]), w1, w2, ALU.add, reads=["bftmp", ("CPall", tau)], writes=[("CPall", tau)], eng="pool")
        for c in range(2):
            cp(CTrall[:, :, c, :, :].rearrange("p q t c -> p t q c"),
               CPall[:, 1:9, c, :].rearrange("p t (q c) -> p t q c", q=16),
               reads=[("CPall", t) for t in range(1, 9)], writes=[("CTrall", c)])
        for o in range(4):
            dma("sp", tab_ctr[l, o], CTrall[:, 4 * o:4 * o + 4, :, :, :].rearrange("p q c t e -> p (q c t e)"),
                reads=[("CTrall", 0), ("CTrall", 1)], writes=[("tabd", l, "ctr", o)], key="tctr")
        for o in range(4):
            for hb in range(2):
                mms = []
                for tl in range(4):
                    tau = 4 * hb + tl
                    mms.append((psb[4 + hb][:, 128 * tl:128 * tl + 128], BXb[:, 0, 4 * o:4 * o + 4, :].rearrange("p q c -> p (q c)"),
                                CPall[:, tau, 0, 128 * o:128 * o + 128], tl == 0, False, {"skip_group_check": True}))
                    mms.append((psb[4 + hb][:, 128 * tl:128 * tl + 128], BXb[:, 1, 4 * o:4 * o + 4, :].rearrange("p q c -> p (q c)"),
                                CPall[:, tau, 1, 128 * o:128 * o + 128], False, True, {"skip_group_check": True}))
                mm_group(mms, reads=["BXb"] + [("CPall", 4 * hb + tl) for tl in range(4)], writes=[("ps", 4 + hb)])
            tt(bt1[:, 0:128], psb[4][:, 0:128], bmask_f[:], ALU.mult, reads=[("ps", 4), "bmask_f"], writes=["bt1"])
            stt(KTall[:, o, 0, :], ident_f[:], spf[:, DCOL, o:o + 1], bt1[:, 0:128], ALU.mult, ALU.add,
                reads=["ident_f", sk(DCOL), "bt1"], writes=[("KTall", o)])
            tt(KTall[:, o, 1:4, :], psb[4][:, 128:512].rearrange("p (t c) -> p t c", t=3),
               bmask_f[:, :].unsqueeze(1).broadcast_to([128, 3, 128]), ALU.mult, reads=[("ps", 4), "bmask_f", ("KTall", o)], writes=[("KTall", o)])
            tt(KTall[:, o, 4:8, :], psb[5][:, :].rearrange("p (t c) -> p t c", t=4),
               bmask_f[:, :].unsqueeze(1).broadcast_to([128, 4, 128]), ALU.mult, reads=[("ps", 5), "bmask_f", ("KTall", o)], writes=[("KTall", o)])
            dma("sp", tab_kt[l, o], KTall[:, o, :, :].rearrange("p t c -> p (t c)"), reads=[("KTall", o)],
                writes=[("tabd", l, "kt", o)], key="tkt")

        p.handoff(["BPall"], ["bcs"])
        jb4 = jidx_f[:, :].unsqueeze(1).broadcast_to([128, 4, 256])
        v3 = lambda t: t[:, :].rearrange("p (q j) -> p q j", q=4)
        for o in range(4):
            fhi = spf[:, FHI, 4 * o:4 * o + 4].unsqueeze(2).broadcast_to([128, 4, 256])
            flo = spf[:, FLO, 4 * o:4 * o + 4].unsqueeze(2).broadcast_to([128, 4, 256])
            tt(v3(bt1), jb4, fhi, ALU.mult, reads=["jidx_f", sk(FHI)], writes=["bt1"])
            cp(bitmp[:], bt1[:], reads=["bt1"], writes=["bitmp"])
            cp(bftmp[:], bitmp[:], reads=["bitmp"], writes=["bftmp"])
            tt(bt1[:], bt1[:], bftmp[:], ALU.subtract, reads=["bt1", "bftmp"], writes=["bt1"])
            tt(v3(bt2), jb4, flo, ALU.mult, reads=["jidx_f", sk(FLO)], writes=["bt2"])
            tt(bt1[:], bt1[:], bt2[:], ALU.add, reads=["bt1", "bt2"], writes=["bt1"])
            cp(bitmp[:], bt1[:], reads=["bt1"], writes=["bitmp"])
            cp(bftmp[:], bitmp[:], reads=["bitmp"], writes=["bftmp"])
            tt(bt2[:], bt1[:], bftmp[:], ALU.subtract, reads=["bt1", "bftmp"], writes=["bt2"])
            act(bcs[:, 1, :, :], v3(bt2), AF.Sin, reads=["bt2"], writes=[("bcs", 1)], scale=TWO_PI)
            ts(bt2[:], bt1[:], 0.25, None, ALU.add, None, reads=["bt1"], writes=["bt2"])
            cp(bitmp[:], bt2[:], reads=["bt2"], writes=["bitmp"])
            cp(bftmp[:], bitmp[:], reads=["bitmp"], writes=["bftmp"])
            tt(bt2[:], bt2[:], bftmp[:], ALU.subtract, reads=["bt2", "bftmp"], writes=["bt2"])
            act(bcs[:, 0, :, :], v3(bt2), AF.Sin, reads=["bt2"], writes=[("bcs", 0)], scale=TWO_PI)
            dma("sp", tab_rot[l, o], bcs[:, :, :, :].rearrange("p c q j -> p (c q j)"), reads=[("bcs", 0), ("bcs", 1)],
                writes=[("tabd", l, "rot", o)], key="trot")

    def ssm_main(l, uT, mixT, pre_glu=None):
        SB = 48 * KB
        p.handoff(["qT", "kT", "vA", "vA1", "qraw", "rq1", "ebuf", "ep_o", "ep_o1", "ep_ob", "accS", "mixS"], S5_MAIN_BUFS + ["mixS"])
        tabsB = carve(SB, [128, 2048], BF16)
        tabsC = [carve(SB + 4096 + i * 6144, [128, 3072], BF16) for i in range(2)]
        SprevB = [carve(SB + 49408, [128, 4, 2, 258], BF16), carve(SB + 16384, [128, 4, 2, 258], BF16)]
        rotb = carve(SB + 20736, [128, 2, 4, 256], F32)
        r8tab = carve(SB + 28928, [128, 4, 256], F32)
        t1 = carve(SB + 33024, [128, 4, 256], F32)
        t2 = carve(SB + 37120, [128, 4, 256], F32)
        b1 = carve(SB + 41216, [128, 4, 256], F32)
        b2 = carve(SB + 45312, [128, 4, 256], F32)
        yact = [carve(SB + 53536, [128, 8, 128], BF16)]
        gluw = carve(SB + 55584, [128, 4, 512], BF16)
        sg = [carve(SB + 59680 + i * 2048, [128, 512], F32) for i in range(2)]
        small = carve(SB + 63776, [128, 32], F32)

        dma("sp", small[:], tab_small[l], reads=[("tabd", l, "small")], writes=["small"], key="lsmall")
        dma("pool", gluw[:], din["glu_w"][l].rearrange("(o p) n -> p o n", p=128), writes=["gluw"], key="gluw")
        for i_ in range(2):
            p.op("dve", lambda h, i_=i_: h.memset(SprevB[i_][:, :, :, 0:1], 0.0), writes=[("Sprev0", i_)])

        def s5_front(o):
            tb = tabsC[o % 2]
            tk = ("tabsC", o % 2)
            Sprev = SprevB[o % 2]
            spk = ("Sprev", o % 2)
            BT8 = tabsB[:, :].rearrange("p (c s m) -> p c s m", c=2, s=8)
            CTr = tb[:, 0:2048].rearrange("p (q c t e) -> p q c t e", q=4, c=2, t=8)
            KT = tb[:, 2048:3072].rearrange("p (t c) -> p t c", t=8)
            dma("sp", tabsB[:, :], tab_bt8[l, o], reads=[("tabd", l, "bt8", o)], writes=["tabsB"], key="ltabsB")
            p.op("sp", lambda h, inc, tb=tb, o=o, l=l: (
                inc(h.dma_start(out=tb[:, 0:2048], in_=tab_ctr[l, o])),
                inc(h.dma_start(out=tb[:, 2048:3072], in_=tab_kt[l, o]))),
                reads=[("tabd", l, "ctr", o), ("tabd", l, "kt", o)], writes=[tk], dma="ltabs%d" % (o % 2), dma_n=2)
            dma("sp", rotb[:, :, :, :].rearrange("p c q j -> p (c q j)"), tab_rot[l, o], reads=[("tabd", l, "rot", o)], writes=["rotb"], key="lrot")
            cosJ = rotb[:, 0, :, :]
            sinJ = rotb[:, 1, :, :]
            cp(r8tab[:], small[:, 4 * o:4 * o + 4].unsqueeze(2).broadcast_to([128, 4, 256]), reads=["small"], writes=["r8tab"])
            p.op("dve", lambda h: h.memset(r8tab[:, :, 0:1], 0.0), writes=["r8tab"])
            mms = []
            for c in range(2):
                for s8 in range(8):
                    for q4 in range(4):
                        prs = slice(32 * q4, 32 * q4 + 32)
                        mms.append((psb[q4][:, 256 * c:256 * c + 256], BT8[prs, c, s8, :], uT[prs, o, s8:S:8],
                                    c == 0 and s8 == 0, s8 == 7, {"tile_position": (32 * q4, 0), "skip_group_check": True}))
            mm_group(mms, reads=["tabsB"] + [("uT", o, mg) for mg in range(4)], writes=[("ps", q4) for q4 in range(4)])
            br = ps_all[:, 0:4, 0:256]
            bi = ps_all[:, 0:4, 256:512]
            pk = [("ps", i) for i in range(4)]
            tt(t1[:], br, cosJ, ALU.mult, reads=pk + ["rotb"], writes=["t1"])
            tt(t2[:], bi, sinJ, ALU.mult, reads=pk + ["rotb"], writes=["t2"])
            tt(b1[:], t1[:], t2[:], ALU.add, reads=["t1", "t2"], writes=["b1"])
            tt(t1[:], bi, cosJ, ALU.mult, reads=pk + ["rotb"], writes=["t1"])
            tt(t2[:], br, sinJ, ALU.mult, reads=pk + ["rotb"], writes=["t2"])
            tt(b2[:], t1[:], t2[:], ALU.subtract, reads=["t1", "t2"], writes=["b2"])
            f2 = lambda t: t[:, :, :].rearrange("p q j -> p (q j)")
            p.op("dve", lambda h: h.tensor_tensor_scan(f2(t1), f2(r8tab), f2(b1), 0.0, ALU.mult, ALU.add),
                 reads=["b1", "r8tab"], writes=["t1"])
            p.op("dve", lambda h: h.tensor_tensor_scan(f2(t2), f2(r8tab), f2(b2), 0.0, ALU.mult, ALU.add),
                 reads=["b2", "r8tab"], writes=["t2"])
            tt(b1[:], t1[:], cosJ, ALU.mult, reads=["t1", "rotb"], writes=["b1"])
            tt(b2[:], t2[:], sinJ, ALU.mult, reads=["t2", "rotb"], writes=["b2"])
            tt(Sprev[:, :, 0, 1:257], b1[:], b2[:], ALU.subtract, reads=["b1", "b2"], writes=[spk])
            tt(b1[:], t1[:], sinJ, ALU.mult, reads=["t1", "rotb"], writes=["b1"])
            tt(b2[:], t2[:], cosJ, ALU.mult, reads=["t2", "rotb"], writes=["b2"])
            stt(Sprev[:, :, 1, 1:257], b1[:], -1.0, b2[:], ALU.mult, ALU.subtract, reads=["b1", "b2", spk], writes=[spk])

            return tk, KT, CTr, Sprev, spk

        def s5_back(o, tk, KT, CTr, Sprev, spk):
            for jb in range(2):
                bA = 4
                bB = 5
                yi = 0
                mms = []
                sg_ = {"skip_group_check": True}
                for s8 in range(8):
                    lt = uT[:, o, 1024 * jb + s8:1024 * jb + 1024:8]
                    if s8 <= 3:
                        mms.append((psb[bA][:, 128 * s8:512], lt, KT[:, 0:4 - s8, :], s8 == 0, False, sg_))
                    lo = max(0, s8 - 4)
                    mms.append((psb[bB][:, 128 * lo:512], lt, KT[:, max(4 - s8, 0):8 - s8, :], s8 == 0, False, sg_))
                for q4 in range(4):
                    for c in range(2):
                        lt = Sprev[:, q4, c, 128 * jb:128 * jb + 128]
                        last = (q4 == 3 and c == 1)
                        mms.append((psb[bA].rearrange("p (t c) -> p t c", t=4)[:, :, 32 * q4:32 * q4 + 32], lt,
                                    CTr[:, q4, c, 0:4, :], False, last, sg_))
                        mms.append((psb[bB].rearrange("p (t c) -> p t c", t=4)[:, :, 32 * q4:32 * q4 + 32], lt,
                                    CTr[:, q4, c, 4:8, :], False, last, sg_))
                mm_group(mms, reads=[tk, ("Sprev0", o % 2), spk, ("uT", o, 2 * jb), ("uT", o, 2 * jb + 1)],
                         writes=[("ps", bA), ("ps", bB)])
                act(yact[yi][:, 0:4, :], psb[bA].rearrange("p (t c) -> p t c", t=4), AF.Gelu_apprx_tanh,
                    reads=[("ps", bA)], writes=[("yact", yi)])
                act(yact[yi][:, 4:8, :], psb[bB].rearrange("p (t c) -> p t c", t=4), AF.Gelu_apprx_tanh,
                    reads=[("ps", bB)], writes=[("yact", yi)])
                pT = psb_bf(6 + jb)
                tr_group([(pT[:, 128 * t8:128 * t8 + 128], yact[yi][:, t8, :], ident_b[:]) for t8 in range(8)],
                         reads=[("yact", yi), "ident_b"], writes=[("ps", 6 + jb)])
                act(mixT[:, o, 1024 * jb:1024 * jb + 1024].rearrange("p (j t) -> p t j", t=8),
                    pT[:, :].rearrange("p (t j) -> p t j", t=8), AF.Copy, reads=[("ps", 6 + jb)],
                    writes=[("mixS", o, 2 * jb), ("mixS", o, 2 * jb + 1)])


        prev = None
        for o in range(4):
            cur = s5_front(o)
            if prev is not None:
                s5_back(o - 1, *prev)
            prev = cur
        s5_back(3, *prev)

        if pre_glu is not None:
            pre_glu()
        for m_ in range(4):
            tsl = slice(512 * m_, 512 * m_ + 512)
            for ct in range(4):
                mm_group([(psb[ct][:, :], gluw[:, oo, 128 * ct:128 * ct + 128], mixT[:, oo, tsl], oo == 0, oo == 3, {}) for oo in range(4)],
                         reads=["gluw"] + [("mixS", oo, m_) for oo in range(4)], writes=[("ps", ct)])
            for ct in range(4):
                si = ct % 2
                act(sg[si][:], psb[ct][:, :], AF.Sigmoid, reads=[("ps", ct), "small"], writes=[("sg", si)],
                    bias=small[:, 16 + ct:17 + ct])
                tt(mixT[:, ct, tsl], mixT[:, ct, tsl], sg[si][:], ALU.mult, reads=[("mixS", ct, m_), ("sg", si)], writes=[("mixS", ct, m_)])

    tab_tokens = {}


    for sq_ in range(nseq):
        dma("sp", ltmp[0:8, 0:128], din["c"][sq_].rearrange("(k p) -> k p", p=128), writes=["ltmp"], key="cc")
        tr_group([(psb[7][:, 0:8], ltmp[0:8, 0:128], ident_f[0:8, 0:8])], reads=["ltmp", "ident_f"], writes=[("ps", 7)])
        act(cond_col[:], psb[7][:, 0:8], AF.Silu, reads=[("ps", 7)], writes=["cond_col"])
        cp(cond_all[:, sq_, :], cond_col[:], reads=["cond_col"], writes=["cond_all"])

    mstate = {"n": 0}

    def _modw_load(l_, ch):
        i_ = mstate["n"] % 2
        mstate["n"] += 1
        dma("pool", modw_buf[i_][:], din["mod_w"][l_][:, ch * 128:(ch + 1) * 128].rearrange("(k p) n -> p k n", p=128),
            writes=[("modw", i_)], key="modw%d" % i_)
        return modw_buf[i_], ("modw", i_)

    def modb_prep(l_, par):
        mbt = ltmp[0:48, 0:128]
        dma("sp", mbt, din["mod_b"][l_].rearrange("(c p) -> c p", p=128), writes=["ltmp"], key="mb")
        tr_group([(psb[7][:, 0:48], mbt, ident_f[0:48, 0:48])], reads=["ltmp", "ident_f"], writes=[("ps", 7)])
        cp(modbB[par][:], psb[7][:, 0:48], reads=[("ps", 7)], writes=[("modb", par)])

    def col_compute(lb, sq_, ch, base):
        buf, bkey = lb
        mm_group([(psb[7][:, base + ch:base + ch + 1], buf[:, k, :], cond_all[:, sq_, k:k + 1], k == 0, k == 7, {"skip_group_check": True})
                  for k in range(8)], reads=["cond_all", bkey], writes=[("ps", 7)])

    def gate_init(l_, kind):
        dma("sp", gate_b[:, 0, :], din["mod_b"][l_][kind * D:(kind + 1) * D].unsqueeze(0).broadcast_to([128, D]), writes=["gate_b"], key="gb")

    def gate_compute(lb, ch):
        buf, bkey = lb
        c0 = (ch % 8) * 128
        mm_group([(psb[7][:, 104:232], cond_rep[:, k, :], buf[:, k, :], k == 0, k == 7, {"skip_group_check": True}) for k in range(8)],
                 reads=["cond_rep", bkey], writes=[("ps", 7)])
        stt(gate_b[:, 0, c0:c0 + 128], psb[7][:, 104:232], 1.0, gate_b[:, 0, c0:c0 + 128], ALU.add, ALU.add,
            reads=[("ps", 7), "gate_b"], writes=["gate_b"])

    def fin_cols(par, base, lo, hi, p1lo, p1hi):
        tt(modcolB[par][:, lo:hi], psb[7][:, base + lo:base + hi], modbB[par][:, lo:hi], ALU.add,
           reads=[("ps", 7), ("modb", par)], writes=[("modcol", par)])
        if p1hi > p1lo:
            ts(modcolB[par][:, p1lo:p1hi], modcolB[par][:, p1lo:p1hi], 1.0, None, ALU.add, None, reads=[("modcol", par)], writes=[("modcol", par)])

    def _q_prefetch(q):
        cnt = 0
        for t_ in q:
            if t_[0] in ("col", "gate"):
                if cnt >= 2:
                    break
                cnt += 1
                if "lb" not in t_[-1]:
                    t_[-1]["lb"] = _modw_load(t_[1], t_[3] if t_[0] == "col" else t_[2])

    def run_tasks(q, n):
        while n > 0 and q:
            _q_prefetch(q)
            t_ = q.pop(0)
            if t_[0] == "col":
                col_compute(t_[-1]["lb"], t_[2], t_[3], t_[4])
            elif t_[0] == "gate":
                gate_compute(t_[-1]["lb"], t_[2])
            else:
                t_[1]()
            _q_prefetch(q)
            n -= 1

    def run_until(q, marker_len):
        while len(q) > marker_len:
            run_tasks(q, 1)

    def T_col(l_, sq_, ch, base):
        return ("col", l_, sq_, ch, base, {})

    def T_gate(l_, ch):
        return ("gate", l_, ch, {})

    def T_fn(f):
        return ("fn", f)

    steps = [(sq_, l_) for sq_ in range(nseq) for l_ in range(nl)]
    nchunk = 0
    for sq in range(nseq):
        xv = din["x"][sq].rearrange("(tt p) d -> p tt d", p=128)
        for t4 in range(4):
            dma("sp", x_sb[:, 4 * t4:4 * t4 + 4, :], xv[:, 4 * t4:4 * t4 + 4, :],
                writes=[("x", 4 * t4 + i) for i in range(4)], key="xld%d" % t4)
        cp(cond_rep[:], cond_all[:, sq, :].unsqueeze(2).broadcast_to([128, 8, 128]), reads=["cond_all"], writes=["cond_rep"])

        for l in range(nl):
            lam_init = 0.8 - 0.6 * math.exp(-0.3 * l)
            si_ = steps.index((sq, l))
            par = si_ % 2
            modcol = modcolB[par]
            mck = ("modcol", par)
            if si_ == 0:
                modb_prep(l, par)
                q0 = [T_col(l, sq, ch_, 48) for ch_ in range(16)]
                run_until(q0, 0)
                fin_cols(par, 48, 0, 16, 8, 16)
                if do_ssm:
                    for l_ in range(nl):
                        ssm_build(l_)
            nxt = steps[si_ + 1] if si_ + 1 < len(steps) else None
            qA = [T_fn(lambda: gate_init(l, 2))] + [T_gate(l, ch_) for ch_ in range(16, 24)]
            qA += [T_col(l, sq, ch_, 48) for ch_ in range(24, 40)] + [T_fn(lambda: fin_cols(par, 48, 24, 40, 32, 40))]
            qB = [T_fn(lambda: gate_init(l, 5))] + [T_gate(l, ch_) for ch_ in range(40, 48)]
            if nxt is not None:
                nsq, nl_ = nxt
                qA += [T_fn(lambda: modb_prep(nl_, 1 - par))] + [T_col(nl_, nsq, ch_, 88) for ch_ in range(0, 8)]
                qA += [T_fn(lambda: fin_cols(1 - par, 88, 0, 8, 0, 0))]
                qB += [T_col(nl_, nsq, ch_, 88) for ch_ in range(8, 16)] + [T_fn(lambda: fin_cols(1 - par, 88, 8, 16, 8, 16))]
            nA_gate = len(qA) - 9
            nB_gate = len(qB) - 9

            def build_hT(hT, hkey, tok0, ntile, sc_col, sh_col):
                for t4 in range(ntile // 4):
                    for k in range(8):
                        bank = (t4 * 8 + k) % 2
                        tr_group([(psb[bank][:, 128 * i:128 * i + 128], x_sb[:, tok0 + 4 * t4 + i, 128 * k:128 * k + 128], ident_f[:])
                                  for i in range(4)],
                                 reads=[("x", tok0 + 4 * t4 + i) for i in range(4)] + ["ident_f"], writes=[("ps", bank)])
                        act(hT[:, k, 512 * t4:512 * t4 + 512], psb[bank][:, :], AF.Identity,
                            reads=[("ps", bank), mck], writes=[(hkey, k, t4)],
                            scale=modcol[:, sc_col + k:sc_col + k + 1], bias=modcol[:, sh_col + k:sh_col + k + 1])

            uT = carve(0, [128, 4, S], BF16)
            mixT = carve(16 * KB, [128, 8, S], BF16)
            qT = carve(48 * KB, [128, 4, S], BF16)
            kT = carve(64 * KB, [128, 4, S], BF16)
            vA = carve(80 * KB, [128, NT, 4, 130], BF16)
            hT = carve(16 * KB, [128, 8, 1024], BF16)
            wch = [carve(32 * KB + i * 2 * KB, [128, 8, 128], BF16) for i in range(2)] + \
                  [carve(102 * KB + 256 + i * 2 * KB, [128, 8, 128], BF16) for i in range(4)]
            rotc = carve(36 * KB, [128, 1024], F32)
            rots = carve(40 * KB, [128, 1024], F32)
            rtmp_i = carve(44 * KB, [128, 1024], F32).bitcast(I32)
            qraw = [carve(96 * KB + 256 + i * KB, [128, 512], BF16) for i in range(2)]
            rq1 = [carve(98 * KB + 256 + i * 2 * KB, [128, 512], F32) for i in range(2)]
            p.handoff(["actT", "hT2", "wd", "gu"] + S5_BUILD_BUFS, ["uT", "mixT", "qT", "kT", "vA", "vA1", "hT", "wch", "rotc", "rots", "rtmp",
                                                     "qraw", "rq1"])

            p.op("pool", lambda h: h.memset(vA[:, :, :, 128:130], 1.0), writes=[("vA1",)])

            NWB = len(wch)
            NPRE = 4
            wlist = [(hf, ct_) for hf in range(2) for ct_ in range(16)
                     if not ((ct_ < 4 and not do_ssm) or (ct_ >= 4 and not do_attn))]
            wissued = [0]

            def issue_w(upto):
                while wissued[0] < min(upto, len(wlist)):
                    i_ = wissued[0]
                    ct_ = wlist[i_][1]
                    dma("pool", wch[i_ % NWB][:], din["w_in"][l][:, ct_ * 128:(ct_ + 1) * 128].rearrange("(k p) n -> p k n", p=128),
                        writes=[("wch", i_ % NWB)], key="wch%d" % (i_ % NWB))
                    wissued[0] += 1
            issue_w(NPRE)
            wchn = 0
            pend_rot = []
            for half in range(2):
                tok0 = 8 * half
                build_hT(hT, "hT", tok0, 8, 8, 0)
                if do_attn:
                    posb = din["positions"][sq][1024 * half:1024 * half + 1024].unsqueeze(0).broadcast_to([128, 1024])
                    dma("pool", rotc[:], posb, writes=["rotc"], key="rotc")
                    ts(rots[:], rotc[:], ropef[:, 1:2], None, ALU.mult, None, reads=["rotc", "ropef"], writes=["rots"])
                    ts(rotc[:], rotc[:], ropef[:, 0:1], 0.25, ALU.mult, ALU.add, reads=["rotc", "ropef"], writes=["rotc"])
                    for nm, tb in (("rotc", rotc), ("rots", rots)):
                        cp(rtmp_i[:], tb[:], reads=[nm], writes=["rtmp"])
                        cp(ltmp[:], rtmp_i[:], reads=["rtmp"], writes=["ltmp"])
                        tt(tb[:], tb[:], ltmp[:], ALU.subtract, reads=[nm, "ltmp"], writes=[nm])
                        act(tb[:], tb[:], AF.Sin, reads=[nm], writes=[nm], scale=TWO_PI)
                for ct in range(16):
                    if (ct < 4 and not do_ssm) or (ct >= 4 and not do_attn):
                        continue
                    assert wlist[wchn] == (half, ct)
                    wb = wch[wchn % NWB]
                    wkey = ("wch", wchn % NWB)
                    issue_w(wchn + 1 + NPRE)
                    wchn += 1
                    if ct < 12:
                        for mm in range(2):
                            bank = 2 + (ct * 2 + mm) % 2
                            mm_group([(psb[bank][:, :], wb[:, k, :], hT[:, k, 512 * mm:512 * mm + 512], k == 0, k == 7, {})
                                      for k in range(8)],
                                     reads=[wkey] + [("hT", k, mm) for k in range(8)], writes=[("ps", bank)])
                            tsl = slice(1024 * half + 512 * mm, 1024 * half + 512 * mm + 512)
                            mg = 2 * half + mm
                            if ct < 4:
                                act(uT[:, ct, tsl], psb[bank][:, :], AF.Copy, reads=[("ps", bank)], writes=[("uT", ct, mg)])
                            else:
                                isq = ct < 8
                                dst = qT if isq else kT
                                dkey = ("qT" if isq else "kT", ct % 4, mg)
                                qi = (ct * 2 + mm) % 2
                                while pend_rot:
                                    pend_rot.pop(0)()
                                act(qraw[qi][:], psb[bank][:, :], AF.Copy, reads=[("ps", bank)], writes=[("qraw", qi)],
                                    scale=(0.125 if isq else 1.0))
                                rsl = slice(512 * mm, 512 * mm + 512)

                                def rot_stage(qi=qi, dst=dst, dkey=dkey, c4=ct % 4, tsl=tsl, rsl=rsl):
                                    rb = 4 + qi
                                    mm_group([(psb[rb][:, :], rmat_b[:], qraw[qi][:], True, True, {})],
                                             reads=["rmat_b", ("qraw", qi)], writes=[("ps", rb)])
                                    tt(rq1[qi][:], qraw[qi][:], rotc[:, rsl], ALU.mult, reads=[("qraw", qi), "rotc"], writes=[("rq1", qi)], eng="pool")
                                    tt(dst[:, c4, tsl], psb[rb][:, :], rots[:, rsl], ALU.mult, reads=[("ps", rb), "rots"], writes=[dkey])
                                    tt(dst[:, c4, tsl], dst[:, c4, tsl], rq1[qi][:], ALU.add, reads=[dkey, ("rq1", qi)], writes=[dkey])
                                pend_rot.append(rot_stage)
                    else:
                        while pend_rot:
                            pend_rot.pop(0)()
                        hd = ct - 12
                        for tl in range(8):
                            bank = 2 + tl % 2
                            mm_group([(psb[bank][:, 0:128], hT[:, k, 128 * tl:128 * tl + 128], wb[:, k, :], k == 0, k == 7, {})
                                      for k in range(8)],
                                     reads=[wkey] + [("hT", k, tl // 4) for k in range(8)], writes=[("ps", bank)])
                            act(vA[:, tok0 + tl, hd, 0:128], psb[bank][:, 0:128], AF.Copy, reads=[("ps", bank)],
                                writes=[("vA", tok0 + tl, hd)])

            if dbg and "dbg_q" in dbg_out and sq == 0 and l == 0 and do_attn:
                for nm, tb, kn in (("dbg_q", qT, "qT"), ("dbg_k", kT, "kT")):
                    cp(ltmp[:, 0:512], tb[:, 0, 0:512], reads=[(kn, 0, 0)], writes=["ltmp"])
                    fin.append(dma("sp", dbg_out[nm], ltmp[:, 0:512], reads=["ltmp"], key="dbgq"))
                cp(ltmp[:, 0:130], vA[:, 0, 0, :], reads=[("vA", 0, 0), ("vA1",)], writes=["ltmp"])
                fin.append(dma("sp", dbg_out["dbg_v"], ltmp[:, 0:130], reads=["ltmp"], key="dbgq"))
            p.handoff(["hT", "wch", "rotc", "rots", "rtmp", "mixT"], ["ebuf", "ep_o", "ep_o1", "ep_ob", "accS", "mixT", "mixS"])
            if do_attn:
                ebuf = [carve(16 * KB + i * KB, [128, 512], BF16) for i in range(4)]
                ep_o = carve(20 * KB, [128, 4, 128], F32)
                ep_o1 = carve(22 * KB, [128, 4, 128], F32)
                ep_ob = carve(24 * KB, [128, 4, 128], BF16)
                accS = carve(25 * KB, [128, 2, 4, 130], F32)
                for i, nm in enumerate(("lam_q1", "lam_k1", "lam_q2", "lam_k2")):
                    dma("sp", lamv[:, i, :], din[nm][l].unsqueeze(0).broadcast_to([128, 64]), writes=[("lamv", i)], key="lamv%d" % i)
                dma("sp", gain_b[:], din["subln_w"][l].unsqueeze(0).broadcast_to([128, 128]), writes=["gain_b"], key="gain")
                ts(gain_b[:], gain_b[:], float(1.0 - lam_init), None, ALU.mult, None, reads=["gain_b"], writes=["gain_b"])
                tt(lamv[:, 0, :], lamv[:, 0, :], lamv[:, 1, :], ALU.mult, reads=[("lamv", 0), ("lamv", 1)], writes=[("lamv", 0)])
                tt(lamv[:, 2, :], lamv[:, 2, :], lamv[:, 3, :], ALU.mult, reads=[("lamv", 2), ("lamv", 3)], writes=[("lamv", 2)])
                p.op("dve", lambda h: h.reduce_sum(lams[:, 0:1], lamv[:, 0, :], AX.X), reads=[("lamv", 0)], writes=[("lams", 0)])
                p.op("dve", lambda h: h.reduce_sum(lams[:, 1:2], lamv[:, 2, :], AX.X), reads=[("lamv", 2)], writes=[("lams", 1)])
                act(lams[:, 2:4], lams[:, 0:2], AF.Exp, reads=[("lams", 0), ("lams", 1)], writes=[("lams", 2)])
                tt(lams[:, 4:5], lams[:, 2:3], lams[:, 3:4], ALU.subtract, reads=[("lams", 2)], writes=[("lams", 4)])
                ts(lams[:, 5:6], lams[:, 4:5], -1.0, float(-lam_init), ALU.mult, ALU.add, reads=[("lams", 4)], writes=[("lams", 5)])

                qz = carve(96 * KB + 256, [128, 4, S], BF16)
                p.handoff(["qraw", "rq1", "wch"], ["qz"])
                p.op("pool", lambda h: h.memset(qz[0:64, :, :], 0.0), writes=[("qz", hd_) for hd_ in range(4)])
                for hd_ in range(4):
                    qk_ = [("qT", hd_, Q_) for Q_ in range(4)]
                    act(qz[64:128, hd_, :], qT[64:128, hd_, :], AF.Copy, reads=qk_ + [("qz", hd_)], writes=[("qz", hd_)])
                    p.op("pool", lambda h, hd_=hd_: h.memset(qT[64:128, hd_, :], 0.0), reads=qk_, writes=qk_)
                ecnt = 0
                pending_tail = []
                for hd in range(4):
                    for Q in range(4):
                        nkt = 4 * (Q + 1)
                        for m in range(2):
                            acc = [psb[2 + 2 * m], psb[3 + 2 * m]]
                            acck = [("ps", 2 + 2 * m), ("ps", 3 + 2 * m)]
                            prs = slice(64 * m, 64 * m + 64)

                            def accv(qs):
                                return acc[qs // 2][:, 256 * (qs % 2):256 * (qs % 2) + 130]

                            def issue_s(kt):
                                nonlocal ecnt
                                o = kt - 4 * Q
                                c0 = 128 * o if o > 0 else 0
                                sb_i = (0, 1, 6)[kt % 3]
                                mms_ = []
                                if o >= 0:
                                    mms_.append((psb[sb_i][:, c0:c0 + 128], ident_b[:], negtri_b[:], True, False, {"skip_group_check": True}))
                                qsrc = qT if m == 0 else qz
                                mms_.append((psb[sb_i][:, c0:512], kT[:, hd, 128 * kt:128 * kt + 128],
                                             qsrc[:, hd, 512 * Q + c0:512 * Q + 512], o < 0, True, {"skip_group_check": True}))
                                mm_group(mms_, reads=[("kT", hd, kt // 4), ("qT", hd, Q), ("qz", hd), "ident_b", "negtri_b"], writes=[("ps", sb_i)])
                                ei = ecnt % 4
                                ecnt += 1
                                act(ebuf[ei][:, c0:512], psb[sb_i][:, c0:512], AF.Exp, reads=[("ps", sb_i)], writes=[("ebuf", ei)])
                                return ei, o

                            def issue_pv(kt, ei, o):
                                qs0 = max(o, 0)
                                mms = []
                                for qs in range(qs0, 4):
                                    last_kt = 4 * Q + qs
                                    mms.append((accv(qs), ebuf[ei][:, 128 * qs:128 * qs + 128], vA[:, kt, hd, :],
                                                kt == 0 and qs % 2 == 0, kt == last_kt, {"skip_group_check": True}))
                                mm_group(mms, reads=[("ebuf", ei), ("vA", kt, hd), ("vA1",)], writes=acck)

                            run_tasks(qA, 1)
                            pendq = [issue_s(kt_) for kt_ in range(min(2, nkt))]
                            for kt in range(nkt):
                                if kt + 2 < nkt:
                                    pendq.append(issue_s(kt + 2))
                                issue_pv(kt, *pendq.pop(0))
                        while len(pending_tail) > 0:
                            pending_tail.pop(0)()
                        for bi_ in range(4):
                            cp(accS[:, bi_ // 2, 2 * (bi_ % 2):2 * (bi_ % 2) + 2, :],
                               psb[2 + bi_][:, :].rearrange("p (a b) -> p a b", a=2)[:, :, 0:130],
                               reads=[("ps", 2 + bi_)], writes=[("accS", bi_)])
                        z0 = accS[:, 0, :, 128]
                        z1 = accS[:, 1, :, 128]
                        ak = [("accS", i_) for i_ in range(4)]
                        p.op("dve", lambda h, z0=z0: h.reciprocal(ep_z[:, 0, :], z0), reads=ak, writes=[("ep_z", 0)])
                        p.op("dve", lambda h, z1=z1: h.reciprocal(ep_z[:, 1, :], z1), reads=ak, writes=[("ep_z", 1)])
                        ts(ep_z[:, 1, :], ep_z[:, 1, :], lams[:, 5:6], None, ALU.mult, None,
                           reads=[("ep_z", 1), ("lams", 5)], writes=[("ep_z", 1)])
                        tt(ep_o1[:], accS[:, 1, :, 0:128], ep_z[:, 1, :].unsqueeze(2).broadcast_to([128, 4, 128]), ALU.mult,
                           reads=ak + [("ep_z", 1)], writes=[("ep_o1", 0), ("ep_o1", 1)])
                        tt(ep_o[:], accS[:, 0, :, 0:128], ep_z[:, 0, :].unsqueeze(2).broadcast_to([128, 4, 128]), ALU.mult,
                           reads=ak + [("ep_z", 0)], writes=[("ep_o", 0), ("ep_o", 1)])
                        tt(ep_o[:], ep_o[:], ep_o1[:], ALU.add, reads=[("ep_o", 0), ("ep_o", 1), ("ep_o1", 0), ("ep_o1", 1)],
                           writes=[("ep_o", 0), ("ep_o", 1)])
                        tt(ep_o1[:], ep_o[:], ep_o[:], ALU.mult, reads=[("ep_o", 0), ("ep_o", 1)], writes=[("ep_o1", 0), ("ep_o1", 1)])
                        p.op("dve", lambda h: h.reduce_sum(ep_ss[:], ep_o1[:], AX.X), reads=[("ep_o1", 0), ("ep_o1", 1)], writes=["ep_ss"])
                        def ep_tail(hd=hd, Q=Q):
                            act(ep_ss[:], ep_ss[:], AF.Ln, reads=["ep_ss", "eps_col"], writes=["ep_ss"], scale=1.0 / 128.0, bias=eps_col[:, 0:1])
                            act(ep_ss[:], ep_ss[:], AF.Exp, reads=["ep_ss"], writes=["ep_ss"], scale=-0.5)
                            tt(ep_o[:], ep_o[:], ep_ss[:, :].unsqueeze(2).broadcast_to([128, 4, 128]), ALU.mult,
                               reads=[("ep_o", 0), ("ep_o", 1), "ep_ss"], writes=[("ep_o", 0), ("ep_o", 1)])
                            tt(ep_ob[:], ep_o[:], gain_b[:, :].unsqueeze(1).broadcast_to([128, 4, 128]), ALU.mult,
                               reads=[("ep_o", 0), ("ep_o", 1), "gain_b"], writes=["ep_ob"])
                            pT = psb[7][:, 256:512].bitcast(BF16)
                            tr_group([(pT[:, 128 * qs:128 * qs + 128], ep_ob[:, qs, :], ident_b[:]) for qs in range(4)],
                                     reads=["ep_ob", "ident_b"], writes=[("ps", 7)])
                            act(mixT[:, 4 + hd, 512 * Q:512 * Q + 512], pT[:, 0:512], AF.Copy, reads=[("ps", 7)], writes=[("mixT", 4 + hd, Q)])
                        pending_tail.append(ep_tail)
                        if (hd * 4 + Q) < 8:
                            run_tasks(qA, 1)
                while len(pending_tail) > 0:
                    pending_tail.pop(0)()
                run_until(qA, 0)
            else:
                p.op("pool", lambda h: h.memset(mixT[:, 4:8, :], 0.0), writes=[("mixT", 4 + hd, Q) for hd in range(4) for Q in range(4)])

            wo = carve(48 * KB, [128, 8, D], BF16)

            def load_wo():
                p.handoff(["qT", "tabsB", "tabsC", "Sprev", "Sprev0", "rotb"], ["wo"])
                dma("pool", wo[:], din["w_out"][l].rearrange("(k p) n -> p k n", p=128), writes=["wo"], key="wo")
            if do_ssm:
                ssm_main(l, uT, mixT, pre_glu=load_wo)
            else:
                p.handoff(["ebuf", "ep_o", "ep_o1", "ep_ob", "accS", "mixS"], ["mixS"])
                p.op("pool", lambda h: h.memset(mixT[:, 0:4, :], 0.0), writes=[("mixS", ct, Q) for ct in range(4) for Q in range(4)])

            if dbg and "dbg_mix" in dbg_out and sq == 0 and l == 0:
                for kk in range(8):
                    cp(ltmp[:, 0:512], mixT[:, kk, 0:512], reads=[("mixT" if kk >= 4 else "mixS", kk, 0)], writes=["ltmp"])
                    fin.append(dma("sp", dbg_out["dbg_mix"][kk * 128:(kk + 1) * 128, :], ltmp[:, 0:512], reads=["ltmp"], key="dbgm"))

            run_until(qA, nA_gate)
            if not do_ssm:
                load_wo()
            dma("sp", lngb[:, 0, :], din["ln1_g"][l].unsqueeze(0).broadcast_to([128, D]), writes=[("lngb", 0)], key="lng")
            dma("sp", lngb[:, 1, :], din["ln1_b"][l].unsqueeze(0).broadcast_to([128, D]), writes=[("lngb", 1)], key="lnb")

            def resid_ln(tt_i, banks, gi):
                xk = ("x", tt_i)
                xs = x_sb[:, tt_i, :]
                for hb in range(2):
                    tt(ltmp[:, 512 * hb:512 * hb + 512], psb[banks[hb]][:, :], gate_b[:, 0, 512 * hb:512 * hb + 512], ALU.mult,
                       reads=[("ps", banks[hb]), "gate_b"], writes=["ltmp"])
                stt(xs, xs, ALPHA, ltmp[:], ALU.mult, ALU.add, reads=[xk, "ltmp"], writes=[xk])
                for hb in range(2):
                    p.op("dve", lambda h, hb=hb: h.bn_stats(st_sb[:, 6 * hb:6 * hb + 6], x_sb[:, tt_i, 512 * hb:512 * hb + 512]),
                         reads=[xk], writes=[("st", hb)])
                p.op("dve", lambda h: h.bn_aggr(mv_sb[:], st_sb[:]), reads=[("st", 0), ("st", 1)], writes=["mv"])
                act(rs_sb[:, 0:1], mv_sb[:, 1:2], AF.Ln, reads=["mv", "eps_col"], writes=["rs"], bias=eps_col[:, 0:1], scale=1.0)
                act(rs_sb[:, 0:1], rs_sb[:, 0:1], AF.Exp, reads=["rs"], writes=["rs"], scale=-0.5)
                ts(rs_sb[:, 1:2], mv_sb[:, 0:1], -1.0, rs_sb[:, 0:1], ALU.mult, ALU.mult, reads=["mv", "rs"], writes=["rs1"])
                act(xs, xs, AF.Identity, reads=[xk, "rs", "rs1"], writes=[xk], scale=rs_sb[:, 0:1], bias=rs_sb[:, 1:2])
                tt(xs, xs, lngb[:, 0, :], ALU.mult, reads=[xk, ("lngb", 0)], writes=[xk], eng="pool")
                tt(xs, xs, lngb[:, 1, :], ALU.add, reads=[xk, ("lngb", 1)], writes=[xk], eng="pool")

            for tt_i in range(NT):
                banks = (0, 1) if tt_i % 2 == 0 else (6, 7)
                for hb in range(2):
                    mm_group([(psb[banks[hb]][:, :], mixT[:, k, 128 * tt_i:128 * tt_i + 128], wo[:, k, 512 * hb:512 * hb + 512],
                               k == 0, k == 7, {}) for k in range(8)],
                             reads=["wo"] + [("mixT" if k >= 4 else "mixS", k, tt_i // 4) for k in range(8)], writes=[("ps", banks[hb])])
                resid_ln(tt_i, banks, 0)

            if dbg and "dbg_x1" in dbg_out and sq == 0 and l == 0:
                fin.append(dma("sp", dbg_out["dbg_x1"], x_sb[:, 0, :], reads=[("x", 0)], key="dbgx1"))

            hT2 = carve(0, [128, 8, 1024], BF16)
            actT = carve(16 * KB, [128, NF, 1024], BF16)
            wd = carve(60 * KB, [128, NF, D], BF16)
            gu = [carve(104 * KB + i * 4 * KB, [128, 2, 8, 128], BF16) for i in range(2)]
            p.handoff(["uT", "mixT", "mixS", "qT", "kT", "vA", "ebuf", "ep_o", "ep_o1", "ep_ob", "accS", "wo", "qraw", "rq1", "vA1"] + S5_MAIN_BUFS + [
                       "hT", "wch", "rotc", "rots", "rtmp"],
                      ["hT2", "actT", "wd", "gu"])
            run_until(qA, 0)
            dma("sp", lngb[:, 0, :], din["ln2_g"][l].unsqueeze(0).broadcast_to([128, D]), writes=[("lngb", 0)], key="lng")
            dma("sp", lngb[:, 1, :], din["ln2_b"][l].unsqueeze(0).broadcast_to([128, D]), writes=[("lngb", 1)], key="lnb")
            gun = 0

            def issue_gu(i_):
                if i_ >= 2 * NF:
                    return
                gb_ = gu[i_ % 2]
                fs_ = slice(128 * (i_ % NF), 128 * (i_ % NF) + 128)
                p.op("pool", lambda h, inc, gb_=gb_, fs_=fs_, l=l: (
                    inc(h.dma_start(out=gb_[:, 0, :, :], in_=din["ffn_w_gate"][l][:, fs_].rearrange("(k p) n -> p k n", p=128))),
                    inc(h.dma_start(out=gb_[:, 1, :, :], in_=din["ffn_w_up"][l][:, fs_].rearrange("(k p) n -> p k n", p=128)))),
                    writes=[("gu", i_ % 2)], dma="gu%d" % (i_ % 2), dma_n=2)
            issue_gu(0)
            issue_gu(1)
            wd_issued = [False]
            for half in range(2):
                tok0 = 8 * half
                build_hT(hT2, "hT2", tok0, 8, 32, 24)
                for f in range(NF):
                    gb = gu[gun % 2]
                    gkey = ("gu", gun % 2)
                    gun += 1
                    for mm in range(2):
                        bg = (f * 2 + mm) % 2
                        bu = 2 + bg
                        mm_group([(psb[bg][:, :], gb[:, 0, k, :], hT2[:, k, 512 * mm:512 * mm + 512], k == 0, k == 7, {}) for k in range(8)],
                                 reads=[gkey] + [("hT2", k, mm) for k in range(8)], writes=[("ps", bg)])
                        mm_group([(psb[bu][:, :], gb[:, 1, k, :], hT2[:, k, 512 * mm:512 * mm + 512], k == 0, k == 7, {}) for k in range(8)],
                                 reads=[gkey] + [("hT2", k, mm) for k in range(8)], writes=[("ps", bu)])
                        asl = actT[:, f, 512 * mm:512 * mm + 512]
                        act(stmp[bg][:], psb[bg][:, :], AF.Silu, reads=[("ps", bg)], writes=[("stmp", bg)])
                        tt(asl, stmp[bg][:], psb[bu][:, :], ALU.mult, reads=[("stmp", bg), ("ps", bu)], writes=[("actT", f, mm)])
                    issue_gu(gun + 1)
                    if not wd_issued[0] and f >= 2:
                        wd_issued[0] = True
                        dma("pool", wd[:], din["ffn_w_down"][l].rearrange("(f p) n -> p f n", p=128), writes=["wd"], key="wd")
                    run_tasks(qB, 1)
                if half == 0:
                    run_until(qB, nB_gate)
                for tl in range(8):
                    tt_i = tok0 + tl
                    banks = (4, 5) if tl % 2 == 0 else (6, 7)
                    for hb in range(2):
                        mm_group([(psb[banks[hb]][:, :], actT[:, f, 128 * tl:128 * tl + 128], wd[:, f, 512 * hb:512 * hb + 512],
                                   f == 0, f == NF - 1, {}) for f in range(NF)],
                                 reads=["wd"] + [("actT", f, tl // 4) for f in range(NF)], writes=[("ps", banks[hb])])
                    resid_ln(tt_i, banks, 1)
            run_until(qB, 0)

        yv = y_out[sq].rearrange("(tt p) d -> p tt d", p=128)
        for t4 in range(4):
            fin.append(dma("sp", yv[:, 4 * t4:4 * t4 + 4, :], x_sb[:, 4 * t4:4 * t4 + 4, :],
                           reads=[("x", 4 * t4 + i) for i in range(4)], key="yst%d" % t4))

    p.emit(fin)
    p.close()
    return nc


def ssm_block(p, din, l, env):
    raise NotImplementedError


_NC_CACHE = {}


def kernel(**inputs):
    n = 8
    if "nc" not in _NC_CACHE:
        _NC_CACHE["nc"] = build_program()
    nc = _NC_CACHE["nc"]
    hc = host_consts()
    in_maps = []
    for c in range(n):
        m = {"x": np.ascontiguousarray(inputs["x"][2 * c:2 * c + 2], dtype=np.float32),
             "c": np.ascontiguousarray(inputs["c"][2 * c:2 * c + 2], dtype=np.float32),
             "positions": np.ascontiguousarray(inputs["positions"][2 * c:2 * c + 2], dtype=np.int32)}
        for name, _ in PARAM_SPECS:
            m[name] = np.ascontiguousarray(inputs[name], dtype=np.float32)
        for k_, v_ in hc.items():
            m[k_] = v_
        in_maps.append(m)
    res = run_bass_kernel_spmd(nc, in_maps, core_ids=list(range(n)))
    out = np.concatenate([np.asarray(r["y"]) for r in res.results], axis=0)
    return out.astype(np.float32)
```

```python
import contextlib
import math
import numpy as np
import ml_dtypes
import concourse.bass as bass
import concourse.mybir as mybir
from concourse.bass_utils import run_bass_kernel_spmd

F32 = mybir.dt.float32
BF16 = mybir.dt.bfloat16
I32 = mybir.dt.int32
ALU = mybir.AluOpType
AF = mybir.ActivationFunctionType
AX = mybir.AxisListType

D = 1024
S = 2048
NT = 16
DFF = 2816
NF = 22
DEPTH = 4
ALPHA = float((2 * DEPTH) ** 0.25)
TWO_PI = float(2 * np.pi)
ENG = ["pe", "act", "dve", "pool", "sp"]
EPOCH = 8192


class Prog:
    def __init__(self, nc):
        self.nc = nc
        self.stack = contextlib.ExitStack()
        self.ops = {e: [] for e in ENG}
        self.count = {e: 0 for e in ENG}
        self.seen = {e: {} for e in ENG}
        self.lastw = {}
        self.readers = {}
        self.dma_val = {}
        self.sems = {}
        self.pending = {}
        self.bufkeys = {}

    def sb(self, name, shape, dt):
        return self.stack.enter_context(self.nc.sbuf_tensor("s_" + name, list(shape), dt))

    def ps(self, name, shape, dt):
        return self.stack.enter_context(self.nc.psum_tensor(name, list(shape), dt))

    def sem(self, name):
        if name not in self.sems:
            self.sems[name] = self.stack.enter_context(self.nc.semaphore(name))
        return self.sems[name]

    def _touch(self, k):
        b = k[0] if isinstance(k, tuple) else k
        ks = self.bufkeys.setdefault(b, set())
        if k not in ks:
            ks.add(k)
            pend = self.pending.get(b)
            if pend and k not in self.lastw and k not in self.readers:
                self.readers[k] = [(s, v) for s, v in pend.items()]

    def handoff(self, old_bufs, new_bufs):
        t = {}
        for b in old_bufs:
            for k in self.bufkeys.get(b, ()):
                toks = list(self.readers.get(k, ()))
                if self.lastw.get(k) is not None:
                    toks.append(self.lastw[k])
                for s, v in toks:
                    if t.get(s, 0) < v:
                        t[s] = v
            for s, v in self.pending.get(b, {}).items():
                if t.get(s, 0) < v:
                    t[s] = v
        for b in new_bufs:
            for k in self.bufkeys.get(b, ()):
                self.lastw.pop(k, None)
                self.readers.pop(k, None)
            self.bufkeys[b] = set()
            self.pending[b] = dict(t)

    def _need(self, eng, tok, waits):
        if tok is None:
            return
        stream, val = tok
        if self.seen[eng].get(stream, 0) >= val:
            return
        self.seen[eng][stream] = val
        waits.append(tok)

    def op(self, eng, fn, reads=(), writes=(), dma=None, dma_n=1):
        waits = []
        is_dma = dma is not None
        for k in reads:
            self._touch(k)
        for k in writes:
            self._touch(k)
        for k in reads:
            tok = self.lastw.get(k)
            if tok is not None:
                if (not is_dma) and tok[0] == ("eng", eng) and eng == "pe":
                    pass
                else:
                    self._need(eng, tok, waits)
            else:
                for tok2 in self.readers.get(k, ()):
                    pass
        for k in writes:
            tok = self.lastw.get(k)
            if tok is not None:
                if (not is_dma) and tok[0] == ("eng", eng):
                    pass
                else:
                    self._need(eng, tok, waits)
            for tok in self.readers.get(k, ()):
                if (not is_dma) and tok[0] == ("eng", eng):
                    continue
                self._need(eng, tok, waits)
        if is_dma:
            prev = self.dma_val.get(dma, 0)
            if prev:
                self._need(eng, (("dma", dma), prev), waits)
            val = prev + 16 * dma_n
            self.dma_val[dma] = val
            mytok = (("dma", dma), val)
            self.ops[eng].append((waits, fn, ("dma", dma)))
        else:
            self.count[eng] += 1
            mytok = (("eng", eng), self.count[eng])
            self.ops[eng].append((waits, fn, ("eng", self.count[eng])))
        for k in writes:
            self.lastw[k] = mytok
            self.readers[k] = []
        for k in reads:
            self.readers.setdefault(k, []).append(mytok)
        return mytok

    def _semfor(self, stream, val):
        if stream[0] == "dma":
            return self.sem("d_" + str(stream[1])), val
        e = stream[1]
        ep = (val - 1) // EPOCH
        return self.sem("e_%s_%d" % (e, ep)), (val - 1) % EPOCH + 1

    def emit(self, final_tokens):
        nc = self.nc
        for e in ENG:
            for waits, fn, inc in self.ops[e]:
                for stream, val in waits:
                    self._semfor(stream, val)
                if inc[0] == "dma":
                    self.sem("d_" + str(inc[1]))
                else:
                    self._semfor(("eng", e), inc[1])
        for stream, val in final_tokens:
            self._semfor(stream, val)
        prog = self
        with nc.Block() as block:
            def replay(e, handle):
                for waits, fn, inc in prog.ops[e]:
                    for stream, val in waits:
                        s, v = prog._semfor(stream, val)
                        handle.wait_ge(s, v)
                    if inc[0] == "dma":
                        s = prog.sem("d_" + str(inc[1]))
                        fn(handle, lambda ins, s=s: ins.then_inc(s, 16))
                    else:
                        ins = fn(handle)
                        s, v = prog._semfor(("eng", e), inc[1])
                        ins.then_inc(s, 1)
                if e == "sp":
                    for stream, val in final_tokens:
                        s, v = prog._semfor(stream, val)
                        handle.wait_ge(s, v)

            @block.tensor
            def _(h):
                replay("pe", h)

            @block.scalar
            def _(h):
                replay("act", h)

            @block.vector
            def _(h):
                replay("dve", h)

            @block.gpsimd
            def _(h):
                replay("pool", h)

            @block.sync
            def _(h):
                replay("sp", h)

    def close(self):
        self.stack.close()


PARAM_SPECS = [
    ("mod_w", [DEPTH, D, 6 * D]), ("mod_b", [DEPTH, 6 * D]), ("w_in", [DEPTH, D, 2048]),
    ("ssm_a_re", [DEPTH, 32, 64]), ("ssm_a_im", [DEPTH, 32, 64]), ("ssm_log_step", [DEPTH, 32]),
    ("ssm_b_re", [DEPTH, 32, 64, 16]), ("ssm_b_im", [DEPTH, 32, 64, 16]),
    ("ssm_c_re", [DEPTH, 32, 16, 64]), ("ssm_c_im", [DEPTH, 32, 16, 64]), ("ssm_d", [DEPTH, 32, 16]),
    ("glu_w", [DEPTH, 512, 512]), ("glu_b", [DEPTH, 512]),
    ("lam_q1", [DEPTH, 64]), ("lam_k1", [DEPTH, 64]), ("lam_q2", [DEPTH, 64]), ("lam_k2", [DEPTH, 64]),
    ("subln_w", [DEPTH, 128]), ("w_out", [DEPTH, D, D]), ("ln1_g", [DEPTH, D]), ("ln1_b", [DEPTH, D]),
    ("ffn_w_gate", [DEPTH, D, DFF]), ("ffn_w_up", [DEPTH, D, DFF]), ("ffn_w_down", [DEPTH, DFF, D]),
    ("ln2_g", [DEPTH, D]), ("ln2_b", [DEPTH, D]),
]


def host_consts():
    c = {}
    c["ident"] = np.eye(128, dtype=np.float32)
    R = np.zeros((128, 128), np.float32)
    for i in range(128):
        d = i % 64
        if d < 8:
            R[i + 8, i] = 1.0
        elif d < 16:
            R[i - 8, i] = 1.0
    c["rmat"] = R
    c["negtri"] = np.where(np.arange(128)[:, None] <= np.arange(128)[None, :], 0.0, -30000.0).astype(np.float32)
    c["bmask"] = (np.arange(128)[:, None] // 16 == np.arange(128)[None, :] // 16).astype(np.float32)
    c["jidx"] = np.arange(256, dtype=np.float32)[None, :]
    freqs = 500000.0 ** (-np.arange(0, 16, 2, dtype=np.float64) / 16.0)
    fc = np.zeros((128, 2), np.float32)
    for i in range(128):
        d = i % 64
        if d < 8:
            fc[i, 0] = freqs[d] / (2 * np.pi)
            fc[i, 1] = -freqs[d] / (2 * np.pi)
        elif d < 16:
            fc[i, 0] = freqs[d - 8] / (2 * np.pi)
            fc[i, 1] = freqs[d - 8] / (2 * np.pi)
    c["ropef"] = fc
    return c


def build_program(nseq=2, nl=DEPTH, dbg=None, do_attn=True, do_ssm=True):
    nc = bass.Bass("TRN2", target_bir_lowering=False)
    din = {}
    din["x"] = nc.dram_tensor("x", [nseq, S, D], F32, kind="ExternalInput").ap()
    din["c"] = nc.dram_tensor("c", [nseq, D], F32, kind="ExternalInput").ap()
    din["positions"] = nc.dram_tensor("positions", [nseq, S], I32, kind="ExternalInput").ap()
    for name, shp in PARAM_SPECS:
        din[name] = nc.dram_tensor(name, shp, F32, kind="ExternalInput").ap()
    hc = host_consts()
    for name, arr in hc.items():
        din[name] = nc.dram_tensor(name, list(arr.shape), F32, kind="ExternalInput").ap()
    y_out = nc.dram_tensor("y", [nseq, S, D], F32, kind="ExternalOutput").ap()
    dbg_out = {}
    if dbg:
        for name, shp in dbg.items():
            dbg_out[name] = nc.dram_tensor(name, shp, F32, kind="ExternalOutput").ap()

    tab_bt8 = nc.dram_tensor("tab_bt8", [DEPTH, 4, 128, 2 * 8 * 128], BF16).ap()
    tab_ctr = nc.dram_tensor("tab_ctr", [DEPTH, 4, 128, 4 * 2 * 8 * 32], BF16).ap()
    tab_kt = nc.dram_tensor("tab_kt", [DEPTH, 4, 128, 8 * 128], BF16).ap()
    tab_rot = nc.dram_tensor("tab_rot", [DEPTH, 4, 128, 2 * 4 * 256], F32).ap()
    tab_small = nc.dram_tensor("tab_small", [DEPTH, 128, 32], F32).ap()

    p = Prog(nc)
    fin = []

    def dma(q, out, in_, reads=(), writes=(), key=None):
        return p.op(q, lambda h, inc: inc(h.dma_start(out=out, in_=in_)), reads=reads, writes=writes, dma=key)

    def act(out, in_, func, reads, writes, **kw):
        return p.op("act", lambda h: h.activation(out, in_, func, **kw), reads=reads, writes=writes)

    def tt(out, a, b, op, reads, writes, eng="dve"):
        return p.op(eng, lambda h: h.tensor_tensor(out, a, b, op), reads=reads, writes=writes)

    def ts(out, a, s1, s2, op0, op1, reads, writes, eng="dve"):
        if s2 is None:
            return p.op(eng, lambda h: h.tensor_scalar(out, a, s1, None, op0), reads=reads, writes=writes)
        return p.op(eng, lambda h: h.tensor_scalar(out, a, s1, s2, op0, op1), reads=reads, writes=writes)

    def stt(out, a, sc, b, op0, op1, reads, writes, eng="dve"):
        return p.op(eng, lambda h: h.scalar_tensor_tensor(out, a, sc, b, op0, op1), reads=reads, writes=writes)

    def cp(out, a, reads, writes, eng="dve"):
        return p.op(eng, lambda h: h.tensor_copy(out, a), reads=reads, writes=writes)

    def mm_group(mms, reads, writes):
        def fn(h):
            ins = None
            for (o, l, r, st, sp_, kw) in mms:
                ins = h.matmul(o, l, r, start=st, stop=sp_, **kw)
            return ins
        return p.op("pe", fn, reads=reads, writes=writes)

    def tr_group(trs, reads, writes):
        def fn(h):
            ins = None
            for (o, i, idn) in trs:
                ins = h.transpose(o, i, idn)
            return ins
        return p.op("pe", fn, reads=reads, writes=writes)

    x_sb = p.sb("x_sb", [128, NT, D], F32)
    ARENA_F32 = 112 * 256 + 64
    arena = p.sb("arena", [128, ARENA_F32], F32)

    def carve(off_bytes, shape, dt):
        size = 2 if dt == BF16 else 4
        n = int(np.prod(shape[1:]))
        assert off_bytes % 4 == 0 and (n * size) % 4 == 0
        assert off_bytes + n * size <= ARENA_F32 * 4, (off_bytes, shape)
        v = arena[:, off_bytes // 4:(off_bytes + n * size) // 4]
        if dt != F32:
            v = v.bitcast(dt)
        if len(shape) == 2:
            return v
        names = " ".join("a%d" % i for i in range(len(shape) - 1))
        kw = {"a%d" % i: shape[i + 1] for i in range(len(shape) - 1)}
        return v.rearrange("p (%s) -> p %s" % (names, names), **kw)

    KB = 1024
    ident_f = p.sb("ident_f", [128, 128], F32)
    ident_b = p.sb("ident_b", [128, 128], BF16)
    rmat_b = p.sb("rmat_b", [128, 128], BF16)
    negtri_b = p.sb("negtri_b", [128, 128], BF16)
    bmask_f = p.sb("bmask_f", [128, 128], F32)
    jidx_f = p.sb("jidx_f", [128, 256], F32)
    ropef = p.sb("ropef", [128, 2], F32)
    modcolB = [p.sb("modcol%d" % i, [128, 48], F32) for i in range(2)]
    modbB = [p.sb("modb_col%d" % i, [128, 48], F32) for i in range(2)]
    cond_all = p.sb("cond_all", [128, 2, 8], BF16)
    gate_b = p.sb("gate_b", [128, 1, D], F32)
    lngb = p.sb("lngb", [128, 2, D], F32)
    modw_buf = [p.sb("modw%d" % i, [128, 8, 128], BF16) for i in range(2)]
    cond_col = p.sb("cond_col", [128, 8], F32)
    cond_rep = p.sb("cond_rep", [128, 8, 128], BF16)
    gain_b = p.sb("gain_b", [128, 128], F32)
    lamv = p.sb("lamv", [128, 4, 64], F32)
    lams = p.sb("lams", [128, 8], F32)
    st_sb = p.sb("st_sb", [128, 12], F32)
    eps_col = p.sb("eps_col", [128, 1], F32)
    one_col = p.sb("one_col", [128, 1], F32)
    mv_sb = p.sb("mv_sb", [128, 2], F32)
    rs_sb = p.sb("rs_sb", [128, 2], F32)
    ltmp = p.sb("ltmp", [128, D], F32)
    stmp = [p.sb("stmp%d" % i, [128, 512], F32) for i in range(2)]
    ep_z = p.sb("ep_z", [128, 2, 4], F32)
    ep_ss = p.sb("ep_ss", [128, 4], F32)

    ps_all = p.ps("ps_all", [128, 8, 512], F32)
    psb = [ps_all[:, i, :] for i in range(8)]

    def psb_bf(i):
        return psb[i].bitcast(BF16)

    dma("sp", ident_f[:], din["ident"], writes=["ident_f"], key="c0")
    dma("pool", ident_b[:], din["ident"], writes=["ident_b"], key="c1")
    dma("pool", rmat_b[:], din["rmat"], writes=["rmat_b"], key="c2")
    dma("pool", negtri_b[:], din["negtri"], writes=["negtri_b"], key="c3")
    dma("sp", bmask_f[:], din["bmask"], writes=["bmask_f"], key="c4")
    dma("sp", jidx_f[:], din["jidx"].broadcast_to([128, 256]), writes=["jidx_f"], key="c5")
    dma("sp", ropef[:], din["ropef"], writes=["ropef"], key="c6")
    p.op("dve", lambda h: h.memset(eps_col[:], 1e-5), writes=["eps_col"])
    p.op("dve", lambda h: h.memset(one_col[:], 1.0), writes=["one_col"])

    def frac_round(dst, src, itmp, ftmp, reads, writes, eng="dve"):
        cp(itmp, src, reads=reads, writes=["_frac_i"], eng=eng)
        cp(ftmp, itmp, reads=["_frac_i"], writes=["_frac_f"], eng=eng)
        tt(dst, src, ftmp, ALU.subtract, reads=list(reads) + ["_frac_f"], writes=writes, eng=eng)


    S5_MAIN_BUFS = ["tabsB", "tabsC", "rotb", "small", "r8tab", "t1", "t2", "b1", "b2", "Sprev", "Sprev0", "yact", "gluw", "sg"]
    S5_BUILD_BUFS = ["spf", "Ppow", "aq", "Bs", "BX", "Cld", "CX", "BXb", "BPall", "CPall", "BT8all", "CTrall", "KTall",
                     "bt1", "bt2", "bitmp", "bftmp", "bcs", "smallb"]

    def ssm_build(l):
        p.handoff(S5_BUILD_BUFS, S5_BUILD_BUFS)
        spf = carve(0, [128, 40, 16], F32)
        Pre = carve(2560, [128, 9, 16], F32)
        Pim = carve(3328, [128, 9, 16], F32)
        aq = carve(4 * KB, [128, 3, 128], F32)
        Bs = carve(5632, [128, 2, 16, 16], F32)
        BX = carve(7680, [128, 2, 16, 32], F32)
        Cld = carve(11776, [128, 2, 4, 128], F32)
        CX = carve(15872, [128, 2, 16, 32], F32)
        BXb = carve(19968, [128, 2, 16, 32], BF16)
        smallb = carve(22016, [128, 32], F32)
        BPall = carve(22 * KB, [128, 8, 2, 512], BF16)
        CPall = carve(38 * KB, [128, 9, 2, 512], BF16)
        BT8all = carve(56 * KB, [128, 4, 2, 8, 128], BF16)
        CTrall = carve(72 * KB, [128, 16, 2, 8, 32], BF16)
        KTall = carve(88 * KB, [128, 4, 8, 128], BF16)
        bt1 = carve(96 * KB, [128, 1024], F32)
        bt2 = carve(100 * KB, [128, 1024], F32)
        bitmp = carve(104 * KB, [128, 1024], F32).bitcast(I32)
        bftmp = carve(108 * KB, [128, 1024], F32)
        bcs = carve(22 * KB, [128, 2, 4, 256], F32)

        def sl(i):
            return spf[:, i, :]
        L_RE, L_IM, DL, LR, TRN, MAG, COSA, SINA, LBR, LBI, DEN, RDEN, NRE, KR, KI, TA, TB, R8, FF, FHI, FLO, TI, TF, TC, GLUB, DCOL = range(26)
        sk = lambda i: ("spf", i)

        dma("sp", aq[0:16, 0, :], din["ssm_a_re"][l].rearrange("(q g) p -> q (g p)", g=2), writes=[("aq", 0)], key="aq0")
        dma("sp", aq[0:16, 1, :], din["ssm_a_im"][l].rearrange("(q g) p -> q (g p)", g=2), writes=[("aq", 1)], key="aq1")
        dma("sp", ltmp[0:16, 0:2], din["ssm_log_step"][l].rearrange("(q g) -> q g", g=2), writes=["ltmp"], key="aq2")
        act(ltmp[0:16, 0:2], ltmp[0:16, 0:2], AF.Exp, reads=["ltmp"], writes=["ltmp"])
        cp(aq[0:16, 2, :].rearrange("q (g p) -> q g p", g=2), ltmp[0:16, 0:2].unsqueeze(2).broadcast_to([16, 2, 64]),
           reads=["ltmp"], writes=[("aq", 2)])
        tr_group([(psb[7][:, 16 * i:16 * i + 16], aq[0:16, i, :], ident_f[0:16, 0:16]) for i in range(3)],
                 reads=[("aq", 0), ("aq", 1), ("aq", 2), "ident_f"], writes=[("ps", 7)])
        ts(sl(L_RE), psb[7][:, 0:16], -1e-4, None, ALU.min, None, reads=[("ps", 7)], writes=[sk(L_RE)])
        cp(sl(L_IM), psb[7][:, 16:32], reads=[("ps", 7)], writes=[sk(L_IM)])
        cp(sl(DL), psb[7][:, 32:48], reads=[("ps", 7)], writes=[sk(DL)])
        dma("sp", ltmp[0:4, 0:128], din["ssm_d"][l].rearrange("g c -> (g c)").rearrange("(o p) -> o p", p=128), writes=["ltmp"], key="aq3")
        dma("sp", ltmp[4:8, 0:128], din["glu_b"][l].rearrange("(o p) -> o p", p=128), writes=["ltmp"], key="aq3")
        tr_group([(psb[7][:, 64:72], ltmp[0:8, 0:128], ident_f[0:8, 0:8])], reads=["ltmp", "ident_f"], writes=[("ps", 7)])
        cp(spf[:, DCOL, 0:4], psb[7][:, 64:68], reads=[("ps", 7)], writes=[sk(DCOL)])
        cp(smallb[:, 16:20], psb[7][:, 68:72], reads=[("ps", 7)], writes=["smallb"])
        for c, nm in enumerate(("ssm_b_re", "ssm_b_im")):
            dma("sp", Bs[:, c, :, :], din[nm][l].rearrange("(q g) p c -> (g p) q c", g=2), writes=[("Bs", c)], key="bs%d" % c)
        for c, nm in enumerate(("ssm_c_re", "ssm_c_im")):
            csrc = din[nm][l].rearrange("(o q g) c s -> q c o g s", o=4, q=4, g=2)
            for q4_ in range(4):
                for o_ in range(4):
                    dma("sp", Cld[16 * q4_:16 * q4_ + 16, c, o_, :].rearrange("p (g s) -> p g s", g=2), csrc[q4_][:, o_, :, :],
                        writes=[("Cld", c)], key="cld%d_%d" % (q4_, o_))

        tt(sl(LR), sl(L_RE), sl(DL), ALU.mult, reads=[sk(L_RE), sk(DL)], writes=[sk(LR)])
        stt(sl(TRN), sl(L_IM), 1.0 / TWO_PI, sl(DL), ALU.mult, ALU.mult, reads=[sk(L_IM), sk(DL)], writes=[sk(TRN)])
        act(sl(MAG), sl(LR), AF.Exp, reads=[sk(LR)], writes=[sk(MAG)])
        act(smallb[:, 0:16], sl(LR), AF.Exp, reads=[sk(LR)], writes=["smallb"], scale=8.0)
        tiv = spf[:, TI, :].bitcast(I32)

        def frac16(dst, src):
            cp(tiv, sl(src), reads=[sk(src)], writes=[sk(TI)])
            cp(sl(TF), tiv, reads=[sk(TI)], writes=[sk(TF)])
            tt(sl(dst), sl(src), sl(TF), ALU.subtract, reads=[sk(src), sk(TF)], writes=[sk(dst)])
        frac16(TA, TRN)
        act(sl(SINA), sl(TA), AF.Sin, reads=[sk(TA)], writes=[sk(SINA)], scale=TWO_PI)
        ts(sl(TC), sl(TRN), 0.25, None, ALU.add, None, reads=[sk(TRN)], writes=[sk(TC)])
        frac16(TA, TC)
        act(sl(COSA), sl(TA), AF.Sin, reads=[sk(TA)], writes=[sk(COSA)], scale=TWO_PI)
        tt(sl(LBR), sl(MAG), sl(COSA), ALU.mult, reads=[sk(MAG), sk(COSA)], writes=[sk(LBR)])
        tt(sl(LBI), sl(MAG), sl(SINA), ALU.mult, reads=[sk(MAG), sk(SINA)], writes=[sk(LBI)])
        tt(sl(TA), sl(L_RE), sl(L_RE), ALU.mult, reads=[sk(L_RE)], writes=[sk(TA)])
        tt(sl(TB), sl(L_IM), sl(L_IM), ALU.mult, reads=[sk(L_IM)], writes=[sk(TB)])
        tt(sl(DEN), sl(TA), sl(TB), ALU.add, reads=[sk(TA), sk(TB)], writes=[sk(DEN)])
        p.op("dve", lambda h: h.reciprocal(spf[:, RDEN, :], spf[:, DEN, :]), reads=[sk(DEN)], writes=[sk(RDEN)])
        ts(sl(NRE), sl(LBR), -1.0, None, ALU.add, None, reads=[sk(LBR)], writes=[sk(NRE)])
        tt(sl(TA), sl(NRE), sl(L_RE), ALU.mult, reads=[sk(NRE), sk(L_RE)], writes=[sk(TA)])
        tt(sl(TB), sl(LBI), sl(L_IM), ALU.mult, reads=[sk(LBI), sk(L_IM)], writes=[sk(TB)])
        tt(sl(TA), sl(TA), sl(TB), ALU.add, reads=[sk(TA), sk(TB)], writes=[sk(TA)])
        tt(sl(KR), sl(TA), sl(RDEN), ALU.mult, reads=[sk(TA), sk(RDEN)], writes=[sk(KR)])
        tt(sl(TA), sl(LBI), sl(L_RE), ALU.mult, reads=[sk(LBI), sk(L_RE)], writes=[sk(TA)])
        tt(sl(TB), sl(NRE), sl(L_IM), ALU.mult, reads=[sk(NRE), sk(L_IM)], writes=[sk(TB)])
        tt(sl(TA), sl(TA), sl(TB), ALU.subtract, reads=[sk(TA), sk(TB)], writes=[sk(TA)])
        tt(sl(KI), sl(TA), sl(RDEN), ALU.mult, reads=[sk(TA), sk(RDEN)], writes=[sk(KI)])
        p.op("dve", lambda h: h.memset(Pre[:, 0, :], 1.0), writes=[("Ppow", 0)])
        p.op("dve", lambda h: h.memset(Pim[:, 0, :], 0.0), writes=[("Ppow", 0)])
        cp(Pre[:, 1, :], sl(LBR), reads=[sk(LBR)], writes=[("Ppow", 1)])
        cp(Pim[:, 1, :], sl(LBI), reads=[sk(LBI)], writes=[("Ppow", 1)])
        for n in range(1, 8):
            rk = [("Ppow", n), sk(LBR), sk(LBI)]
            tt(sl(TA), Pre[:, n, :], sl(LBR), ALU.mult, reads=rk, writes=[sk(TA)])
            tt(sl(TB), Pim[:, n, :], sl(LBI), ALU.mult, reads=rk, writes=[sk(TB)])
            tt(Pre[:, n + 1, :], sl(TA), sl(TB), ALU.subtract, reads=[sk(TA), sk(TB)], writes=[("Ppow", n + 1)])
            tt(sl(TA), Pre[:, n, :], sl(LBI), ALU.mult, reads=rk, writes=[sk(TA)])
            tt(sl(TB), Pim[:, n, :], sl(LBR), ALU.mult, reads=rk, writes=[sk(TB)])
            tt(Pim[:, n + 1, :], sl(TA), sl(TB), ALU.add, reads=[sk(TA), sk(TB), ("Ppow", n + 1)], writes=[("Ppow", n + 1)])
        ts(sl(FF), sl(TRN), 8.0, None, ALU.mult, None, reads=[sk(TRN)], writes=[sk(FF)])
        ts(sl(TA), sl(FF), 1024.0, None, ALU.mult, None, reads=[sk(FF)], writes=[sk(TA)])
        cp(tiv, sl(TA), reads=[sk(TA)], writes=[sk(TI)])
        cp(sl(TF), tiv, reads=[sk(TI)], writes=[sk(TF)])
        ts(sl(FHI), sl(TF), 1.0 / 1024.0, None, ALU.mult, None, reads=[sk(TF)], writes=[sk(FHI)])
        tt(sl(FLO), sl(FF), sl(FHI), ALU.subtract, reads=[sk(FF), sk(FHI)], writes=[sk(FLO)])
        dma("sp", tab_small[l], smallb[:], reads=["smallb"], writes=[("tabd", l, "small")], key="tsmall")

        p.op("dve", lambda h: h.memset(BX[:], 0.0), writes=[("BX", 0), ("BX", 1)])
        p.op("dve", lambda h: h.memset(CX[:], 0.0), writes=[("CX", 0), ("CX", 1)])
        for hp in range(2):
            rs = slice(64 * hp, 64 * hp + 64)
            cs = slice(16 * hp, 16 * hp + 16)
            krb = spf[rs, KR, :].unsqueeze(2).broadcast_to([64, 16, 16])
            kib = spf[rs, KI, :].unsqueeze(2).broadcast_to([64, 16, 16])
            u1 = bt1[rs, 0:256].rearrange("p (q c) -> p q c", q=16)
            u2 = bt2[rs, 0:256].rearrange("p (q c) -> p q c", q=16)
            rk = [("Bs", 0), ("Bs", 1), sk(KR), sk(KI)]
            tt(u1, Bs[rs, 0, :, :], krb, ALU.mult, reads=rk, writes=["bt1"])
            tt(u2, Bs[rs, 1, :, :], kib, ALU.mult, reads=rk, writes=["bt2"])
            tt(BX[rs, 0, :, cs], u1, u2, ALU.subtract, reads=["bt1", "bt2", ("BX", 0)], writes=[("BX", 0)])
            tt(u1, Bs[rs, 1, :, :], krb, ALU.mult, reads=rk, writes=["bt1"])
            tt(u2, Bs[rs, 0, :, :], kib, ALU.mult, reads=rk, writes=["bt2"])
            tt(BX[rs, 1, :, cs], u1, u2, ALU.add, reads=["bt1", "bt2", ("BX", 1)], writes=[("BX", 1)])
        cp(BXb[:, 0, :, :], BX[:, 0, :, :], reads=[("BX", 0)], writes=["BXb"])
        ts(BXb[:, 1, :, :], BX[:, 1, :, :], -1.0, None, ALU.mult, None, reads=[("BX", 1), "BXb"], writes=["BXb"])
        for c in range(2):
            tr_group([(psb[6][:, 64 * o:64 * o + 64], Cld[0:64, c, o, :], ident_f[0:64, 0:64]) for o in range(4)],
                     reads=[("Cld", c), "ident_f"], writes=[("ps", 6)])
            for hp in range(2):
                rs = slice(64 * hp, 64 * hp + 64)
                cs = slice(16 * hp, 16 * hp + 16)
                cp(CX[rs, c, :, cs], psb[6][rs, 0:256].rearrange("p (q c) -> p q c", q=16), reads=[("ps", 6), ("CX", c)], writes=[("CX", c)])

        def pb(tab, n):
            return tab[:, n, :].unsqueeze(2).broadcast_to([128, 16, 32])

        def v16(ap2):
            return ap2.rearrange("p (q c) -> p q c", q=16)
        u1 = bt1[:, 0:512].rearrange("p (q c) -> p q c", q=16)
        u2 = bt2[:, 0:512].rearrange("p (q c) -> p q c", q=16)
        bxk = [("BX", 0), ("BX", 1)]
        cxk = [("CX", 0), ("CX", 1)]
        x1 = bftmp[:, 0:512].rearrange("p (q c) -> p q c", q=16)
        x2 = bftmp[:, 512:1024].rearrange("p (q c) -> p q c", q=16)
        for n in range(8):
            rk = bxk + [("Ppow", n)]
            tt(u1, BX[:, 0, :, :], pb(Pre, n), ALU.mult, reads=rk, writes=["bt1"])
            tt(u2, BX[:, 1, :, :], pb(Pim, n), ALU.mult, reads=rk, writes=["bt2"])
            tt(v16(BPall[:, n, 0, :]), u1, u2, ALU.subtract, reads=["bt1", "bt2"], writes=[("BPall", n)])
            tt(x1, BX[:, 1, :, :], pb(Pre, n), ALU.mult, reads=rk, writes=["bftmp"], eng="pool")
            tt(x2, BX[:, 0, :, :], pb(Pim, n), ALU.mult, reads=rk, writes=["bftmp"], eng="pool")
            tt(v16(BPall[:, n, 1, :]), x1, x2, ALU.add, reads=["bftmp", ("BPall", n)], writes=[("BPall", n)], eng="pool")
        for o in range(4):
            for c in range(2):
                bank = 6 + (o * 2 + c) % 2
                pT = psb_bf(bank)
                tr_group([(pT[:, 128 * s8:128 * s8 + 128], BPall[:, 7 - s8, c, 128 * o:128 * o + 128], ident_b[:]) for s8 in range(8)],
                         reads=[("BPall", n) for n in range(8)] + ["ident_b"], writes=[("ps", bank)])
                cp(BT8all[:, o, c, :, :], pT[:, :].rearrange("p (s m) -> p s m", s=8), reads=[("ps", bank)], writes=[("BT8all", o)])
            dma("sp", tab_bt8[l, o], BT8all[:, o, :, :, :].rearrange("p c s m -> p (c s m)"), reads=[("BT8all", o)],
                writes=[("tabd", l, "bt8", o)], key="tbt8")
        w1 = bftmp[:, 0:512].rearrange("p (q c) -> p q c", q=16)
        w2 = bftmp[:, 512:1024].rearrange("p (q c) -> p q c", q=16)
        for tau in range(9):
            rk = cxk + [("Ppow", tau)]
            tt(w1, CX[:, 0, :, :], pb(Pre, tau), ALU.mult, reads=rk, writes=["bftmp"], eng="pool")
            tt(w2, CX[:, 1, :, :], pb(Pim, tau), ALU.mult, reads=rk, writes=["bftmp"], eng="pool")
            tt(v16(CPall[:, tau, 0, :]), w1, w2, ALU.subtract, reads=["bftmp"], writes=[("CPall", tau)], eng="pool")
            tt(w1, CX[:, 1, :, :], pb(Pre, tau), ALU.mult, reads=rk, writes=["bftmp"], eng="pool")
            tt(w2, CX[:, 0, :, :], pb(Pim, tau), ALU.mult, reads=rk, writes=["bftmp"], eng="pool")
            tt(v16(CPall[:, tau, 1, :]), w1, w2, ALU.add, reads=["bftmp", ("CPall", tau)], writes=[("CPall", tau)], eng="pool")
        for c in range(2):
            cp(CTrall[:, :, c, :, :].rearrange("p q t c -> p t q c"),
               CPall[:, 1:9, c, :].rearrange("p t (q c) -> p t q c", q=16),
               reads=[("CPall", t) for t in range(1, 9)], writes=[("CTrall", c)])
        for o in range(4):
            dma("sp", tab_ctr[l, o], CTrall[:, 4 * o:4 * o + 4, :, :, :].rearrange("p q c t e -> p (q c t e)"),
                reads=[("CTrall", 0), ("CTrall", 1)], writes=[("tabd", l, "ctr", o)], key="tctr")
        for o in range(4):
            for hb in range(2):
                mms = []
                for tl in range(4):
                    tau = 4 * hb + tl
                    mms.append((psb[4 + hb][:, 128 * tl:128 * tl + 128], BXb[:, 0, 4 * o:4 * o + 4, :].rearrange("p q c -> p (q c)"),
                                CPall[:, tau, 0, 128 * o:128 * o + 128], tl == 0, False, {"skip_group_check": True}))
                    mms.append((psb[4 + hb][:, 128 * tl:128 * tl + 128], BXb[:, 1, 4 * o:4 * o + 4, :].rearrange("p q c -> p (q c)"),
                                CPall[:, tau, 1, 128 * o:128 * o + 128], False, True, {"skip_group_check": True}))
                mm_group(mms, reads=["BXb"] + [("CPall", 4 * hb + tl) for tl in range(4)], writes=[("ps", 4 + hb)])
            tt(bt1[:, 0:128], psb[4][:, 0:128], bmask_f[:], ALU.mult, reads=[("ps", 4), "bmask_f"], writes=["bt1"])
            stt(KTall[:, o, 0, :], ident_f[:], spf[:, DCOL, o:o + 1], bt1[:, 0:128], ALU.mult, ALU.add,
                reads=["ident_f", sk(DCOL), "bt1"], writes=[("KTall", o)])
            tt(KTall[:, o, 1:4, :], psb[4][:, 128:512].rearrange("p (t c) -> p t c", t=3),
               bmask_f[:, :].unsqueeze(1).broadcast_to([128, 3, 128]), ALU.mult, reads=[("ps", 4), "bmask_f", ("KTall", o)], writes=[("KTall", o)])
            tt(KTall[:, o, 4:8, :], psb[5][:, :].rearrange("p (t c) -> p t c", t=4),
               bmask_f[:, :].unsqueeze(1).broadcast_to([128, 4, 128]), ALU.mult, reads=[("ps", 5), "bmask_f", ("KTall", o)], writes=[("KTall", o)])
            dma("sp", tab_kt[l, o], KTall[:, o, :, :].rearrange("p t c -> p (t c)"), reads=[("KTall", o)],
                writes=[("tabd", l, "kt", o)], key="tkt")

        p.handoff(["BPall"], ["bcs"])
        jb4 = jidx_f[:, :].unsqueeze(1).broadcast_to([128, 4, 256])
        v3 = lambda t: t[:, :].rearrange("p (q j) -> p q j", q=4)
        for o in range(4):
            fhi = spf[:, FHI, 4 * o:4 * o + 4].unsqueeze(2).broadcast_to([128, 4, 256])
            flo = spf[:, FLO, 4 * o:4 * o + 4].unsqueeze(2).broadcast_to([128, 4, 256])
            tt(v3(bt1), jb4, fhi, ALU.mult, reads=["jidx_f", sk(FHI)], writes=["bt1"])
            cp(bitmp[:], bt1[:], reads=["bt1"], writes=["bitmp"])
            cp(bftmp[:], bitmp[:], reads=["bitmp"], writes=["bftmp"])
            tt(bt1[:], bt1[:], bftmp[:], ALU.subtract, reads=["bt1", "bftmp"], writes=["bt1"])
            tt(v3(bt2), jb4, flo, ALU.mult, reads=["jidx_f", sk(FLO)], writes=["bt2"])
            tt(bt1[:], bt1[:], bt2[:], ALU.add, reads=["bt1", "bt2"], writes=["bt1"])
            cp(bitmp[:], bt1[:], reads=["bt1"], writes=["bitmp"])
            cp(bftmp[:], bitmp[:], reads=["bitmp"], writes=["bftmp"])
            tt(bt2[:], bt1[:], bftmp[:], ALU.subtract, reads=["bt1", "bftmp"], writes=["bt2"])
            act(bcs[:, 1, :, :], v3(bt2), AF.Sin, reads=["bt2"], writes=[("bcs", 1)], scale=TWO_PI)
            act(bcs[:, 0, :, :], v3(bt2), AF.Sin, reads=["bt2"], writes=[("bcs", 0)], scale=float(np.pi))
            act(bcs[:, 0, :, :], bcs[:, 0, :, :], AF.Square, reads=[("bcs", 0)], writes=[("bcs", 0)])
            act(bcs[:, 0, :, :], bcs[:, 0, :, :], AF.Identity, reads=[("bcs", 0), "one_col"], writes=[("bcs", 0)], scale=-2.0, bias=one_col[:, 0:1])
            dma("sp", tab_rot[l, o], bcs[:, :, :, :].rearrange("p c q j -> p (c q j)"), reads=[("bcs", 0), ("bcs", 1)],
                writes=[("tabd", l, "rot", o)], key="trot")

    def ssm_main(l, uT, mixT, pre_glu=None):
        SB = 48 * KB
        p.handoff(["qT", "kT", "vA", "vA1", "qraw", "rq1", "ebuf", "ep_o", "ep_o1", "ep_ob", "accS", "mixS"], S5_MAIN_BUFS + ["mixS"])
        tabsB = carve(SB, [128, 2048], BF16)
        tabsC = [carve(SB + 4096 + i * 6144, [128, 3072], BF16) for i in range(2)]
        SprevB = [carve(SB + 49408, [128, 4, 2, 258], BF16), carve(SB + 16384, [128, 4, 2, 258], BF16)]
        rotb = carve(SB + 20736, [128, 2, 4, 256], F32)
        r8tab = carve(SB + 28928, [128, 4, 256], F32)
        t1 = carve(SB + 33024, [128, 4, 256], F32)
        t2 = carve(SB + 37120, [128, 4, 256], F32)
        b1 = carve(SB + 41216, [128, 4, 256], F32)
        b2 = carve(SB + 45312, [128, 4, 256], F32)
        yact = [carve(SB + 53536, [128, 8, 128], BF16)]
        gluw = carve(SB + 55584, [128, 4, 512], BF16)
        sg = [carve(SB + 59680 + i * 2048, [128, 512], F32) for i in range(2)]
        small = carve(SB + 63776, [128, 32], F32)

        dma("sp", small[:], tab_small[l], reads=[("tabd", l, "small")], writes=["small"], key="lsmall")
        dma("pool", gluw[:], din["glu_w"][l].rearrange("(o p) n -> p o n", p=128), writes=["gluw"], key="gluw")
        for i_ in range(2):
            p.op("dve", lambda h, i_=i_: h.memset(SprevB[i_][:, :, :, 0:1], 0.0), writes=[("Sprev0", i_)])

        def s5_front(o):
            tb = tabsC[o % 2]
            tk = ("tabsC", o % 2)
            Sprev = SprevB[o % 2]
            spk = ("Sprev", o % 2)
            BT8 = tabsB[:, :].rearrange("p (c s m) -> p c s m", c=2, s=8)
            CTr = tb[:, 0:2048].rearrange("p (q c t e) -> p q c t e", q=4, c=2, t=8)
            KT = tb[:, 2048:3072].rearrange("p (t c) -> p t c", t=8)
            dma("sp", tabsB[:, :], tab_bt8[l, o], reads=[("tabd", l, "bt8", o)], writes=["tabsB"], key="ltabsB")
            p.op("sp", lambda h, inc, tb=tb, o=o, l=l: (
                inc(h.dma_start(out=tb[:, 0:2048], in_=tab_ctr[l, o])),
                inc(h.dma_start(out=tb[:, 2048:3072], in_=tab_kt[l, o]))),
                reads=[("tabd", l, "ctr", o), ("tabd", l, "kt", o)], writes=[tk], dma="ltabs%d" % (o % 2), dma_n=2)
            dma("sp", rotb[:, :, :, :].rearrange("p c q j -> p (c q j)"), tab_rot[l, o], reads=[("tabd", l, "rot", o)], writes=["rotb"], key="lrot")
            cosJ = rotb[:, 0, :, :]
            sinJ = rotb[:, 1, :, :]
            cp(r8tab[:], small[:, 4 * o:4 * o + 4].unsqueeze(2).broadcast_to([128, 4, 256]), reads=["small"], writes=["r8tab"])
            p.op("dve", lambda h: h.memset(r8tab[:, :, 0:1], 0.0), writes=["r8tab"])
            mms = []
            for c in range(2):
                for s8 in range(8):
                    for q4 in range(4):
                        prs = slice(32 * q4, 32 * q4 + 32)
                        mms.append((psb[q4][:, 256 * c:256 * c + 256], BT8[prs, c, s8, :], uT[prs, o, s8:S:8],
                                    c == 0 and s8 == 0, s8 == 7, {"tile_position": (32 * q4, 0), "skip_group_check": True}))
            mm_group(mms, reads=["tabsB"] + [("uT", o, mg) for mg in range(4)], writes=[("ps", q4) for q4 in range(4)])
            br = ps_all[:, 0:4, 0:256]
            bi = ps_all[:, 0:4, 256:512]
            pk = [("ps", i) for i in range(4)]
            tt(t1[:], br, cosJ, ALU.mult, reads=pk + ["rotb"], writes=["t1"])
            tt(t2[:], bi, sinJ, ALU.mult, reads=pk + ["rotb"], writes=["t2"])
            tt(b1[:], t1[:], t2[:], ALU.add, reads=["t1", "t2"], writes=["b1"])
            tt(t1[:], bi, cosJ, ALU.mult, reads=pk + ["rotb"], writes=["t1"])
            tt(t2[:], br, sinJ, ALU.mult, reads=pk + ["rotb"], writes=["t2"])
            tt(b2[:], t1[:], t2[:], ALU.subtract, reads=["t1", "t2"], writes=["b2"])
            f2 = lambda t: t[:, :, :].rearrange("p q j -> p (q j)")
            p.op("dve", lambda h: h.tensor_tensor_scan(f2(t1), f2(r8tab), f2(b1), 0.0, ALU.mult, ALU.add),
                 reads=["b1", "r8tab"], writes=["t1"])
            p.op("dve", lambda h: h.tensor_tensor_scan(f2(t2), f2(r8tab), f2(b2), 0.0, ALU.mult, ALU.add),
                 reads=["b2", "r8tab"], writes=["t2"])
            tt(b1[:], t1[:], cosJ, ALU.mult, reads=["t1", "rotb"], writes=["b1"])
            tt(b2[:], t2[:], sinJ, ALU.mult, reads=["t2", "rotb"], writes=["b2"])
            tt(Sprev[:, :, 0, 1:257], b1[:], b2[:], ALU.subtract, reads=["b1", "b2"], writes=[spk])
            tt(b1[:], t1[:], sinJ, ALU.mult, reads=["t1", "rotb"], writes=["b1"])
            tt(b2[:], t2[:], cosJ, ALU.mult, reads=["t2", "rotb"], writes=["b2"])
            stt(Sprev[:, :, 1, 1:257], b1[:], -1.0, b2[:], ALU.mult, ALU.subtract, reads=["b1", "b2", spk], writes=[spk])

            return tk, KT, CTr, Sprev, spk

        def s5_back(o, tk, KT, CTr, Sprev, spk):
            for jb in range(2):
                bA = 4
                bB = 5
                yi = 0
                mms = []
                sg_ = {"skip_group_check": True}
                for s8 in range(8):
                    lt = uT[:, o, 1024 * jb + s8:1024 * jb + 1024:8]
                    if s8 <= 3:
                        mms.append((psb[bA][:, 128 * s8:512], lt, KT[:, 0:4 - s8, :], s8 == 0, False, sg_))
                    lo = max(0, s8 - 4)
                    mms.append((psb[bB][:, 128 * lo:512], lt, KT[:, max(4 - s8, 0):8 - s8, :], s8 == 0, False, sg_))
                for q4 in range(4):
                    for c in range(2):
                        lt = Sprev[:, q4, c, 128 * jb:128 * jb + 128]
                        last = (q4 == 3 and c == 1)
                        mms.append((psb[bA].rearrange("p (t c) -> p t c", t=4)[:, :, 32 * q4:32 * q4 + 32], lt,
                                    CTr[:, q4, c, 0:4, :], False, last, sg_))
                        mms.append((psb[bB].rearrange("p (t c) -> p t c", t=4)[:, :, 32 * q4:32 * q4 + 32], lt,
                                    CTr[:, q4, c, 4:8, :], False, last, sg_))
                mm_group(mms, reads=[tk, ("Sprev0", o % 2), spk, ("uT", o, 2 * jb), ("uT", o, 2 * jb + 1)],
                         writes=[("ps", bA), ("ps", bB)])
                act(yact[yi][:, 0:4, :], psb[bA].rearrange("p (t c) -> p t c", t=4), AF.Gelu_apprx_tanh,
                    reads=[("ps", bA)], writes=[("yact", yi)])
                act(yact[yi][:, 4:8, :], psb[bB].rearrange("p (t c) -> p t c", t=4), AF.Gelu_apprx_tanh,
                    reads=[("ps", bB)], writes=[("yact", yi)])
                pT = psb_bf(6 + jb)
                tr_group([(pT[:, 128 * t8:128 * t8 + 128], yact[yi][:, t8, :], ident_b[:]) for t8 in range(8)],
                         reads=[("yact", yi), "ident_b"], writes=[("ps", 6 + jb)])
                act(mixT[:, o, 1024 * jb:1024 * jb + 1024].rearrange("p (j t) -> p t j", t=8),
                    pT[:, :].rearrange("p (t j) -> p t j", t=8), AF.Copy, reads=[("ps", 6 + jb)],
                    writes=[("mixS", o, 2 * jb), ("mixS", o, 2 * jb + 1)])


        prev = None
        for o in range(4):
            cur = s5_front(o)
            if prev is not None:
                s5_back(o - 1, *prev)
            prev = cur
        s5_back(3, *prev)

        if pre_glu is not None:
            pre_glu()
        for m_ in range(4):
            tsl = slice(512 * m_, 512 * m_ + 512)
            for ct in range(4):
                mm_group([(psb[ct][:, :], gluw[:, oo, 128 * ct:128 * ct + 128], mixT[:, oo, tsl], oo == 0, oo == 3, {}) for oo in range(4)],
                         reads=["gluw"] + [("mixS", oo, m_) for oo in range(4)], writes=[("ps", ct)])
            for ct in range(4):
                si = ct % 2
                act(sg[si][:], psb[ct][:, :], AF.Sigmoid, reads=[("ps", ct), "small"], writes=[("sg", si)],
                    bias=small[:, 16 + ct:17 + ct])
                tt(mixT[:, ct, tsl], mixT[:, ct, tsl], sg[si][:], ALU.mult, reads=[("mixS", ct, m_), ("sg", si)], writes=[("mixS", ct, m_)])

    tab_tokens = {}


    for sq_ in range(nseq):
        dma("sp", ltmp[0:8, 0:128], din["c"][sq_].rearrange("(k p) -> k p", p=128), writes=["ltmp"], key="cc")
        tr_group([(psb[7][:, 0:8], ltmp[0:8, 0:128], ident_f[0:8, 0:8])], reads=["ltmp", "ident_f"], writes=[("ps", 7)])
        act(cond_col[:], psb[7][:, 0:8], AF.Silu, reads=[("ps", 7)], writes=["cond_col"])
        cp(cond_all[:, sq_, :], cond_col[:], reads=["cond_col"], writes=["cond_all"])

    mstate = {"n": 0}

    def _modw_load(l_, ch):
        i_ = mstate["n"] % 2
        mstate["n"] += 1
        dma("pool", modw_buf[i_][:], din["mod_w"][l_][:, ch * 128:(ch + 1) * 128].rearrange("(k p) n -> p k n", p=128),
            writes=[("modw", i_)], key="modw%d" % i_)
        return modw_buf[i_], ("modw", i_)

    def modb_prep(l_, par):
        mbt = ltmp[0:48, 0:128]
        dma("sp", mbt, din["mod_b"][l_].rearrange("(c p) -> c p", p=128), writes=["ltmp"], key="mb")
        tr_group([(psb[7][:, 0:48], mbt, ident_f[0:48, 0:48])], reads=["ltmp", "ident_f"], writes=[("ps", 7)])
        cp(modbB[par][:], psb[7][:, 0:48], reads=[("ps", 7)], writes=[("modb", par)])

    def col_compute(lb, sq_, ch, base):
        buf, bkey = lb
        mm_group([(psb[7][:, base + ch:base + ch + 1], buf[:, k, :], cond_all[:, sq_, k:k + 1], k == 0, k == 7, {"skip_group_check": True})
                  for k in range(8)], reads=["cond_all", bkey], writes=[("ps", 7)])

    def gate_init(l_, kind):
        dma("sp", gate_b[:, 0, :], din["mod_b"][l_][kind * D:(kind + 1) * D].unsqueeze(0).broadcast_to([128, D]), writes=["gate_b"], key="gb")

    def gate_compute(lb, ch):
        buf, bkey = lb
        c0 = (ch % 8) * 128
        mm_group([(psb[7][:, 104:232], cond_rep[:, k, :], buf[:, k, :], k == 0, k == 7, {"skip_group_check": True}) for k in range(8)],
                 reads=["cond_rep", bkey], writes=[("ps", 7)])
        stt(gate_b[:, 0, c0:c0 + 128], psb[7][:, 104:232], 1.0, gate_b[:, 0, c0:c0 + 128], ALU.add, ALU.add,
            reads=[("ps", 7), "gate_b"], writes=["gate_b"])

    def fin_cols(par, base, lo, hi, p1lo, p1hi):
        tt(modcolB[par][:, lo:hi], psb[7][:, base + lo:base + hi], modbB[par][:, lo:hi], ALU.add,
           reads=[("ps", 7), ("modb", par)], writes=[("modcol", par)])
        if p1hi > p1lo:
            ts(modcolB[par][:, p1lo:p1hi], modcolB[par][:, p1lo:p1hi], 1.0, None, ALU.add, None, reads=[("modcol", par)], writes=[("modcol", par)])

    def _q_prefetch(q):
        cnt = 0
        for t_ in q:
            if t_[0] in ("col", "gate"):
                if cnt >= 2:
                    break
                cnt += 1
                if "lb" not in t_[-1]:
                    t_[-1]["lb"] = _modw_load(t_[1], t_[3] if t_[0] == "col" else t_[2])

    def run_tasks(q, n):
        while n > 0 and q:
            _q_prefetch(q)
            t_ = q.pop(0)
            if t_[0] == "col":
                col_compute(t_[-1]["lb"], t_[2], t_[3], t_[4])
            elif t_[0] == "gate":
                gate_compute(t_[-1]["lb"], t_[2])
            else:
                t_[1]()
            _q_prefetch(q)
            n -= 1

    def run_until(q, marker_len):
        while len(q) > marker_len:
            run_tasks(q, 1)

    def T_col(l_, sq_, ch, base):
        return ("col", l_, sq_, ch, base, {})

    def T_gate(l_, ch):
        return ("gate", l_, ch, {})

    def T_fn(f):
        return ("fn", f)

    steps = [(sq_, l_) for sq_ in range(nseq) for l_ in range(nl)]
    nchunk = 0
    for sq in range(nseq):
        xv = din["x"][sq].rearrange("(tt p) d -> p tt d", p=128)
        for t4 in range(4):
            dma("sp", x_sb[:, 4 * t4:4 * t4 + 4, :], xv[:, 4 * t4:4 * t4 + 4, :],
                writes=[("x", 4 * t4 + i) for i in range(4)], key="xld%d" % t4)
        cp(cond_rep[:], cond_all[:, sq, :].unsqueeze(2).broadcast_to([128, 8, 128]), reads=["cond_all"], writes=["cond_rep"])

        for l in range(nl):
            lam_init = 0.8 - 0.6 * math.exp(-0.3 * l)
            si_ = steps.index((sq, l))
            par = si_ % 2
            modcol = modcolB[par]
            mck = ("modcol", par)
            if si_ == 0:
                modb_prep(l, par)
                q0 = [T_col(l, sq, ch_, 48) for ch_ in range(16)]
                run_until(q0, 0)
                fin_cols(par, 48, 0, 16, 8, 16)
                if do_ssm:
                    for l_ in range(nl):
                        ssm_build(l_)
            nxt = steps[si_ + 1] if si_ + 1 < len(steps) else None
            qA = [T_fn(lambda: gate_init(l, 2))] + [T_gate(l, ch_) for ch_ in range(16, 24)]
            qA += [T_col(l, sq, ch_, 48) for ch_ in range(24, 40)] + [T_fn(lambda: fin_cols(par, 48, 24, 40, 32, 40))]
            qB = [T_fn(lambda: gate_init(l, 5))] + [T_gate(l, ch_) for ch_ in range(40, 48)]
            if nxt is not None:
                nsq, nl_ = nxt
                qA += [T_fn(lambda: modb_prep(nl_, 1 - par))] + [T_col(nl_, nsq, ch_, 88) for ch_ in range(0, 8)]
                qA += [T_fn(lambda: fin_cols(1 - par, 88, 0, 8, 0, 0))]
                qB += [T_col(nl_, nsq, ch_, 88) for ch_ in range(8, 16)] + [T_fn(lambda: fin_cols(1 - par, 88, 8, 16, 8, 16))]
            nA_gate = len(qA) - 9
            nB_gate = len(qB) - 9

            def build_hT(hT, hkey, tok0, ntile, sc_col, sh_col):
                for t4 in range(ntile // 4):
                    for k in range(8):
                        bank = (t4 * 8 + k) % 2
                        tr_group([(psb[bank][:, 128 * i:128 * i + 128], x_sb[:, tok0 + 4 * t4 + i, 128 * k:128 * k + 128], ident_f[:])
                                  for i in range(4)],
                                 reads=[("x", tok0 + 4 * t4 + i) for i in range(4)] + ["ident_f"], writes=[("ps", bank)])
                        act(hT[:, k, 512 * t4:512 * t4 + 512], psb[bank][:, :], AF.Identity,
                            reads=[("ps", bank), mck], writes=[(hkey, k, t4)],
                            scale=modcol[:, sc_col + k:sc_col + k + 1], bias=modcol[:, sh_col + k:sh_col + k + 1])

            uT = carve(0, [128, 4, S], BF16)
            mixT = carve(16 * KB, [128, 8, S], BF16)
            qT = carve(48 * KB, [128, 4, S], BF16)
            kT = carve(64 * KB, [128, 4, S], BF16)
            vA = carve(80 * KB, [128, NT, 4, 130], BF16)
            hT = carve(16 * KB, [128, 8, 1024], BF16)
            wch = [carve(32 * KB + i * 2 * KB, [128, 8, 128], BF16) for i in range(2)] + \
                  [carve(102 * KB + 256 + i * 2 * KB, [128, 8, 128], BF16) for i in range(4)]
            rotc = carve(36 * KB, [128, 1024], F32)
            rots = carve(40 * KB, [128, 1024], F32)
            rtmp_i = carve(44 * KB, [128, 1024], F32).bitcast(I32)
            qraw = [carve(96 * KB + 256 + i * KB, [128, 512], BF16) for i in range(2)]
            rq1 = [carve(98 * KB + 256 + i * 2 * KB, [128, 512], F32) for i in range(2)]
            p.handoff(["actT", "hT2", "wd", "gu"] + S5_BUILD_BUFS, ["uT", "mixT", "qT", "kT", "vA", "vA1", "hT", "wch", "rotc", "rots", "rtmp",
                                                     "qraw", "rq1"])

            p.op("pool", lambda h: h.memset(vA[:, :, :, 128:130], 1.0), writes=[("vA1",)])

            NWB = len(wch)
            NPRE = 4
            wlist = [(hf, ct_) for hf in range(2) for ct_ in range(16)
                     if not ((ct_ < 4 and not do_ssm) or (ct_ >= 4 and not do_attn))]
            wissued = [0]

            def issue_w(upto):
                while wissued[0] < min(upto, len(wlist)):
                    i_ = wissued[0]
                    ct_ = wlist[i_][1]
                    dma("pool", wch[i_ % NWB][:], din["w_in"][l][:, ct_ * 128:(ct_ + 1) * 128].rearrange("(k p) n -> p k n", p=128),
                        writes=[("wch", i_ % NWB)], key="wch%d" % (i_ % NWB))
                    wissued[0] += 1
            issue_w(NPRE)
            wchn = 0
            pend_rot = []
            for half in range(2):
                tok0 = 8 * half
                build_hT(hT, "hT", tok0, 8, 8, 0)
                if do_attn:
                    posb = din["positions"][sq][1024 * half:1024 * half + 1024].unsqueeze(0).broadcast_to([128, 1024])
                    dma("pool", rotc[:], posb, writes=["rotc"], key="rotc")
                    ts(rots[:], rotc[:], ropef[:, 1:2], None, ALU.mult, None, reads=["rotc", "ropef"], writes=["rots"])
                    ts(rotc[:], rotc[:], ropef[:, 0:1], 0.25, ALU.mult, ALU.add, reads=["rotc", "ropef"], writes=["rotc"])
                    for nm, tb in (("rotc", rotc), ("rots", rots)):
                        cp(rtmp_i[:], tb[:], reads=[nm], writes=["rtmp"])
                        cp(ltmp[:], rtmp_i[:], reads=["rtmp"], writes=["ltmp"])
                        tt(tb[:], tb[:], ltmp[:], ALU.subtract, reads=[nm, "ltmp"], writes=[nm])
                        act(tb[:], tb[:], AF.Sin, reads=[nm], writes=[nm], scale=TWO_PI)
                for ct in range(16):
                    if (ct < 4 and not do_ssm) or (ct >= 4 and not do_attn):
                        continue
                    assert wlist[wchn] == (half, ct)
                    wb = wch[wchn % NWB]
                    wkey = ("wch", wchn % NWB)
                    issue_w(wchn + 1 + NPRE)
                    wchn += 1
                    if ct < 12:
                        for mm in range(2):
                            bank = 2 + (ct * 2 + mm) % 2
                            mm_group([(psb[bank][:, :], wb[:, k, :], hT[:, k, 512 * mm:512 * mm + 512], k == 0, k == 7, {})
                                      for k in range(8)],
                                     reads=[wkey] + [("hT", k, mm) for k in range(8)], writes=[("ps", bank)])
                            tsl = slice(1024 * half + 512 * mm, 1024 * half + 512 * mm + 512)
                            mg = 2 * half + mm
                            if ct < 4:
                                act(uT[:, ct, tsl], psb[bank][:, :], AF.Copy, reads=[("ps", bank)], writes=[("uT", ct, mg)])
                            else:
                                isq = ct < 8
                                dst = qT if isq else kT
                                dkey = ("qT" if isq else "kT", ct % 4, mg)
                                qi = (ct * 2 + mm) % 2
                                while pend_rot:
                                    pend_rot.pop(0)()
                                act(qraw[qi][:], psb[bank][:, :], AF.Copy, reads=[("ps", bank)], writes=[("qraw", qi)],
                                    scale=(0.125 if isq else 1.0))
                                rsl = slice(512 * mm, 512 * mm + 512)

                                def rot_stage(qi=qi, dst=dst, dkey=dkey, c4=ct % 4, tsl=tsl, rsl=rsl):
                                    rb = 4 + qi
                                    mm_group([(psb[rb][:, :], rmat_b[:], qraw[qi][:], True, True, {})],
                                             reads=["rmat_b", ("qraw", qi)], writes=[("ps", rb)])
                                    tt(rq1[qi][:], qraw[qi][:], rotc[:, rsl], ALU.mult, reads=[("qraw", qi), "rotc"], writes=[("rq1", qi)], eng="pool")
                                    tt(dst[:, c4, tsl], psb[rb][:, :], rots[:, rsl], ALU.mult, reads=[("ps", rb), "rots"], writes=[dkey])
                                    tt(dst[:, c4, tsl], dst[:, c4, tsl], rq1[qi][:], ALU.add, reads=[dkey, ("rq1", qi)], writes=[dkey])
                                pend_rot.append(rot_stage)
                    else:
                        while pend_rot:
                            pend_rot.pop(0)()
                        hd = ct - 12
                        for tl in range(8):
                            bank = 2 + tl % 2
                            mm_group([(psb[bank][:, 0:128], hT[:, k, 128 * tl:128 * tl + 128], wb[:, k, :], k == 0, k == 7, {})
                                      for k in range(8)],
                                     reads=[wkey] + [("hT", k, tl // 4) for k in range(8)], writes=[("ps", bank)])
                            act(vA[:, tok0 + tl, hd, 0:128], psb[bank][:, 0:128], AF.Copy, reads=[("ps", bank)],
                                writes=[("vA", tok0 + tl, hd)])

            if dbg and "dbg_q" in dbg_out and sq == 0 and l == 0 and do_attn:
                for nm, tb, kn in (("dbg_q", qT, "qT"), ("dbg_k", kT, "kT")):
                    cp(ltmp[:, 0:512], tb[:, 0, 0:512], reads=[(kn, 0, 0)], writes=["ltmp"])
                    fin.append(dma("sp", dbg_out[nm], ltmp[:, 0:512], reads=["ltmp"], key="dbgq"))
                cp(ltmp[:, 0:130], vA[:, 0, 0, :], reads=[("vA", 0, 0), ("vA1",)], writes=["ltmp"])
                fin.append(dma("sp", dbg_out["dbg_v"], ltmp[:, 0:130], reads=["ltmp"], key="dbgq"))
            p.handoff(["hT", "wch", "rotc", "rots", "rtmp", "mixT"], ["ebuf", "ep_o", "ep_o1", "ep_ob", "accS", "mixT", "mixS"])
            if do_attn:
                ebuf = [carve(16 * KB + i * KB, [128, 512], BF16) for i in range(4)]
                ep_o = carve(20 * KB, [128, 4, 128], F32)
                ep_o1 = carve(22 * KB, [128, 4, 128], F32)
                ep_ob = carve(24 * KB, [128, 4, 128], BF16)
                accS = carve(25 * KB, [128, 2, 4, 130], F32)
                for i, nm in enumerate(("lam_q1", "lam_k1", "lam_q2", "lam_k2")):
                    dma("sp", lamv[:, i, :], din[nm][l].unsqueeze(0).broadcast_to([128, 64]), writes=[("lamv", i)], key="lamv%d" % i)
                dma("sp", gain_b[:], din["subln_w"][l].unsqueeze(0).broadcast_to([128, 128]), writes=["gain_b"], key="gain")
                ts(gain_b[:], gain_b[:], float(1.0 - lam_init), None, ALU.mult, None, reads=["gain_b"], writes=["gain_b"])
                tt(lamv[:, 0, :], lamv[:, 0, :], lamv[:, 1, :], ALU.mult, reads=[("lamv", 0), ("lamv", 1)], writes=[("lamv", 0)])
                tt(lamv[:, 2, :], lamv[:, 2, :], lamv[:, 3, :], ALU.mult, reads=[("lamv", 2), ("lamv", 3)], writes=[("lamv", 2)])
                p.op("dve", lambda h: h.reduce_sum(lams[:, 0:1], lamv[:, 0, :], AX.X), reads=[("lamv", 0)], writes=[("lams", 0)])
                p.op("dve", lambda h: h.reduce_sum(lams[:, 1:2], lamv[:, 2, :], AX.X), reads=[("lamv", 2)], writes=[("lams", 1)])
                act(lams[:, 2:4], lams[:, 0:2], AF.Exp, reads=[("lams", 0), ("lams", 1)], writes=[("lams", 2)])
                tt(lams[:, 4:5], lams[:, 2:3], lams[:, 3:4], ALU.subtract, reads=[("lams", 2)], writes=[("lams", 4)])
                ts(lams[:, 5:6], lams[:, 4:5], -1.0, float(-lam_init), ALU.mult, ALU.add, reads=[("lams", 4)], writes=[("lams", 5)])

                qz = carve(96 * KB + 256, [128, 4, S], BF16)
                p.handoff(["qraw", "rq1", "wch"], ["qz"])
                p.op("pool", lambda h: h.memset(qz[0:64, :, :], 0.0), writes=[("qz", hd_) for hd_ in range(4)])
                for hd_ in range(4):
                    qk_ = [("qT", hd_, Q_) for Q_ in range(4)]
                    act(qz[64:128, hd_, :], qT[64:128, hd_, :], AF.Copy, reads=qk_ + [("qz", hd_)], writes=[("qz", hd_)])
                    p.op("pool", lambda h, hd_=hd_: h.memset(qT[64:128, hd_, :], 0.0), reads=qk_, writes=qk_)
                ecnt = 0
                pending_tail = []
                for hd in range(4):
                    for Q in range(4):
                        nkt = 4 * (Q + 1)
                        for m in range(2):
                            acc = [psb[2 + 2 * m], psb[3 + 2 * m]]
                            acck = [("ps", 2 + 2 * m), ("ps", 3 + 2 * m)]
                            prs = slice(64 * m, 64 * m + 64)

                            def accv(qs):
                                return acc[qs // 2][:, 256 * (qs % 2):256 * (qs % 2) + 130]

                            def issue_s(kt):
                                nonlocal ecnt
                                o = kt - 4 * Q
                                c0 = 128 * o if o > 0 else 0
                                sb_i = (0, 1, 6)[kt % 3]
                                mms_ = []
                                if o >= 0:
                                    mms_.append((psb[sb_i][:, c0:c0 + 128], ident_b[:], negtri_b[:], True, False, {"skip_group_check": True}))
                                qsrc = qT if m == 0 else qz
                                mms_.append((psb[sb_i][:, c0:512], kT[:, hd, 128 * kt:128 * kt + 128],
                                             qsrc[:, hd, 512 * Q + c0:512 * Q + 512], o < 0, True, {"skip_group_check": True}))
                                mm_group(mms_, reads=[("kT", hd, kt // 4), ("qT", hd, Q), ("qz", hd), "ident_b", "negtri_b"], writes=[("ps", sb_i)])
                                ei = ecnt % 4
                                ecnt += 1
                                act(ebuf[ei][:, c0:512], psb[sb_i][:, c0:512], AF.Exp, reads=[("ps", sb_i)], writes=[("ebuf", ei)])
                                return ei, o

                            def issue_pv(kt, ei, o):
                                qs0 = max(o, 0)
                                mms = []
                                for qs in range(qs0, 4):
                                    last_kt = 4 * Q + qs
                                    mms.append((accv(qs), ebuf[ei][:, 128 * qs:128 * qs + 128], vA[:, kt, hd, :],
                                                kt == 0 and qs % 2 == 0, kt == last_kt, {"skip_group_check": True}))
                                mm_group(mms, reads=[("ebuf", ei), ("vA", kt, hd), ("vA1",)], writes=acck)

                            run_tasks(qA, 1)
                            pendq = [issue_s(kt_) for kt_ in range(min(2, nkt))]
                            for kt in range(nkt):
                                if kt + 2 < nkt:
                                    pendq.append(issue_s(kt + 2))
                                issue_pv(kt, *pendq.pop(0))
                        while len(pending_tail) > 0:
                            pending_tail.pop(0)()
                        for bi_ in range(4):
                            cp(accS[:, bi_ // 2, 2 * (bi_ % 2):2 * (bi_ % 2) + 2, :],
                               psb[2 + bi_][:, :].rearrange("p (a b) -> p a b", a=2)[:, :, 0:130],
                               reads=[("ps", 2 + bi_)], writes=[("accS", bi_)])
                        z0 = accS[:, 0, :, 128]
                        z1 = accS[:, 1, :, 128]
                        ak = [("accS", i_) for i_ in range(4)]
                        p.op("dve", lambda h, z0=z0: h.reciprocal(ep_z[:, 0, :], z0), reads=ak, writes=[("ep_z", 0)])
                        p.op("dve", lambda h, z1=z1: h.reciprocal(ep_z[:, 1, :], z1), reads=ak, writes=[("ep_z", 1)])
                        ts(ep_z[:, 1, :], ep_z[:, 1, :], lams[:, 5:6], None, ALU.mult, None,
                           reads=[("ep_z", 1), ("lams", 5)], writes=[("ep_z", 1)])
                        tt(ep_o1[:], accS[:, 1, :, 0:128], ep_z[:, 1, :].unsqueeze(2).broadcast_to([128, 4, 128]), ALU.mult,
                           reads=ak + [("ep_z", 1)], writes=[("ep_o1", 0), ("ep_o1", 1)])
                        tt(ep_o[:], accS[:, 0, :, 0:128], ep_z[:, 0, :].unsqueeze(2).broadcast_to([128, 4, 128]), ALU.mult,
                           reads=ak + [("ep_z", 0)], writes=[("ep_o", 0), ("ep_o", 1)])
                        tt(ep_o[:], ep_o[:], ep_o1[:], ALU.add, reads=[("ep_o", 0), ("ep_o", 1), ("ep_o1", 0), ("ep_o1", 1)],
                           writes=[("ep_o", 0), ("ep_o", 1)])
                        tt(ep_o1[:], ep_o[:], ep_o[:], ALU.mult, reads=[("ep_o", 0), ("ep_o", 1)], writes=[("ep_o1", 0), ("ep_o1", 1)])
                        p.op("dve", lambda h: h.reduce_sum(ep_ss[:], ep_o1[:], AX.X), reads=[("ep_o1", 0), ("ep_o1", 1)], writes=["ep_ss"])
                        def ep_tail(hd=hd, Q=Q):
                            act(ep_ss[:], ep_ss[:], AF.Ln, reads=["ep_ss", "eps_col"], writes=["ep_ss"], scale=1.0 / 128.0, bias=eps_col[:, 0:1])
                            act(ep_ss[:], ep_ss[:], AF.Exp, reads=["ep_ss"], writes=["ep_ss"], scale=-0.5)
                            tt(ep_o[:], ep_o[:], ep_ss[:, :].unsqueeze(2).broadcast_to([128, 4, 128]), ALU.mult,
                               reads=[("ep_o", 0), ("ep_o", 1), "ep_ss"], writes=[("ep_o", 0), ("ep_o", 1)])
                            tt(ep_ob[:], ep_o[:], gain_b[:, :].unsqueeze(1).broadcast_to([128, 4, 128]), ALU.mult,
                               reads=[("ep_o", 0), ("ep_o", 1), "gain_b"], writes=["ep_ob"])
                            pT = psb[7][:, 256:512].bitcast(BF16)
                            tr_group([(pT[:, 128 * qs:128 * qs + 128], ep_ob[:, qs, :], ident_b[:]) for qs in range(4)],
                                     reads=["ep_ob", "ident_b"], writes=[("ps", 7)])
                            act(mixT[:, 4 + hd, 512 * Q:512 * Q + 512], pT[:, 0:512], AF.Copy, reads=[("ps", 7)], writes=[("mixT", 4 + hd, Q)])
                        pending_tail.append(ep_tail)
                        if (hd * 4 + Q) < 8:
                            run_tasks(qA, 1)
                while len(pending_tail) > 0:
                    pending_tail.pop(0)()
                run_until(qA, 0)
            else:
                p.op("pool", lambda h: h.memset(mixT[:, 4:8, :], 0.0), writes=[("mixT", 4 + hd, Q) for hd in range(4) for Q in range(4)])

            wo = carve(48 * KB, [128, 8, D], BF16)

            def load_wo():
                p.handoff(["qT", "tabsB", "tabsC", "Sprev", "Sprev0", "rotb"], ["wo"])
                dma("pool", wo[:], din["w_out"][l].rearrange("(k p) n -> p k n", p=128), writes=["wo"], key="wo")
            if do_ssm:
                ssm_main(l, uT, mixT, pre_glu=load_wo)
            else:
                p.handoff(["ebuf", "ep_o", "ep_o1", "ep_ob", "accS", "mixS"], ["mixS"])
                p.op("pool", lambda h: h.memset(mixT[:, 0:4, :], 0.0), writes=[("mixS", ct, Q) for ct in range(4) for Q in range(4)])

            if dbg and "dbg_mix" in dbg_out and sq == 0 and l == 0:
                for kk in range(8):
                    cp(ltmp[:, 0:512], mixT[:, kk, 0:512], reads=[("mixT" if kk >= 4 else "mixS", kk, 0)], writes=["ltmp"])
                    fin.append(dma("sp", dbg_out["dbg_mix"][kk * 128:(kk + 1) * 128, :], ltmp[:, 0:512], reads=["ltmp"], key="dbgm"))

            run_until(qA, nA_gate)
            if not do_ssm:
                load_wo()
            dma("sp", lngb[:, 0, :], din["ln1_g"][l].unsqueeze(0).broadcast_to([128, D]), writes=[("lngb", 0)], key="lng")
            dma("sp", lngb[:, 1, :], din["ln1_b"][l].unsqueeze(0).broadcast_to([128, D]), writes=[("lngb", 1)], key="lnb")

            def resid_ln(tt_i, banks, gi):
                xk = ("x", tt_i)
                xs = x_sb[:, tt_i, :]
                for hb in range(2):
                    tt(ltmp[:, 512 * hb:512 * hb + 512], psb[banks[hb]][:, :], gate_b[:, 0, 512 * hb:512 * hb + 512], ALU.mult,
                       reads=[("ps", banks[hb]), "gate_b"], writes=["ltmp"])
                stt(xs, xs, ALPHA, ltmp[:], ALU.mult, ALU.add, reads=[xk, "ltmp"], writes=[xk])
                for hb in range(2):
                    p.op("dve", lambda h, hb=hb: h.bn_stats(st_sb[:, 6 * hb:6 * hb + 6], x_sb[:, tt_i, 512 * hb:512 * hb + 512]),
                         reads=[xk], writes=[("st", hb)])
                p.op("dve", lambda h: h.bn_aggr(mv_sb[:], st_sb[:]), reads=[("st", 0), ("st", 1)], writes=["mv"])
                act(rs_sb[:, 0:1], mv_sb[:, 1:2], AF.Ln, reads=["mv", "eps_col"], writes=["rs"], bias=eps_col[:, 0:1], scale=1.0)
                act(rs_sb[:, 0:1], rs_sb[:, 0:1], AF.Exp, reads=["rs"], writes=["rs"], scale=-0.5)
                ts(rs_sb[:, 1:2], mv_sb[:, 0:1], -1.0, rs_sb[:, 0:1], ALU.mult, ALU.mult, reads=["mv", "rs"], writes=["rs1"])
                act(xs, xs, AF.Identity, reads=[xk, "rs", "rs1"], writes=[xk], scale=rs_sb[:, 0:1], bias=rs_sb[:, 1:2])
                tt(xs, xs, lngb[:, 0, :], ALU.mult, reads=[xk, ("lngb", 0)], writes=[xk], eng="pool")
                tt(xs, xs, lngb[:, 1, :], ALU.add, reads=[xk, ("lngb", 1)], writes=[xk], eng="pool")

            for tt_i in range(NT):
                banks = (0, 1) if tt_i % 2 == 0 else (6, 7)
                for hb in range(2):
                    mm_group([(psb[banks[hb]][:, :], mixT[:, k, 128 * tt_i:128 * tt_i + 128], wo[:, k, 512 * hb:512 * hb + 512],
                               k == 0, k == 7, {}) for k in range(8)],
                             reads=["wo"] + [("mixT" if k >= 4 else "mixS", k, tt_i // 4) for k in range(8)], writes=[("ps", banks[hb])])
                resid_ln(tt_i, banks, 0)

            if dbg and "dbg_x1" in dbg_out and sq == 0 and l == 0:
                fin.append(dma("sp", dbg_out["dbg_x1"], x_sb[:, 0, :], reads=[("x", 0)], key="dbgx1"))

            hT2 = carve(0, [128, 8, 1024], BF16)
            actT = carve(16 * KB, [128, NF, 1024], BF16)
            wd = carve(60 * KB, [128, NF, D], BF16)
            gu = [carve(104 * KB + i * 4 * KB, [128, 2, 8, 128], BF16) for i in range(2)]
            p.handoff(["uT", "mixT", "mixS", "qT", "kT", "vA", "ebuf", "ep_o", "ep_o1", "ep_ob", "accS", "wo", "qraw", "rq1", "vA1"] + S5_MAIN_BUFS + [
                       "hT", "wch", "rotc", "rots", "rtmp"],
                      ["hT2", "actT", "wd", "gu"])
            run_until(qA, 0)
            dma("sp", lngb[:, 0, :], din["ln2_g"][l].unsqueeze(0).broadcast_to([128, D]), writes=[("lngb", 0)], key="lng")
            dma("sp", lngb[:, 1, :], din["ln2_b"][l].unsqueeze(0).broadcast_to([128, D]), writes=[("lngb", 1)], key="lnb")
            gun = 0

            def issue_gu(i_):
                if i_ >= 2 * NF:
                    return
                gb_ = gu[i_ % 2]
                fs_ = slice(128 * (i_ % NF), 128 * (i_ % NF) + 128)
                p.op("pool", lambda h, inc, gb_=gb_, fs_=fs_, l=l: (
                    inc(h.dma_start(out=gb_[:, 0, :, :], in_=din["ffn_w_gate"][l][:, fs_].rearrange("(k p) n -> p k n", p=128))),
                    inc(h.dma_start(out=gb_[:, 1, :, :], in_=din["ffn_w_up"][l][:, fs_].rearrange("(k p) n -> p k n", p=128)))),
                    writes=[("gu", i_ % 2)], dma="gu%d" % (i_ % 2), dma_n=2)
            issue_gu(0)
            issue_gu(1)
            wd_issued = [False]
            for half in range(2):
                tok0 = 8 * half
                build_hT(hT2, "hT2", tok0, 8, 32, 24)
                for f in range(NF):
                    gb = gu[gun % 2]
                    gkey = ("gu", gun % 2)
                    gun += 1
                    for mm in range(2):
                        bg = (f * 2 + mm) % 2
                        bu = 2 + bg
                        mm_group([(psb[bg][:, :], gb[:, 0, k, :], hT2[:, k, 512 * mm:512 * mm + 512], k == 0, k == 7, {}) for k in range(8)],
                                 reads=[gkey] + [("hT2", k, mm) for k in range(8)], writes=[("ps", bg)])
                        mm_group([(psb[bu][:, :], gb[:, 1, k, :], hT2[:, k, 512 * mm:512 * mm + 512], k == 0, k == 7, {}) for k in range(8)],
                                 reads=[gkey] + [("hT2", k, mm) for k in range(8)], writes=[("ps", bu)])
                        asl = actT[:, f, 512 * mm:512 * mm + 512]
                        act(stmp[bg][:], psb[bg][:, :], AF.Silu, reads=[("ps", bg)], writes=[("stmp", bg)])
                        tt(asl, stmp[bg][:], psb[bu][:, :], ALU.mult, reads=[("stmp", bg), ("ps", bu)], writes=[("actT", f, mm)])
                    issue_gu(gun + 1)
                    if not wd_issued[0] and f >= 2:
                        wd_issued[0] = True
                        dma("pool", wd[:], din["ffn_w_down"][l].rearrange("(f p) n -> p f n", p=128), writes=["wd"], key="wd")
                    run_tasks(qB, 1)
                if half == 0:
                    run_until(qB, nB_gate)
                for tl in range(8):
                    tt_i = tok0 + tl
                    banks = (4, 5) if tl % 2 == 0 else (6, 7)
                    for hb in range(2):
                        mm_group([(psb[banks[hb]][:, :], actT[:, f, 128 * tl:128 * tl + 128], wd[:, f, 512 * hb:512 * hb + 512],
                                   f == 0, f == NF - 1, {}) for f in range(NF)],
                                 reads=["wd"] + [("actT", f, tl // 4) for f in range(NF)], writes=[("ps", banks[hb])])
                    resid_ln(tt_i, banks, 1)
            run_until(qB, 0)

        yv = y_out[sq].rearrange("(tt p) d -> p tt d", p=128)
        for t4 in range(4):
            fin.append(dma("sp", yv[:, 4 * t4:4 * t4 + 4, :], x_sb[:, 4 * t4:4 * t4 + 4, :],
                           reads=[("x", 4 * t4 + i) for i in range(4)], key="yst%d" % t4))

    p.emit(fin)
    p.close()
    return nc


def ssm_block(p, din, l, env):
    raise NotImplementedError


_NC_CACHE = {}


def kernel(**inputs):
    n = 8
    if "nc" not in _NC_CACHE:
        _NC_CACHE["nc"] = build_program()
    nc = _NC_CACHE["nc"]
    hc = host_consts()
    in_maps = []
    for c in range(n):
        m = {"x": np.ascontiguousarray(inputs["x"][2 * c:2 * c + 2], dtype=np.float32),
             "c": np.ascontiguousarray(inputs["c"][2 * c:2 * c + 2], dtype=np.float32),
             "positions": np.ascontiguousarray(inputs["positions"][2 * c:2 * c + 2], dtype=np.int32)}
        for name, _ in PARAM_SPECS:
            m[name] = np.ascontiguousarray(inputs[name], dtype=np.float32)
        for k_, v_ in hc.items():
            m[k_] = v_
        in_maps.append(m)
    res = run_bass_kernel_spmd(nc, in_maps, core_ids=list(range(n)))
    out = np.concatenate([np.asarray(r["y"]) for r in res.results], axis=0)
    return out.astype(np.float32)
```
